# Optimizing a Trainium2 kernel written in Bass

```python
import jax
import jax.numpy as jnp
from jax import lax
import numpy as np

D_MODEL = 1024
BATCH = 1
SEQ = 16384
DEPTH = 4
DEC_BATCH = 8
DEC_SEQ = 32
PAST_LEN = 4096

CHUNK = 64
EPS = 1e-6
A_HEADS = 4
A_DK = 128
A_DV = 128
A_QK = A_HEADS * A_DK
A_VW = A_HEADS * A_DV
A_QKV = 2 * A_QK + A_VW
CONV_W = 4
B_HEADS = 8
B_KV_HEADS = 2
B_GROUP = B_HEADS // B_KV_HEADS
B_HD = 64
B_QW = B_HEADS * B_HD
B_KW = B_KV_HEADS * B_HD
WINDOW = 128
WIN_CHUNKS = WINDOW // CHUNK
ROPE_DIM = B_HD // 4
ROPE_THETA = 500000.0
D_MIX = A_VW + B_QW
D_IN = A_QKV + A_VW + 2 * A_HEADS + B_QW + 2 * B_KW
D_FF = 2816

kernel_name = "hybrid_stream_gdn_swa_step"


def rmsnorm(x, w):
    xf = x.astype(jnp.float32)
    y = xf * lax.rsqrt(jnp.mean(xf * xf, axis=-1, keepdims=True) + EPS)
    return (y * w.astype(jnp.float32)).astype(x.dtype)


def swiglu(h, w_in, w_out):
    gate, up = jnp.split(h @ w_in, 2, axis=-1)
    return (jax.nn.silu(gate) * up) @ w_out


def l2norm(x):
    return x * lax.rsqrt(jnp.sum(x * x, axis=-1, keepdims=True) + EPS)


def partial_rope(x, pos):
    half = ROPE_DIM // 2
    inv = jnp.power(ROPE_THETA, -jnp.arange(0, ROPE_DIM, 2, dtype=jnp.float32) / ROPE_DIM)
    ang = pos.astype(jnp.float32)[:, None] * inv[None, :]
    cos = jnp.cos(ang)[None, :, None, :]
    sin = jnp.sin(ang)[None, :, None, :]
    xf = x.astype(jnp.float32)
    x1, x2, rest = xf[..., :half], xf[..., half:ROPE_DIM], xf[..., ROPE_DIM:]
    out = jnp.concatenate([x1 * cos - x2 * sin, x2 * cos + x1 * sin, rest], axis=-1)
    return out.astype(x.dtype)


def chunked_gated_delta(q, k, v, g, beta, s0, chunk):
    B, L, H, DK = q.shape
    DV = v.shape[-1]
    n = L // chunk

    def blk(t):
        return t.reshape(B, n, chunk, H, t.shape[-1]).transpose(1, 0, 3, 2, 4)

    def blk_s(t):
        return t.reshape(B, n, chunk, H).transpose(1, 0, 3, 2)

    qc, kc, vc = blk(q), blk(k), blk(v)
    bc = blk_s(beta)
    G = jnp.cumsum(blk_s(g), axis=-1)
    idx = jnp.arange(chunk)
    causal = idx[:, None] >= idx[None, :]
    strict = idx[:, None] > idx[None, :]
    decay = jnp.exp(jnp.where(causal, G[..., :, None] - G[..., None, :], -jnp.inf))
    kk = jnp.einsum('nbhid,nbhjd->nbhij', kc, kc)
    lmat = jnp.where(strict, kk * decay * bc[..., :, None], 0.0)
    amat = lmat + jnp.eye(chunk, dtype=lmat.dtype)
    rhs = jnp.concatenate([vc * bc[..., None], kc * (bc * jnp.exp(G))[..., None]], axis=-1)
    sol = lax.linalg.triangular_solve(amat, rhs, left_side=True, lower=True, unit_diagonal=True)
    u, w = sol[..., :DV], sol[..., DV:]
    qk = jnp.einsum('nbhid,nbhjd->nbhij', qc, kc) * decay
    q_dec = qc * jnp.exp(G)[..., None]
    k_dec = kc * jnp.exp(G[..., -1:] - G)[..., None]
    g_tot = jnp.exp(G[..., -1])

    def step(S, xs):
        u_i, w_i, q_i, k_i, qk_i, gt_i = xs
        v_new = u_i - jnp.einsum('bhck,bhkv->bhcv', w_i, S)
        o = jnp.einsum('bhck,bhkv->bhcv', q_i, S) + jnp.einsum('bhij,bhjv->bhiv', qk_i, v_new)
        S = S * gt_i[..., None, None] + jnp.einsum('bhck,bhcv->bhkv', k_i, v_new)
        return S, o

    S, o = lax.scan(step, s0, (u, w, q_dec, k_dec, qk, g_tot))
    return o.transpose(1, 0, 3, 2, 4).reshape(B, L, H, DV), S


def gated_deltanet(qkv, z, b_logit, a_logit, conv_buf, s0, conv_w, a_log, dt_bias, gnorm_w, chunk):
    B, L, _ = qkv.shape
    xp = jnp.concatenate([conv_buf.astype(qkv.dtype), qkv], axis=1)
    conv = sum(xp[:, j:j + L] * conv_w[j] for j in range(CONV_W))
    new_buf = xp[:, L:]
    act = jax.nn.silu(conv.astype(jnp.float32))
    q = l2norm(act[..., :A_QK].reshape(B, L, A_HEADS, A_DK)) * (A_DK ** -0.5)
    k = l2norm(act[..., A_QK:2 * A_QK].reshape(B, L, A_HEADS, A_DK))
    v = act[..., 2 * A_QK:].reshape(B, L, A_HEADS, A_DV)
    beta = jax.nn.sigmoid(b_logit.astype(jnp.float32))
    g = -jnp.exp(a_log.astype(jnp.float32)) * jax.nn.softplus(a_logit.astype(jnp.float32) + dt_bias.astype(jnp.float32))
    o, s_new = chunked_gated_delta(q, k, v, g, beta, s0.astype(jnp.float32), chunk)
    o = o * lax.rsqrt(jnp.mean(o * o, axis=-1, keepdims=True) + EPS) * gnorm_w.astype(jnp.float32)
    o = o * jax.nn.silu(z.astype(jnp.float32).reshape(B, L, A_HEADS, A_DV))
    return o.reshape(B, L, A_VW).astype(qkv.dtype), new_buf, s_new.astype(s0.dtype)


def sink_attention(q, k, v, valid, sinks):
    s = jnp.einsum('bnqhgd,bnkhd->bnhgqk', q, k).astype(jnp.float32) * (B_HD ** -0.5)
    if valid is not None:
        s = jnp.where(valid, s, -jnp.inf)
    sk = sinks.astype(jnp.float32).reshape(B_KV_HEADS, B_GROUP)[None, None, :, :, None, None]
    m = jnp.maximum(jnp.max(s, axis=-1, keepdims=True), sk)
    p = jnp.exp(s - m)
    denom = jnp.sum(p, axis=-1, keepdims=True) + jnp.exp(sk - m)
    return jnp.einsum('bnhgqk,bnkhd->bnqhgd', (p / denom).astype(v.dtype), v)


def swa_prompt(q, k, v, sinks):
    B, L = q.shape[:2]
    n = L // CHUNK
    pad = WIN_CHUNKS * CHUNK

    def band(t):
        tp = jnp.pad(t, ((0, 0), (pad, 0), (0, 0), (0, 0))).reshape(B, n + WIN_CHUNKS, CHUNK, B_KV_HEADS, B_HD)
        return jnp.concatenate([tp[:, j:j + n] for j in range(WIN_CHUNKS + 1)], axis=2)

    kpos = (jnp.arange(n)[:, None] - WIN_CHUNKS) * CHUNK + jnp.arange((WIN_CHUNKS + 1) * CHUNK)[None, :]
    valid = (kpos >= 0)[None, :, None, None, None, :]
    o = sink_attention(q.reshape(B, n, CHUNK, B_KV_HEADS, B_GROUP, B_HD), band(k), band(v), valid, sinks)
    return o.reshape(B, L, B_QW)


def swa_sample(q, k, v, k_cache, v_cache, sinks):
    B, L = q.shape[:2]
    k_all = jnp.concatenate([k_cache.astype(k.dtype), k], axis=1)
    v_all = jnp.concatenate([v_cache.astype(v.dtype), v], axis=1)
    o = sink_attention(q.reshape(B, 1, L, B_KV_HEADS, B_GROUP, B_HD), k_all[:, None], v_all[:, None], None, sinks)
    rows = k_cache.shape[1]
    return o.reshape(B, L, B_QW), k_all[:, -rows:], v_all[:, -rows:]


def token_mix(h, pos, chunk, conv_buf, s0, k_cache, v_cache, w_in, conv_w, a_log, dt_bias, gnorm_w, sinks, w_out):
    B, L, _ = h.shape
    proj = h @ w_in
    offs = np.cumsum([A_QKV, A_VW, A_HEADS, A_HEADS, B_QW, B_KW]).tolist()
    qkv_a, z_a, b_a, a_a, q_b, k_b, v_b = jnp.split(proj, offs, axis=-1)
    o_a, new_buf, s_new = gated_deltanet(qkv_a, z_a, b_a, a_a, conv_buf, s0, conv_w, a_log, dt_bias, gnorm_w, chunk)
    q_b = partial_rope(q_b.reshape(B, L, B_HEADS, B_HD), pos)
    k_b = partial_rope(k_b.reshape(B, L, B_KV_HEADS, B_HD), pos)
    v_b = v_b.reshape(B, L, B_KV_HEADS, B_HD)
    if k_cache is None:
        o_b = swa_prompt(q_b, k_b, v_b, sinks)
        rows = min(WINDOW, PAST_LEN)
        new_k, new_v = k_b[:, L - rows:], v_b[:, L - rows:]
    else:
        o_b, new_k, new_v = swa_sample(q_b, k_b, v_b, k_cache, v_cache, sinks)
    out = jnp.concatenate([o_a, o_b], axis=-1) @ w_out
    return out, new_buf, s_new, new_k, new_v


def trunk(x, pos, chunk, conv_bufs, s0s, k_caches, v_caches,
          norm_ff1, ff1_w_in, ff1_w_out, norm_mix, w_mix_in, conv_w, a_log, dt_bias,
          gnorm_w, sinks, w_mix_out, norm_ff2, ff2_w_in, ff2_w_out, norm_final):
    bufs, states, ks, vs = [], [], [], []
    for l in range(DEPTH):
        x = x + 0.5 * swiglu(rmsnorm(x, norm_ff1[l]), ff1_w_in[l], ff1_w_out[l])
        kc = None if k_caches is None else k_caches[l]
        vc = None if v_caches is None else v_caches[l]
        o, b, s, kn, vn = token_mix(rmsnorm(x, norm_mix[l]), pos, chunk, conv_bufs[l], s0s[l], kc, vc,
                                    w_mix_in[l], conv_w[l], a_log[l], dt_bias[l], gnorm_w[l], sinks[l], w_mix_out[l])
        x = x + o
        x = x + 0.5 * swiglu(rmsnorm(x, norm_ff2[l]), ff2_w_in[l], ff2_w_out[l])
        bufs.append(b)
        states.append(s)
        ks.append(kn)
        vs.append(vn)
    return rmsnorm(x, norm_final), jnp.stack(bufs), jnp.stack(states), jnp.stack(ks), jnp.stack(vs)


def setup_inputs(seed: int = 0) -> dict:
    key = jax.random.key(seed)
    ks = jax.random.split(key, 24)
    f32 = jnp.float32

    def nrm(k, shape, scale):
        return jax.random.normal(k, shape, f32) * scale

    rows = min(WINDOW, PAST_LEN)
    dt = jnp.exp(jax.random.uniform(ks[13], (DEPTH, A_HEADS), f32, float(np.log(1e-3)), float(np.log(1e-1))))
    return {
        "x_prompt": nrm(ks[0], (BATCH, SEQ, D_MODEL), 1.0),
        "x_sample": nrm(ks[1], (DEC_BATCH, DEC_SEQ, D_MODEL), 1.0),
        "cache_conv": nrm(ks[2], (DEPTH, DEC_BATCH, CONV_W - 1, A_QKV), 1.0),
        "state_delta": nrm(ks[3], (DEPTH, DEC_BATCH, A_HEADS, A_DK, A_DV), 0.1),
        "cache_k": nrm(ks[4], (DEPTH, DEC_BATCH, rows, B_KV_HEADS, B_HD), 1.0),
        "cache_v": nrm(ks[5], (DEPTH, DEC_BATCH, rows, B_KV_HEADS, B_HD), 1.0),
        "norm_ff1": 1.0 + nrm(ks[6], (DEPTH, D_MODEL), 0.02),
        "ff1_w_in": nrm(ks[7], (DEPTH, D_MODEL, 2 * D_FF), D_MODEL ** -0.5),
        "ff1_w_out": nrm(ks[8], (DEPTH, D_FF, D_MODEL), D_FF ** -0.5),
        "norm_mix": 1.0 + nrm(ks[9], (DEPTH, D_MODEL), 0.02),
        "w_mix_in": nrm(ks[10], (DEPTH, D_MODEL, D_IN), D_MODEL ** -0.5),
        "conv_w": nrm(ks[11], (DEPTH, CONV_W, A_QKV), 0.5),
        "a_log": jnp.log(jax.random.uniform(ks[12], (DEPTH, A_HEADS), f32, 1.0, 16.0)),
        "dt_bias": dt + jnp.log(-jnp.expm1(-dt)),
        "gnorm_w": 1.0 + nrm(ks[14], (DEPTH, A_DV), 0.02),
        "sinks": nrm(ks[15], (DEPTH, B_HEADS), 1.0),
        "w_mix_out": nrm(ks[16], (DEPTH, D_MIX, D_MODEL), D_MIX ** -0.5),
        "norm_ff2": 1.0 + nrm(ks[17], (DEPTH, D_MODEL), 0.02),
        "ff2_w_in": nrm(ks[18], (DEPTH, D_MODEL, 2 * D_FF), D_MODEL ** -0.5),
        "ff2_w_out": nrm(ks[19], (DEPTH, D_FF, D_MODEL), D_FF ** -0.5),
        "norm_final": 1.0 + nrm(ks[20], (D_MODEL,), 0.02),
    }


def reference(x_prompt, x_sample, cache_conv, state_delta, cache_k, cache_v,
              norm_ff1, ff1_w_in, ff1_w_out, norm_mix, w_mix_in, conv_w, a_log, dt_bias,
              gnorm_w, sinks, w_mix_out, norm_ff2, ff2_w_in, ff2_w_out, norm_final):
    weights = (norm_ff1, ff1_w_in, ff1_w_out, norm_mix, w_mix_in, conv_w, a_log, dt_bias,
               gnorm_w, sinks, w_mix_out, norm_ff2, ff2_w_in, ff2_w_out, norm_final)
    bp, lp = x_prompt.shape[:2]
    zero_buf = jnp.zeros((DEPTH, bp, CONV_W - 1, A_QKV), x_prompt.dtype)
    zero_state = jnp.zeros((DEPTH, bp, A_HEADS, A_DK, A_DV), state_delta.dtype)
    pos_p = jnp.arange(lp, dtype=jnp.int32)
    y_prompt, conv_p, delta_p, k_p, v_p = trunk(x_prompt, pos_p, CHUNK, zero_buf, zero_state, None, None, *weights)
    ds = x_sample.shape[1]
    pos_s = PAST_LEN + jnp.arange(ds, dtype=jnp.int32)
    y_sample, conv_s, delta_s, k_s, v_s = trunk(x_sample, pos_s, ds, cache_conv, state_delta, cache_k, cache_v, *weights)
    return (y_prompt, y_sample, conv_p, delta_p, k_p, v_p, conv_s, delta_s, k_s, v_s)
```

```python
import numpy as np
from contextlib import ExitStack
import concourse.bass as bass
import concourse.mybir as mybir
from concourse.bass_utils import run_bass_kernel_spmd

F32 = mybir.dt.float32
BF16 = mybir.dt.bfloat16
AF = mybir.ActivationFunctionType
ALU = mybir.AluOpType
AX = mybir.AxisListType

NCORES = 8
L = 4
DM = 1024
KC = 8
TP = 2048
TS = 32
NT = TP + TS
DFF = 2816
HC = 22
DIN = 2824
EPS = 1e-6
NB = TP // 64
FT = 416
NFT = NT // FT
QS = 128.0 ** -0.5
NEG = -30000.0


_STOPPED = [False]


class _Eng:
    def __init__(self, name, sem):
        self.name = name
        self.sem = sem
        self.count = 0
        self.waited = {}
        self.prog = []


class _Slot:
    def __init__(self, name, sem):
        self.name = name
        self.sem = sem
        self.count = 0


class Tracker:
    def __init__(self, sems):
        self._sems = list(sems)
        self.eng = {n: _Eng(n, self._sems.pop()) for n in ("pe", "act", "dve", "pool", "sp")}
        self.last_write = {}
        self.readers = {}
        self.slots = {}

    def slot(self, name):
        if name not in self.slots:
            self.slots[name] = _Slot(name, self._sems.pop())
        return self.slots[name]

    def _wait(self, e, tok, same_ok):
        if tok is None:
            return
        if tok[0] == "e":
            _, en, seq = tok
            if en == e.name and same_ok:
                return
            if e.waited.get(en, 0) >= seq:
                return
            e.waited[en] = seq
            e.prog.append(("w", self.eng[en].sem, seq))
        else:
            sl = tok[1]
            key = "slot:" + sl.name
            if e.waited.get(key, 0) >= sl.count:
                return
            e.waited[key] = sl.count
            e.prog.append(("w", sl.sem, sl.count))

    def _deps(self, e, reads, writes):
        for k in reads:
            self._wait(e, self.last_write.get(k), False)
        for k in writes:
            self._wait(e, self.last_write.get(k), False)
            for t in self.readers.get(k, ()):
                self._wait(e, t, False)

    def _record(self, tok, reads, writes):
        for k in writes:
            self.last_write[k] = tok
            self.readers[k] = []
        for k in reads:
            lst = self.readers.setdefault(k, [])
            if tok[0] == "e":
                lst[:] = [t for t in lst if not (t[0] == "e" and t[1] == tok[1])]
            elif tok in lst:
                continue
            lst.append(tok)

    kmap = None

    def _km(self, keys):
        if self.kmap is None:
            return list(keys)
        return [self.kmap(k) for k in keys]

    def op(self, engname, fn, reads=(), writes=()):
        if _STOPPED[0]:
            return
        reads, writes = self._km(reads), self._km(writes)
        e = self.eng[engname]
        if engname in ("act", "dve"):
            writes = list(writes) + [("psl", k[1]) for k in list(reads) + list(writes)
                                     if isinstance(k, tuple) and k[0] == "ps"]
        self._deps(e, reads, writes)
        e.count += 1
        e.prog.append(("o", fn, e.sem, 1))
        self._record(("e", engname, e.count), reads, writes)

    def dma(self, qname, slotname, out, in_, reads=(), writes=(), **kw):
        if _STOPPED[0]:
            return
        reads, writes = self._km(reads), self._km(writes)
        e = self.eng[qname]
        sl = self.slot(slotname)
        self._deps(e, reads, writes)
        e.prog.append(("o", (lambda h, out=out, in_=in_, kw=kw: h.dma_start(out=out, in_=in_, **kw)), sl.sem, 16))
        sl.count += 16
        self._record(("d", sl), reads, writes)

    def custom(self, qname, slotname, fn, inc, reads=(), writes=()):
        if _STOPPED[0]:
            return
        e = self.eng[qname]
        sl = self.slot(slotname)
        self._deps(e, reads, writes)
        e.prog.append(("o", fn, sl.sem, inc))
        sl.count += inc
        self._record(("d", sl), reads, writes)

    def barrier(self):
        for e in self.eng.values():
            for en, o in self.eng.items():
                if o.count > 0 and en != e.name:
                    self._wait(e, ("e", en, o.count), False)
            for sl in self.slots.values():
                if sl.count > 0:
                    self._wait(e, ("d", sl), False)

    def emit(self, block):
        def run(e):
            def body(h):
                for it in e.prog:
                    if it[0] == "w":
                        h.wait_ge(it[1], it[2])
                    else:
                        it[1](h).then_inc(it[2], it[3])
            return body
        block.tensor(run(self.eng["pe"]))
        block.scalar(run(self.eng["act"]))
        block.vector(run(self.eng["dve"]))
        block.gpsimd(run(self.eng["pool"]))
        block.sync(run(self.eng["sp"]))


def _const_layout():
    off = {}
    n = 0
    for name, w in (("ident", 128), ("ones", 128),
                    ("triU64", 64), ("mL64", 64), ("mU64", 64), ("mI64", 64),
                    ("triU32", 64), ("mL32", 64), ("mU32", 64), ("mI32", 64),
                    ("sel64", 128), ("sel32", 128), ("mask0", 192), ("mask1", 192),
                    ("cos", 33 * 8), ("sin", 33 * 8), ("ohp", 8), ("mlt", 8)):
        off[name] = (n, w)
        n += w
    return off, n


COFF, CN = _const_layout()


def _make_consts(core):
    c = np.zeros((128, CN), np.float32)

    def put(name, arr):
        o, w = COFF[name]
        c[:arr.shape[0], o:o + arr.shape[1]] = arr
    put("ident", np.eye(128, dtype=np.float32))
    put("ones", np.ones((128, 128), np.float32))
    for sz, sfx in ((64, "64"), (32, "32")):
        j = np.arange(sz)[:, None]
        i = np.arange(sz)[None, :]
        put("triU" + sfx, (j <= i).astype(np.float32))
        put("mL" + sfx, -(j > i).astype(np.float32))
        put("mU" + sfx, -(j < i).astype(np.float32))
        put("mI" + sfx, (j <= i).astype(np.float32) * QS)
    s64 = np.zeros((64, 128), np.float32); s64[63, :] = 1.0
    s32 = np.zeros((32, 128), np.float32); s32[31, :] = 1.0
    put("sel64", s64)
    put("sel32", s32)
    hv = NEG if core == 0 else 0.0
    m0 = np.zeros((64, 192), np.float32); m0[:, 64:192] = hv
    m1 = np.zeros((64, 192), np.float32); m1[:, 128:192] = hv
    put("mask0", m0)
    put("mask1", m1)
    inv = np.power(np.float32(500000.0), -np.arange(0, 16, 2, dtype=np.float32) / np.float32(16)).astype(np.float32)
    cos = np.zeros((64, 33, 8), np.float32)
    sin = np.zeros((64, 33, 8), np.float32)
    for b in range(33):
        if b < 32:
            pos = (core * TP + b * 64 + np.arange(64)).astype(np.float32)
        else:
            pos = (4096 + np.arange(32)).astype(np.float32)
        ang = pos[:, None] * inv[None, :]
        cos[:len(pos), b] = np.cos(ang)
        sin[:len(pos), b] = np.sin(ang)
    put("cos", cos.reshape(64, -1))
    put("sin", sin.reshape(64, -1))
    ohp = np.zeros((128, 8), np.float32)
    if core > 0:
        ohp[:, core - 1] = 1.0
    mlt = np.zeros((128, 8), np.float32)
    mlt[:, :core] = 1.0
    put("ohp", ohp)
    put("mlt", mlt)
    return c


class _Stop(Exception):
    pass


import os as _os
_KSTOP = int(_os.environ.get("KSTOP", "0"))
_KSTAGE = _os.environ.get("KSTAGE", "all")


def chk(n):
    if _KSTOP == n:
        _STOPPED[0] = True


def build_program(nlayers=L):
    nc = bass.Bass("TRN2", target_bir_lowering=False, dynamic_dma_scratch_size=8192)

    def din(name, shape):
        return nc.dram_tensor(name, list(shape), F32, kind="ExternalInput").ap()

    def dout(name, shape):
        return nc.dram_tensor(name, list(shape), F32, kind="ExternalOutput").ap()

    d_x = din("xT", (128, KC, NT))
    LF = 1 if _KSTAGE == "mix" else L
    d_wfi = din("wfi", (LF * 2, 128, KC, 2 * DFF))
    d_wfo = din("wfo", (LF * 2, 128, HC, DM))
    d_wmi = din("wmi", (L, 128, KC, DIN))
    d_wmo = din("wmo", (L, 128, KC, DM))
    d_nrm = din("nrm", (128, L * 3 + 1, KC))
    d_cw = din("cw", (128, L, 12, 4))
    d_row = din("rowc", (128, L, 144))
    d_cc = din("cconv", (128, L, 12, 3))
    d_st = din("state", (128, L, 4, 128))
    d_ck = din("ck", (128, L, 128))
    d_cv = din("cv", (128, L, 128))
    d_cst = din("cst", (128, CN))
    o_y = dout("yT", (128, KC, NT))
    o_convp = dout("convp", (128, L, 12, 3))
    o_deltap = dout("deltap", (128, L, 4, 128))
    o_kp = dout("kp", (128, L, 128))
    o_vp = dout("vp", (128, L, 128))
    o_convs = dout("convs", (128, L, 12, 3))
    o_deltas = dout("deltas", (128, L, 4, 128))
    o_ks = dout("ks", (128, L, 128))
    o_vs = dout("vs", (128, L, 128))
    XW = 512
    x_in = [nc.dram_tensor(f"x_in{k}", [128, XW], F32) for k in range(3)]
    x_g4 = [nc.dram_tensor(f"x_g4{k}", [4 * 128, XW], F32) for k in range(3)]
    x_g8 = [nc.dram_tensor(f"x_g8{k}", [8 * 128, XW], F32) for k in range(3)]
    sp_o = nc.dram_tensor("sp_o", [NB, 64, 512], F32)
    sp_z = nc.dram_tensor("sp_z", [NB, 128, 256], F32)
    G4 = [[0, 1, 2, 3], [4, 5, 6, 7]]
    G2 = [[0, 4], [1, 5], [2, 6], [3, 7]]

    es = ExitStack()
    with es:
        _uid = [0]

        def sb(name, shape, dt=F32, stack=None):
            _uid[0] += 1
            return (stack or es).enter_context(nc.sbuf_tensor(f"s{_uid[0]}_{name}", list(shape), dt))

        sems = [es.enter_context(nc.semaphore(f"s{i}")) for i in range(60)]
        T = Tracker(sems)
        ps = [es.enter_context(nc.psum_tensor(f"ps{i}", [128, 512], F32)) for i in range(8)]

        def pk(i):
            return ("ps", i)

        xT = sb("xT", (128, KC, NT))
        cst = sb("cst", (128, CN))
        nrm = sb("nrm", (128, L * 3 + 1, KC))
        cwt = sb("cwt", (128, L, 12, 4))
        rowc = sb("rowc", (128, L, 144))
        identb = sb("identb", (128, 128), BF16)
        onesb = sb("onesb", (128, 128), BF16)
        ea = sb("ea", (128, L, 4))

        def C(name, rows=128, c0=0, c1=None):
            o, w = COFF[name]
            c1 = w if c1 is None else c1
            return cst[0:rows, o + c0:o + c1]

        T.dma("sp", "ld0", xT[:], d_x, writes=["xT"])
        T.dma("sp", "ld1", cst[:], d_cst, writes=["cst"])
        T.dma("sp", "ld1", nrm[:], d_nrm, writes=["nrm"])
        T.dma("sp", "ld1", cwt[:], d_cw, writes=["cwt"])
        T.dma("sp", "ld1", rowc[:], d_row, writes=["rowc"])
        T.op("dve", lambda h: h.tensor_copy(out=identb[:], in_=C("ident")), reads=["cst"], writes=["identb"])
        T.op("dve", lambda h: h.tensor_copy(out=onesb[:], in_=C("ones")), reads=["cst"], writes=["onesb"])
        T.op("act", lambda h: h.activation(out=ea[:], in_=rowc[:, :, 0:4], func=AF.Exp), reads=["rowc"], writes=["ea"])

        def PE(mms, reads, writes):
            def fn(h, mms=mms):
                ins = None
                for (o, l, r, st, sp_) in mms:
                    ins = h.matmul(o, lhsT=l, rhs=r, start=st, stop=sp_)
                return ins
            T.op("pe", fn, reads, writes)

        def TR(outs_ins, ident, reads, writes):
            def fn(h, oi=outs_ins, ident=ident):
                ins = None
                for (o, i) in oi:
                    ins = h.transpose(out=o, in_=i, identity=ident)
                return ins
            T.op("pe", fn, reads, writes)

        def ACT(out, in_, func, reads, writes, bias=None, scale=1.0, accum=None):
            def fn(h):
                kw = {}
                if bias is not None:
                    kw["bias"] = bias
                if accum is not None:
                    kw["accum_out"] = accum
                return h.activation(out=out, in_=in_, func=func, scale=scale, **kw)
            T.op("act", fn, reads, writes)

        def TT(out, in0, in1, op, reads, writes, eng="dve"):
            T.op(eng, lambda h: h.tensor_tensor(out=out, in0=in0, in1=in1, op=op), reads, writes)

        def TS_(out, in0, s1, s2, op0, op1, reads, writes):
            if s2 is None:
                T.op("dve", lambda h: h.tensor_scalar(out=out, in0=in0, scalar1=s1, scalar2=None, op0=op0), reads, writes)
            else:
                T.op("dve", lambda h: h.tensor_scalar(out=out, in0=in0, scalar1=s1, scalar2=s2, op0=op0, op1=op1), reads, writes)

        def STT(out, in0, scalar, in1, op0, op1, reads, writes):
            T.op("dve", lambda h: h.scalar_tensor_tensor(out=out, in0=in0, scalar=scalar, in1=in1, op0=op0, op1=op1), reads, writes)

        def CP(out, in_, reads, writes, eng="dve"):
            if eng == "act":
                ACT(out, in_, AF.Copy, reads, writes)
            else:
                T.op("dve", lambda h: h.tensor_scalar(out=out, in0=in_, scalar1=1.0, scalar2=None, op0=ALU.mult), reads, writes)

        def RSQRT(out, in_, scale, eps, reads, writes):
            ACT(out, in_, AF.Ln, reads, writes, bias=eps_ap(eps, out.shape[0]), scale=scale)
            ACT(out, out, AF.Exp, writes, writes, scale=-0.5)

        epst = sb("epst", (128, 3))
        T.op("dve", lambda h: h.memset(epst[:, 0:1], EPS), (), ["epst"])
        T.op("dve", lambda h: h.memset(epst[:, 1:2], 128.0 * EPS), ["epst"], ["epst"])
        T.op("dve", lambda h: h.memset(epst[:, 2:3], 1.0), ["epst"], ["epst"])

        def eps_ap(which, rows=128):
            return epst[0:rows, which:which + 1]

        def ffn(li, widx):
            with ExitStack() as st:
                hT = sb("hT", (128, KC, NT), BF16, st)
                actT = sb("actT", (128, 11, NT), BF16, st)
                sq = sb("sqf", (128, KC, FT), BF16, st)
                rstd = sb("rstdf", (128, FT), F32, st)
                sg = sb("sg", (128, 2, FT), F32, st)
                wgu = [sb(f"wgu{i}", (128, 2, KC, 128), BF16, st) for i in range(4)]
                wo = [sb(f"wo{i}", (128, 11, 128), BF16, st) for i in range(3)]
                T.barrier()
                for t in range(NFT):
                    cs = slice(t * FT, (t + 1) * FT)
                    ACT(sq[:], xT[:, :, cs], AF.Square, ["xT"], ["sqf"])
                    PE([(ps[7][:, 0:FT], onesb[:], sq[:, kc, :], kc == 0, kc == KC - 1) for kc in range(KC)],
                       ["sqf", "onesb"], [pk(7)])
                    RSQRT(rstd[:], ps[7][:, 0:FT], 1.0 / DM, 0, [pk(7), "epst"], ["rstdf"])
                    for kc in range(KC):
                        STT(hT[:, kc, cs], xT[:, kc, cs], nrm[:, widx, kc:kc + 1], rstd[:],
                            ALU.mult, ALU.mult, ["xT", "nrm", "rstdf"], [("hT", t)])
                cnt = {"a": 0, "b": 0, "ga": 0, "gb": 0}

                def loadA(j):
                    s = cnt["a"] % 4
                    cnt["a"] += 1
                    T.dma("pool", f"wgu{s}", wgu[s][:, 0], d_wfi[li, :, :, j * 128:(j + 1) * 128], writes=[("wgu", s)])
                    T.dma("pool", f"wgu{s}", wgu[s][:, 1], d_wfi[li, :, :, DFF + j * 128:DFF + (j + 1) * 128], writes=[("wgu", s)])
                    return s

                def loadB(j0, m):
                    s = cnt["b"] % 3
                    cnt["b"] += 1
                    T.dma("pool", f"wo{s}", wo[s][:], d_wfo[li, :, j0:j0 + 11, m * 128:(m + 1) * 128], writes=[("wo", s)])
                    return s

                for piece in range(2):
                    j0 = piece * 11
                    pend = [loadA(j0 + q) for q in range(3)]
                    for jj in range(11):
                        if jj + 3 < 11:
                            pend.append(loadA(j0 + jj + 3))
                        s = pend.pop(0)
                        for t in range(NFT):
                            cs = slice(t * FT, (t + 1) * FT)
                            n = cnt["ga"] % 2
                            cnt["ga"] += 1
                            bg, bu = 2 * n, 2 * n + 1
                            PE([(ps[bg][:, 0:FT], wgu[s][:, 0, kc, :], hT[:, kc, cs], kc == 0, kc == KC - 1) for kc in range(KC)] +
                               [(ps[bu][:, 0:FT], wgu[s][:, 1, kc, :], hT[:, kc, cs], kc == 0, kc == KC - 1) for kc in range(KC)],
                               [("wgu", s), ("hT", t)], [pk(bg), pk(bu)])
                            ACT(sg[:, n, :], ps[bg][:, 0:FT], AF.Silu, [pk(bg)], [("sg", n)])
                            TT(actT[:, jj, cs], sg[:, n, :], ps[bu][:, 0:FT], ALU.mult, [("sg", n), pk(bu)], [("actT", t)])
                    pendb = [loadB(j0, m) for m in range(2)]
                    for m in range(KC):
                        if m + 2 < KC:
                            pendb.append(loadB(j0, m + 2))
                        s = pendb.pop(0)
                        for t in range(NFT):
                            cs = slice(t * FT, (t + 1) * FT)
                            n = 4 + cnt["gb"] % 3
                            cnt["gb"] += 1
                            PE([(ps[n][:, 0:FT], wo[s][:, q, :], actT[:, q, cs], q == 0, q == 10) for q in range(11)],
                               [("wo", s), ("actT", t)], [pk(n)])
                            STT(xT[:, m, cs], ps[n][:, 0:FT], 0.5, xT[:, m, cs], ALU.mult, ALU.add, [pk(n), "xT"], ["xT"])
                T.barrier()

        def allgather(k):
            T.custom("pool", "cc", lambda h, k=k: h.collective_compute(
                "AllGather", ALU.bypass, replica_groups=G4, ins=[x_in[k].ap().opt()], outs=[x_g4[k].ap().opt()]), 1,
                reads=[f"d:x_in{k}"], writes=[f"d:x_g4{k}"])
            T.custom("pool", "cc", lambda h, k=k: h.collective_compute(
                "AllGather", ALU.bypass, replica_groups=G2, ins=[x_g4[k].ap().opt()], outs=[x_g8[k].ap().opt()]), 1,
                reads=[f"d:x_g4{k}"], writes=[f"d:x_g8{k}"])

        g8v = [x_g8[k].ap().rearrange("(r p) f -> p r f", p=128) for k in range(3)]

        def mixer(li):
            with ExitStack() as st:
                CURQ = [0]
                LOCALKEYS = {"sqb", "rsb", "pre", "cvb", "ctmp", "sq2", "rs2", "qtok", "small", "dGB", "DU", "DL", "t1", "t2",
                             "Pm0", "Pm1", "Qm0", "Qm1", "Rm0", "Rm1", "Mqk", "nMqk", "uw", "kdec", "dE", "AT", "gta", "QpT",
                             "osb", "zs", "oa", "ssmA", "zst"}
                T.kmap = lambda k: (k, CURQ[0]) if (isinstance(k, str) and k in LOCALKEYS) else k
                BKS = [{0: 0, 1: 1, 2: 2, 3: 3, 4: 4, 5: 5}, {0: 0, 1: 1, 2: 2, 3: 3, 4: 4, 5: 5}]

                def PSB(i):
                    return ps[BKS[CURQ[0]][i]]

                def pkA(i):
                    return ("ps", BKS[CURQ[0]][i])

                class _NS:
                    pass

                def mkset(q, stk):
                    S = _NS()
                    S.hb = sb(f"hb{q}", (128, KC, 64), BF16, stk)
                    S.sqb = sb(f"sqb{q}", (128, KC, 64), BF16, stk)
                    S.rsb = sb(f"rsb{q}", (128, 64), F32, stk)
                    S.pre = sb(f"pre{q}", (128, 12, 67), F32, stk)
                    S.cvb = sb(f"cvb{q}", (128, 12, 64), F32, stk)
                    S.ctmp_full = sb(f"ctmp{q}", (128, 1024), F32, stk)
                    S.ctmp = S.ctmp_full[:, 0:768].rearrange("p (f t) -> p f t", f=12)
                    S.rhs_t = S.ctmp_full[0:64, :].rearrange("p (h w) -> p h w", h=4)
                    S.sq2 = sb(f"sq2{q}", (128, 8, 64), BF16, stk)
                    S.rs2 = sb(f"rs2{q}", (128, 8, 64), F32, stk)
                    S.qtok = sb(f"qtok{q}", (64, 4, 128), F32, stk)
                    S.sm_ = sb(f"small{q}", (128, 96), F32, stk)
                    S.dGB = sb(f"dGB{q}", (64, 4, 2, 64), F32, stk)
                    S.DU = sb(f"DU{q}", (64, 4, 64), F32, stk)
                    S.DL = sb(f"DL{q}", (64, 4, 64), F32, stk)
                    S.t1 = sb(f"t1{q}", (64, 4, 64), F32, stk)
                    S.t2 = sb(f"t2{q}", (64, 4, 64), F32, stk)
                    S.Pm = [sb(f"Pm{i}{q}", (64, 4, 64), F32, stk) for i in range(2)]
                    S.Qm = [sb(f"Qm{i}{q}", (64, 4, 64), F32, stk) for i in range(2)]
                    S.Rm = [sb(f"Rm{i}{q}", (64, 4, 64), F32, stk) for i in range(2)]
                    S.Mqk = sb(f"Mqk{q}", (64, 4, 64), F32, stk)
                    S.nMqk = sb(f"nMqk{q}", (64, 4, 64), F32, stk)
                    S.uw = sb(f"uw{q}", (128, 4, 256), F32, stk)
                    S.kdec = sb(f"kdec{q}", (64, 4, 128), F32, stk)
                    S.dE = sb(f"dE{q}", (64, 4, 64), F32, stk)
                    S.AT = sb(f"AT{q}", (128, 4, 128), F32, stk)
                    S.QpT = sb(f"QpT{q}", (128, 4, 64), F32, stk)
                    S.gta = sb(f"gta{q}", (128, 4), F32, stk)
                    return S

                SETS = [mkset(0, st), None]

                def B():
                    return SETS[CURQ[0]]

                hbs = [SETS[0].hb, None]
                Tt = sb("Tt", (128, 4, 256), F32, st)
                Ssm = sb("Ssm", (128, 4, 128), F32, st)
                halo = sb("halo", (128, 12, 3), F32, st)
                osb = sb("osb", (64, 512), F32, st)
                zst = sb("zst", (128, 256), F32, st)
                qb = sb("qb", (64, 10, 64), F32, st)
                rt = sb("rt", (64, 4, 10, 8), F32, st)
                vtok = sb("vtok", (64, 128), F32, st)
                qr = sb("qr", (64, 10, 64), BF16, st)
                Vw = sb("Vw", (64, 3, 2, 2, 128), BF16, st)
                kTw = sb("kTw", (64, 2, 3, 64), BF16, st)
                qT = sb("qT", (64, 8, 64), BF16, st)
                p1 = ExitStack()
                wmi1 = sb("wmi1", (128, KC, 1800), BF16, p1)
                hsel = sb("hsel", (128, 8, 36), F32, p1)
                SETS[1] = mkset(1, p1)
                hbs = [SETS[0].hb, SETS[1].hb]
                BKS[0] = {0: 0, 1: 1, 2: 2, 3: 0, 4: 0, 5: 3}
                BKS[1] = {0: 4, 1: 5, 2: 6, 3: 4, 4: 4, 5: 7}
                WS = {"w": wmi1, "gate": 1536, "kvb": 1544, "z": None, "qb": None}

                def W():
                    return WS["w"]

                T.barrier()
                for kc in range(KC):
                    T.dma("pool", "wmi", wmi1[:, kc, 0:1536], d_wmi[li, :, kc, 0:1536], writes=["wmi"])
                    T.dma("pool", "wmi", wmi1[:, kc, 1536:1544], d_wmi[li, :, kc, 2048:2056], writes=["wmi"])
                    T.dma("pool", "wmi", wmi1[:, kc, 1544:1800], d_wmi[li, :, kc, 2568:2824], writes=["wmi"])
                T.op("dve", lambda h: h.memset(Vw[:], 0.0), (), ["Vw"])

                a_dt = rowc[:, li, 4:8]
                sinks = rowc[:, li, 8:16]
                gnw = rowc[:, li, 16:144]

                def make_h(col0, P, par=0):
                    hb = hbs[par]
                    cs = slice(col0, col0 + P)
                    ACT(B().sqb[:, :, 0:P], xT[:, :, cs], AF.Square, ["xT"], ["sqb"])
                    PE([(PSB(5)[:, 0:P], onesb[:], B().sqb[:, kc, 0:P], kc == 0, kc == KC - 1) for kc in range(KC)],
                       ["sqb", "onesb"], [pkA(5)])
                    RSQRT(B().rsb[:, 0:P], PSB(5)[:, 0:P], 1.0 / DM, 0, [pkA(5), "epst"], ["rsb"])
                    for kc in range(KC):
                        STT(hb[:, kc, 0:P], xT[:, kc, cs], nrm[:, li * 3 + 1, kc:kc + 1], B().rsb[:, 0:P],
                            ALU.mult, ALU.mult, ["xT", "nrm", "rsb"], [("hb", par)])

                def proj_fm(P, ncols_from, nchunks, banks, par=0):
                    hb = hbs[par]
                    for g0 in range(0, nchunks, 4):
                        bnk = banks[g0 // 4]
                        mms = []
                        for f in range(g0, min(g0 + 4, nchunks)):
                            c0 = ncols_from + f * 128
                            for kc in range(KC):
                                mms.append((PSB(bnk)[:, (f - g0) * 128:(f - g0) * 128 + P], W()[:, kc, c0:c0 + 128],
                                            hb[:, kc, 0:P], kc == 0, kc == KC - 1))
                        PE(mms, ["wmi", ("hb", par)], [pkA(bnk)])

                def gdn_pre(P, sfx, par=0):
                    hb = hbs[par]
                    proj_fm(P, 0, 12, [0, 1, 2], par)
                    yield
                    for g0 in range(3):
                        CP(B().pre[:, g0 * 4:(g0 + 1) * 4, 3:3 + P],
                           PSB(g0)[:, 0:512].rearrange("p (f t) -> p f t", f=4)[:, :, 0:P], [pkA(g0)], ["pre"], eng="act")
                    CP(B().pre[:, :, 0:3], halo[:], ["halo"], ["pre"])
                    for j in range(4):
                        dst = B().cvb if j == 0 else B().ctmp
                        TT(dst[:, :, 0:P], B().pre[:, :, j:j + P], cwt[:, li, :, j:j + 1].broadcast_to([128, 12, P]), ALU.mult,
                           ["pre", "cwt"], ["cvb" if j == 0 else "ctmp"])
                        if j > 0:
                            TT(B().cvb[:, :, 0:P], B().cvb[:, :, 0:P], B().ctmp[:, :, 0:P], ALU.add, ["cvb", "ctmp"], ["cvb"])
                    chk(41)
                    yield
                    CP(halo[:], B().pre[:, :, P:P + 3], ["pre"], ["halo"])
                    yield "halo"
                    ACT(B().ctmp[:, :, 0:P], B().cvb[:, :, 0:P], AF.Exp, ["cvb"], ["ctmp"], scale=-1.0)
                    ACT(B().ctmp[:, :, 0:P], B().ctmp[:, :, 0:P], AF.Ln, ["ctmp", "epst"], ["ctmp"], bias=epst[:, 2:3])
                    ACT(B().ctmp[:, :, 0:P], B().ctmp[:, :, 0:P], AF.Exp, ["ctmp"], ["ctmp"], scale=-1.0)
                    TT(B().cvb[:, :, 0:P], B().cvb[:, :, 0:P], B().ctmp[:, :, 0:P], ALU.mult, ["cvb", "ctmp"], ["cvb"])
                    ACT(B().sq2[:, :, 0:P], B().cvb[:, 0:8, 0:P], AF.Square, ["cvb"], ["sq2"])
                    if P == 64:
                        PE([(PSB(3)[:, 0:512], onesb[:], B().sq2[:].rearrange("p f t -> p (f t)"), True, True)], ["sq2", "onesb"], [pkA(3)])
                    else:
                        PE([(PSB(3)[:, f * 64:f * 64 + P], onesb[:], B().sq2[:, f, 0:P], True, True) for f in range(8)],
                           ["sq2", "onesb"], [pkA(3)])
                    RSQRT(B().rs2[:, :, 0:P], PSB(3)[:, 0:512].rearrange("p (f t) -> p f t", f=8)[:, :, 0:P], 1.0, 0,
                          [pkA(3), "epst"], ["rs2"])
                    TT(B().cvb[:, 0:8, 0:P], B().cvb[:, 0:8, 0:P], B().rs2[:, :, 0:P], ALU.mult, ["cvb", "rs2"], ["cvb"])
                    chk(42)
                    yield
                    PE([(PSB(4)[0:P, 0:8], hb[:, kc, 0:P], W()[:, kc, WS["gate"]:WS["gate"] + 8], kc == 0, kc == KC - 1) for kc in range(KC)],
                       [("hb", par), "wmi"], [pkA(4)])
                    beta = B().sm_[0:P, 0:4]
                    gg = B().sm_[0:P, 4:8]
                    Gc = B().sm_[0:P, 8:12]
                    Gl = B().sm_[0:P, 12:16]
                    ex3 = B().sm_[0:P, 16:28]
                    bG = B().sm_[0:P, 28:32]
                    ACT(beta, PSB(4)[0:P, 0:4], AF.Exp, [pkA(4)], ["small"], scale=-1.0)
                    ACT(beta, beta, AF.Ln, ["small", "epst"], ["small"], bias=epst[0:P, 2:3])
                    ACT(beta, beta, AF.Exp, ["small"], ["small"], scale=-1.0)
                    TT(gg, PSB(4)[0:P, 4:8], a_dt[0:P], ALU.add, [pkA(4), "rowc"], ["small"])
                    ACT(gg, gg, AF.Exp, ["small"], ["small"])
                    ACT(gg, gg, AF.Ln, ["small", "epst"], ["small"], bias=epst[0:P, 2:3])
                    STT(gg, gg, -1.0, ea[0:P, li, :], ALU.mult, ALU.mult, ["small", "ea"], ["small"])
                    PE([(PSB(4)[0:P, 16:20], C("triU" + sfx, P, 0, P), gg, True, True),
                        (PSB(4)[0:P, 20:24], C("ones", P, 0, P), gg, True, True),
                        (PSB(4)[:, 24:28], C("ones", P, 0, 128), gg, True, True)],
                       ["small", "cst"], [pkA(4)])
                    CP(B().sm_[0:P, 8:16], PSB(4)[0:P, 16:24], [pkA(4)], ["small"])
                    CP(B().sm_[0:P, 32:36], Gc, ["small"], ["small"])
                    TT(B().sm_[0:P, 36:40], Gl, Gc, ALU.subtract, ["small"], ["small"])
                    CP(B().sm_[0:P, 40:44], Gl, ["small"], ["small"])
                    ACT(ex3, B().sm_[0:P, 32:44], AF.Exp, ["small"], ["small"])
                    ACT(B().gta[:], PSB(4)[:, 24:28], AF.Exp, [pkA(4)], ["gta"])
                    eG = B().sm_[0:P, 16:20]
                    eGlG = B().sm_[0:P, 20:24]
                    TT(bG, beta, eG, ALU.mult, ["small"], ["small"])
                    chk(43)
                    yield
                    idf = C("ident")
                    TR([(PSB(0)[0:P, f * 128:(f + 1) * 128], B().cvb[:, f, 0:P]) for f in range(4)], idf, ["cvb", "cst", "pre"], [pkA(0)])
                    TR([(PSB(1)[0:P, f * 128:(f + 1) * 128], B().cvb[:, 4 + f, 0:P]) for f in range(4)], idf, ["cvb", "cst"], [pkA(1)])
                    TR([(PSB(2)[0:P, f * 128:(f + 1) * 128], B().cvb[:, 8 + f, 0:P]) for f in range(4)], idf, ["cvb", "cst"], [pkA(2)])
                    CP(B().qtok[0:P], PSB(0)[0:P, :].rearrange("p (h d) -> p h d", h=4), [pkA(0)], ["qtok"], eng="act")
                    k3 = PSB(1)[0:P, :].rearrange("p (h d) -> p h d", h=4)
                    v3 = PSB(2)[0:P, :].rearrange("p (h d) -> p h d", h=4)
                    TT(B().rhs_t[0:P, :, 128:256], k3, bG.unsqueeze(2).broadcast_to([P, 4, 128]), ALU.mult, [pkA(1), "small"], ["ctmp"])
                    TT(B().kdec[0:P], k3, eGlG.unsqueeze(2).broadcast_to([P, 4, 128]), ALU.mult, [pkA(1), "small"], ["kdec"])
                    TT(B().rhs_t[0:P, :, 0:128], v3, beta.unsqueeze(2).broadcast_to([P, 4, 128]), ALU.mult, [pkA(2), "small"], ["ctmp"])
                    chk(44)
                    yield
                    for hh in range(4):
                        TS_(B().dGB[0:P, hh, 0, 0:P], C("ident", P, 0, P), Gc[:, hh:hh + 1], None, ALU.mult, None, ["cst", "small"], ["dGB"])
                        TS_(B().dGB[0:P, hh, 1, 0:P], C("ident", P, 0, P), beta[:, hh:hh + 1], None, ALU.mult, None, ["cst", "small"], ["dGB"])
                    if P == 64:
                        PE([(PSB(5)[0:P, 0:512], C("ones", P, 0, P), B().dGB[:].rearrange("p h w t -> p (h w t)"), True, True)],
                           ["dGB", "cst", "qtok"], [pkA(5)])
                    else:
                        PE([(PSB(5)[0:P, hh * 128 + w * 64: hh * 128 + w * 64 + P], C("ones", P, 0, P), B().dGB[0:P, hh, w, 0:P], True, True)
                            for hh in range(4) for w in range(2)], ["dGB", "cst", "qtok"], [pkA(5)])
                    GB = PSB(5)[0:P, :].rearrange("p (h w t) -> p h w t", h=4, w=2)
                    Grow = GB[:, :, 0, 0:P]
                    brow = GB[:, :, 1, 0:P]
                    PE([(PSB(3)[0:P, hh * 64:hh * 64 + P], B().cvb[:, 4 + hh, 0:P], B().cvb[:, 4 + hh, 0:P], True, True) for hh in range(4)] +
                       [(PSB(3)[0:P, 256 + hh * 64:256 + hh * 64 + P], B().cvb[:, 4 + hh, 0:P], B().cvb[:, hh, 0:P], True, True) for hh in range(4)],
                       ["cvb", "kdec", "ctmp", "rs2"], [pkA(3)])
                    yield
                    kk = PSB(3)[0:P, 0:256].rearrange("p (h t) -> p h t", h=4)[:, :, 0:P]
                    qkT = PSB(3)[0:P, 256:512].rearrange("p (h t) -> p h t", h=4)[:, :, 0:P]
                    Gb = Gc.unsqueeze(2).broadcast_to([P, 4, P])
                    TT(B().t1[0:P, :, 0:P], Grow, Gb, ALU.subtract, [pkA(5), "small"], ["t1"])
                    TS_(B().DU[0:P, :, 0:P], B().t1[0:P, :, 0:P], 0.0, None, ALU.min, None, ["t1"], ["DU"])
                    TS_(B().DL[0:P, :, 0:P], B().t1[0:P, :, 0:P], 0.0, None, ALU.max, None, ["t1"], ["DL"])
                    ACT(B().DU[0:P, :, 0:P], B().DU[0:P, :, 0:P], AF.Exp, ["DU"], ["DU"])
                    ACT(B().DL[0:P, :, 0:P], B().DL[0:P, :, 0:P], AF.Exp, ["DL"], ["DL"], scale=-1.0)
                    yield
                    mU = C("mU" + sfx, P, 0, P).unsqueeze(1).broadcast_to([P, 4, P])
                    mL = C("mL" + sfx, P, 0, P).unsqueeze(1).broadcast_to([P, 4, P])
                    mI = C("mI" + sfx, P, 0, P).unsqueeze(1).broadcast_to([P, 4, P])
                    TT(B().t1[0:P, :, 0:P], kk, B().DU[0:P, :, 0:P], ALU.mult, [pkA(3), "DU"], ["t1"])
                    TT(B().t2[0:P, :, 0:P], brow, mU, ALU.mult, [pkA(5), "cst"], ["t2"])
                    TT(B().Pm[0][0:P, :, 0:P], B().t1[0:P, :, 0:P], B().t2[0:P, :, 0:P], ALU.mult, ["t1", "t2"], ["Pm0"])
                    TT(B().t1[0:P, :, 0:P], kk, B().DL[0:P, :, 0:P], ALU.mult, [pkA(3), "DL"], ["t1"])
                    TT(B().t2[0:P, :, 0:P], mL, beta.unsqueeze(2).broadcast_to([P, 4, P]), ALU.mult, ["cst", "small"], ["t2"])
                    TT(B().Qm[0][0:P, :, 0:P], B().t1[0:P, :, 0:P], B().t2[0:P, :, 0:P], ALU.mult, ["t1", "t2"], ["Qm0"])
                    TT(B().t1[0:P, :, 0:P], qkT, B().DU[0:P, :, 0:P], ALU.mult, [pkA(3), "DU"], ["t1"])
                    TT(B().Mqk[0:P, :, 0:P], B().t1[0:P, :, 0:P], mI, ALU.mult, ["t1", "cst"], ["Mqk"])
                    TS_(B().nMqk[0:P, :, 0:P], B().Mqk[0:P, :, 0:P], -1.0, None, ALU.mult, None, ["Mqk"], ["nMqk"])
                    for hh in range(4):
                        TS_(B().dE[0:P, hh, 0:P], C("ident", P, 0, P), eG[:, hh:hh + 1], QS, ALU.mult, ALU.mult, ["cst", "small"], ["dE"])
                    chk(45)
                    TT(B().Rm[0][0:P, :, 0:P], B().Pm[0][0:P, :, 0:P], C("ident", P, 0, P).unsqueeze(1).broadcast_to([P, 4, P]), ALU.add,
                       ["Pm0", "cst"], ["Rm0"])
                    nr = 5 if P == 64 else 4
                    cur = 0
                    rc = 0
                    yield
                    for r in range(nr + 1):
                        nx = 1 - cur
                        sq_ = r < nr
                        upd = r >= 1
                        mms = []
                        if sq_:
                            if r < nr - 1:
                                mms += [(PSB(5)[0:P, hh * 64:hh * 64 + P], B().Qm[cur][0:P, hh, 0:P], B().Pm[cur][0:P, hh, 0:P], True, True) for hh in range(4)]
                            mms += [(PSB(5)[0:P, 256 + hh * 64:256 + hh * 64 + P], B().Pm[cur][0:P, hh, 0:P], B().Qm[cur][0:P, hh, 0:P], True, True) for hh in range(4)]
                        wr = [pkA(5)] if sq_ else []
                        if upd:
                            mms += [(PSB(4)[0:P, hh * 64:hh * 64 + P], B().Qm[cur][0:P, hh, 0:P], B().Rm[rc][0:P, hh, 0:P], True, True) for hh in range(4)]
                            wr.append(pkA(4))
                        PE(mms, [f"Pm{cur}", f"Qm{cur}", f"Rm{rc}", "t1", "t2", "Mqk", "gta", "small"], wr)
                        pq = PSB(5)[0:P, :].rearrange("p (w h t) -> p w h t", w=2, h=4)
                        if sq_:
                            if r < nr - 1:
                                CP(B().Pm[nx][0:P, :, 0:P], pq[:, 0, :, 0:P], [pkA(5)], [f"Pm{nx}"], eng="act")
                            CP(B().Qm[nx][0:P, :, 0:P], pq[:, 1, :, 0:P], [pkA(5)], [f"Qm{nx}"], eng="act")
                        if upd:
                            TT(B().Rm[1 - rc][0:P, :, 0:P], B().Rm[rc][0:P, :, 0:P],
                               PSB(4)[0:P, 0:256].rearrange("p (h t) -> p h t", h=4)[:, :, 0:P], ALU.add, [f"Rm{rc}", pkA(4)], [f"Rm{1 - rc}"])
                            rc = 1 - rc
                        cur = nx
                        yield
                    cur = rc
                    chk(46)
                    PE([(PSB((0, 1)[hh // 2])[0:P, (hh % 2) * 256:(hh % 2) * 256 + 256], B().Rm[cur][0:P, hh, 0:P], B().rhs_t[0:P, hh, :], True, True)
                        for hh in range(4)], [f"Rm{cur}", "ctmp"], [pkA(0), pkA(1)])
                    CP(B().uw[0:P, 0:2, :], PSB(0)[0:P, :].rearrange("p (h w) -> p h w", h=2), [pkA(0)], ["uw"], eng="act")
                    CP(B().uw[0:P, 2:4, :], PSB(1)[0:P, :].rearrange("p (h w) -> p h w", h=2), [pkA(1)], ["uw"])
                    yield
                    PE([(PSB(2)[:, hh * 128:(hh + 1) * 128], B().uw[0:P, hh, 128:256], B().kdec[0:P, hh, :], True, True) for hh in range(4)],
                       ["uw", "kdec"], [pkA(2)])
                    for hh in range(4):
                        STT(B().AT[:, hh, :], C("ident"), B().gta[:, hh:hh + 1], PSB(2)[:, hh * 128:(hh + 1) * 128], ALU.mult, ALU.subtract,
                            ["cst", "gta", pkA(2)], ["AT"])

                def chain_step(P, state, c0, width):
                    mms = []
                    for hh in range(4):
                        bnk = (0, 1)[hh // 2]
                        o0 = (hh % 2) * 256
                        mms.append((PSB(bnk)[:, o0:o0 + width], B().AT[:, hh, :], state[:, hh, c0:c0 + width], True, False))
                        mms.append((PSB(bnk)[:, o0 + width - 128:o0 + width], B().kdec[0:P, hh, :], B().uw[0:P, hh, 0:128], False, True))
                    PE(mms, ["AT", "state", "kdec", "uw"], [pkA(0), pkA(1)])
                    yield
                    for hp in range(2):
                        CP(state[:, 2 * hp:2 * hp + 2, c0:c0 + width],
                           PSB(hp)[:, :].rearrange("p (h w) -> p h w", h=2)[:, :, 0:width], [pkA(hp)], ["state"],
                           eng=("act" if hp == 0 else "dve"))

                def gdn_qp(P):
                    PE([m for hh in range(4) for m in (
                        (PSB(3)[:, hh * 64:hh * 64 + P], B().qtok[0:P, hh, :], B().dE[0:P, hh, 0:P], True, False),
                        (PSB(3)[:, hh * 64:hh * 64 + P], B().uw[0:P, hh, 128:256], B().nMqk[0:P, hh, 0:P], False, True))],
                       ["qtok", "dE", "uw", "nMqk"], [pkA(3)])
                    yield
                    CP(B().QpT[:, :, 0:P], PSB(3)[:, 0:256].rearrange("p (h t) -> p h t", h=4)[:, :, 0:P], [pkA(3)], ["QpT"], eng="act")

                def gdn_fin(P, par, mv=None, mk=None):
                    mixT = mixTs[par] if mv is None else mv
                    mk = ("mixT", par) if mk is None else mk
                    for hh in range(4):
                        ACT(B().oa[0:P, hh * 128:(hh + 1) * 128], B().osb[0:P, hh * 128:(hh + 1) * 128], AF.Square, ["osb"], ["oa", "ssmA"],
                            accum=B().ssmA[0:P, hh:hh + 1])
                    yield
                    RSQRT(B().ssmA[0:P, 4:8], B().ssmA[0:P, 0:4], 1.0 / 128, 0, ["ssmA", "epst"], ["ssmA"])
                    o3 = B().osb[0:P, :].rearrange("p (h d) -> p h d", h=4)
                    TT(o3, o3, B().ssmA[0:P, 4:8].unsqueeze(2).broadcast_to([P, 4, 128]), ALU.mult, ["osb", "ssmA"], ["osb"])
                    TT(o3, o3, gnw[0:P].unsqueeze(1).broadcast_to([P, 4, 128]), ALU.mult, ["osb", "rowc"], ["osb"])
                    TT(B().oa[0:P, :], B().osb[0:P, :], B().zs[0:P, :], ALU.mult, ["osb", "zs"], ["oa"])
                    TRb([(pbf(BKS[CURQ[0]][2])[:, hh * 64:hh * 64 + P], B().oa[0:P, hh * 128:(hh + 1) * 128]) for hh in range(4)],
                        ["oa", "identb"], [pkA(2)])
                    yield
                    CP(mixT[:, 0:4, 0:P], pbf(BKS[CURQ[0]][2])[:, 0:256].rearrange("p (h t) -> p h t", h=4)[:, :, 0:P], [pkA(2)], [mk], eng="act")

                def zproj(P, par):
                    hb = hbs[par]
                    PE([(PSB(5)[0:P, :], hb[:, kc, 0:P], W()[:, kc, WS["z"]:WS["z"] + 512], kc == 0, kc == KC - 1) for kc in range(KC)],
                       [("hb", par), "wmi"], [pkA(5)])
                    yield
                    ACT(B().zs[0:P, :], PSB(5)[0:P, :], AF.Exp, [pkA(5)], ["zs"], scale=-1.0)
                    ACT(B().zs[0:P, :], B().zs[0:P, :], AF.Ln, ["zs", "epst"], ["zs"], bias=epst[0:P, 2:3])
                    ACT(B().zs[0:P, :], B().zs[0:P, :], AF.Exp, ["zs"], ["zs"], scale=-1.0)
                    TT(B().zs[0:P, :], B().zs[0:P, :], PSB(5)[0:P, :], ALU.mult, ["zs", pkA(5)], ["zs"])

                def gdn_out(P, state, c0, col0, par=0):
                    yield from gdn_qp(P)
                    PE([m for hh in range(4) for m in (
                        (PSB(4)[0:P, hh * 128:(hh + 1) * 128], B().QpT[:, hh, 0:P], state[:, hh, c0:c0 + 128], True, False),
                        (PSB(4)[0:P, hh * 128:(hh + 1) * 128], B().Mqk[0:P, hh, 0:P], B().uw[0:P, hh, 0:128], False, True))],
                       ["QpT", "state", "Mqk", "uw"], [pkA(4)])
                    yield from zproj(P, par)
                    CP(B().osb[0:P, :], PSB(4)[0:P, :], [pkA(4)], ["osb"])
                    yield from gdn_fin(P, par)

                def gdn_loc(P, b):
                    yield from gdn_qp(P)
                    PE([(PSB(5)[:, hh * 64:hh * 64 + P], Tt[:, hh, 0:128], B().QpT[:, hh, 0:P], True, True) for hh in range(4)] +
                       [m for hh in range(4) for m in (
                        (PSB(4)[0:P, hh * 128:(hh + 1) * 128], B().QpT[:, hh, 0:P], Tt[:, hh, 128:256], True, False),
                        (PSB(4)[0:P, hh * 128:(hh + 1) * 128], B().Mqk[0:P, hh, 0:P], B().uw[0:P, hh, 0:128], False, True))],
                       ["QpT", "state", "Mqk", "uw"], [pkA(5), pkA(4)])
                    yield
                    CP(zst[:], PSB(5)[:, 0:256], [pkA(5)], ["zstS"], eng="act")
                    CP(osb[0:P, :], PSB(4)[0:P, :], [pkA(4)], ["osbS"])
                    T.dma("sp", "spill", sp_z.ap()[b], zst[:], reads=["zstS"], writes=[("d:spz", b)])
                    T.dma("sp", "spill", sp_o.ap()[b], osb[0:P, :], reads=["osbS"], writes=[("d:spo", b)])

                def gdn_corr(P, b, par, mv=None, mk=None):
                    T.dma("sp", "fill", B().zst[:], sp_z.ap()[b], reads=[("d:spz", b)], writes=["zst"])
                    T.dma("sp", "fill", B().osb[0:P, :], sp_o.ap()[b], reads=[("d:spo", b)], writes=["osb"])
                    yield from zproj(P, par)
                    PE([(PSB(4)[0:P, hh * 128:(hh + 1) * 128], B().zst[:, hh * 64:hh * 64 + P], Tt[:, hh, 128:256], True, True) for hh in range(4)],
                       ["zst", "state"], [pkA(4)])
                    yield
                    TT(B().osb[0:P, :], B().osb[0:P, :], PSB(4)[0:P, :], ALU.add, ["osb", pkA(4)], ["osb"])
                    yield from gdn_fin(P, par, mv, mk)

                def pbf(i):
                    return psb16[i]

                def TRb(oi, reads, writes):
                    def fn(h, oi=oi):
                        ins = None
                        for (o, i) in oi:
                            ins = h.transpose(out=o, in_=i, identity=identb[0:i.shape[0], 0:i.shape[0]])
                        return ins
                    T.op("pe", fn, reads, writes)

                def swa_proj(P, bidx, slot, par=0, kout=None, vout=None, kv_only=False):
                    hb = hbs[par]
                    if kv_only:
                        PE([(ps[7][0:P, 0:256], hb[:, kc, 0:P], W()[:, kc, WS["kvb"]:WS["kvb"] + 256], kc == 0, kc == KC - 1) for kc in range(KC)],
                           [("hb", par), "wmi"], [pk(7)])
                        yield
                        CP(qb[0:P, 8:10, :], ps[7][0:P, 0:128].rearrange("p (h d) -> p h d", h=2), [pk(7)], ["qb"], eng="act")
                        CP(vtok[0:P, :], ps[7][0:P, 128:256], [pk(7)], ["vtok"], eng="act")
                        co, _ = COFF["cos"]
                        so, _ = COFF["sin"]
                        cosb = cst[0:P, co + bidx * 8:co + bidx * 8 + 8].unsqueeze(1).broadcast_to([P, 2, 8])
                        sinb = cst[0:P, so + bidx * 8:so + bidx * 8 + 8].unsqueeze(1).broadcast_to([P, 2, 8])
                        x1 = qb[0:P, 8:10, 0:8]
                        x2 = qb[0:P, 8:10, 8:16]
                        TT(rt[0:P, 0, 8:10], x1, cosb, ALU.mult, ["qb", "cst"], ["rt"])
                        TT(rt[0:P, 1, 8:10], x2, sinb, ALU.mult, ["qb", "cst"], ["rt"])
                        TT(rt[0:P, 2, 8:10], x2, cosb, ALU.mult, ["qb", "cst"], ["rt"])
                        TT(rt[0:P, 3, 8:10], x1, sinb, ALU.mult, ["qb", "cst"], ["rt"])
                        TT(x1, rt[0:P, 0, 8:10], rt[0:P, 1, 8:10], ALU.subtract, ["rt"], ["qb"])
                        TT(x2, rt[0:P, 2, 8:10], rt[0:P, 3, 8:10], ALU.add, ["rt"], ["qb"])
                        return
                    PE([(ps[6][0:P, :], hb[:, kc, 0:P], W()[:, kc, WS["qb"]:WS["qb"] + 512], kc == 0, kc == KC - 1) for kc in range(KC)] +
                       [(ps[7][0:P, 0:256], hb[:, kc, 0:P], W()[:, kc, WS["kvb"]:WS["kvb"] + 256], kc == 0, kc == KC - 1) for kc in range(KC)],
                       [("hb", par), "wmi"], [pk(6), pk(7)])
                    yield
                    CP(qb[0:P, 0:8, :], ps[6][0:P, :].rearrange("p (h d) -> p h d", h=8), [pk(6)], ["qb"], eng="act")
                    CP(qb[0:P, 8:10, :], ps[7][0:P, 0:128].rearrange("p (h d) -> p h d", h=2), [pk(7)], ["qb"], eng="act")
                    CP(vtok[0:P, :], ps[7][0:P, 128:256], [pk(7)], ["vtok"], eng="act")
                    yield
                    co, _ = COFF["cos"]
                    so, _ = COFF["sin"]
                    cosb = cst[0:P, co + bidx * 8:co + bidx * 8 + 8].unsqueeze(1).broadcast_to([P, 10, 8])
                    sinb = cst[0:P, so + bidx * 8:so + bidx * 8 + 8].unsqueeze(1).broadcast_to([P, 10, 8])
                    x1 = qb[0:P, :, 0:8]
                    x2 = qb[0:P, :, 8:16]
                    TT(rt[0:P, 0], x1, cosb, ALU.mult, ["qb", "cst"], ["rt"])
                    TT(rt[0:P, 1], x2, sinb, ALU.mult, ["qb", "cst"], ["rt"])
                    TT(rt[0:P, 2], x2, cosb, ALU.mult, ["qb", "cst"], ["rt"])
                    TT(rt[0:P, 3], x1, sinb, ALU.mult, ["qb", "cst"], ["rt"])
                    TT(x1, rt[0:P, 0], rt[0:P, 1], ALU.subtract, ["rt"], ["qb"])
                    TT(x2, rt[0:P, 2], rt[0:P, 3], ALU.add, ["rt"], ["qb"])
                    CP(qr[0:P], qb[0:P], ["qb"], ["qr"])
                    for g in range(2):
                        CP(Vw[0:P, slot, g, 0, 0:64], vtok[0:P, g * 64:(g + 1) * 64], ["vtok"], ["Vw"])
                        CP(Vw[0:P, slot, g, 1, 64:128], vtok[0:P, g * 64:(g + 1) * 64], ["vtok"], ["Vw"])
                    yield
                    TRb([(pbf(6)[0:64, hh * 64:hh * 64 + P], qr[0:P, hh, :]) for hh in range(10)], ["qr", "identb"], [pk(6)])
                    yield
                    CP(qT[:, :, 0:P], pbf(6)[0:64, 0:512].rearrange("p (h t) -> p h t", h=8)[:, :, 0:P], [pk(6)], ["qT"], eng="act")
                    CP(kTw[:, :, slot, 0:P], pbf(6)[0:64, 512:640].rearrange("p (h t) -> p h t", h=2)[:, :, 0:P], [pk(6)], ["kTw"], eng="act")
                    if kout is not None:
                        T.dma("sp", "outk", kout, qb[0:P, 8:10, :].rearrange("p h d -> p (h d)"), reads=["qb"])
                        T.dma("sp", "outk", vout, vtok[0:P, :], reads=["vtok"])

                def swa_attn(P, nkeys, maskname, par=0, mv=None, mk=None):
                    mixT = mixTs[par] if mv is None else mv
                    mk = ("mixT", par) if mk is None else mk
                    NK = 192
                    for hg in range(2):
                        PE([(ps[6 + (hq // 2)][0:P, (hq % 2) * 192 + s * 64: (hq % 2) * 192 + s * 64 + nkeys[s]], qT[:, hg * 4 + hq, 0:P],
                             kTw[:, hg, s, 0:nkeys[s]], True, True) for hq in range(4) for s in range(3)],
                           ["qT", "kTw"], [pk(6), pk(7)])
                        yield
                        for hp in range(2):
                            src = ps[6 + hp][0:P, 0:384].rearrange("p (h k) -> p h k", h=2)
                            if maskname is None:
                                nkw = 128 + nkeys[2]
                                TS_(sco[0:P, 2 * hp:2 * hp + 2, 0:nkw], src[:, :, 0:nkw], 0.125, None, ALU.mult, None, [pk(6 + hp)], ["sco"])
                            else:
                                STT(sco[0:P, 2 * hp:2 * hp + 2, :], src, 0.125,
                                    C(maskname, P).unsqueeze(1).broadcast_to([P, 2, 192]), ALU.mult, ALU.add,
                                    [pk(6 + hp), "cst"], ["sco"])
                        if nkeys[2] < 64 or nkeys[1] < 64 or nkeys[0] < 64:
                            for s in range(3):
                                if nkeys[s] < 64:
                                    T.op("dve", lambda h, s=s: h.memset(sco[0:P, :, s * 64 + nkeys[s]:(s + 1) * 64], NEG), ["sco"], ["sco"])
                        mx = ssm[0:P, 8:12]
                        T.op("dve", lambda h: h.tensor_reduce(out=mx, in_=sco[0:P], axis=AX.X, op=ALU.max), ["sco"], ["ssm"])
                        TT(mx, mx, sinks[0:P, hg * 4:hg * 4 + 4], ALU.max, ["ssm", "rowc"], ["ssm"])
                        TS_(ssm[0:P, 12:16], mx, -1.0, None, ALU.mult, None, ["ssm"], ["ssm"])
                        yield
                        for hq in range(4):
                            ACT(sco[0:P, hq, :], sco[0:P, hq, :], AF.Exp, ["sco", "ssm"], ["sco", "ssm2"],
                                bias=ssm[0:P, 12 + hq:13 + hq], accum=ssm[0:P, 16 + hq:17 + hq])
                        yield
                        TT(ssm[0:P, 20:24], sinks[0:P, hg * 4:hg * 4 + 4], mx, ALU.subtract, ["ssm", "rowc"], ["ssm"])
                        ACT(ssm[0:P, 20:24], ssm[0:P, 20:24], AF.Exp, ["ssm"], ["ssm"])
                        TT(ssm[0:P, 24:28], ssm[0:P, 20:24], ssm[0:P, 16:20], ALU.add, ["ssm", "ssm2"], ["ssm"])
                        T.op("dve", lambda h: h.reciprocal(out=ssm[0:P, 24:28], in_=ssm[0:P, 24:28]), ["ssm"], ["ssm"])
                        TT(pn[0:P], sco[0:P], ssm[0:P, 24:28].unsqueeze(2).broadcast_to([P, 4, 192]), ALU.mult, ["sco", "ssm"], ["pn"])
                        yield
                        TRb([(pbf(6)[0:nkeys[s], (hq * 3 + s) * 64:(hq * 3 + s) * 64 + P], pn[0:P, hq, s * 64:s * 64 + nkeys[s]])
                             for hq in range(4) for s in range(3)], ["pn", "identb", "qT", "kTw"], [pk(6)])
                        yield
                        CP(pT[:, :, :, 0:P], pbf(6)[0:64, 0:768].rearrange("p (h s t) -> p h s t", h=4, s=3)[:, :, :, 0:P], [pk(6)], ["pT"],
                           eng="act")
                        yield
                        mms = []
                        for pr in range(2):
                            k = 0
                            for par_ in range(2):
                                for s in range(3):
                                    mms.append((ps[7][:, pr * 64:pr * 64 + P], Vw[0:nkeys[s], s, hg, par_, :],
                                                pT[0:nkeys[s], pr * 2 + par_, s, 0:P], k == 0, k == 5))
                                    k += 1
                        PE(mms, ["Vw", "pT"], [pk(7)])
                        yield
                        CP(mixT[:, 4 + hg * 2:6 + hg * 2, 0:P], ps[7][:, 0:128].rearrange("p (h t) -> p h t", h=2)[:, :, 0:P], [pk(7)], [mk],
                           eng="act")

                def out_proj(P, col0, par=0, mv=None, mk=None):
                    mixT = mixTs[par] if mv is None else mv
                    mk = ("mixT", par) if mk is None else mk
                    sw = max(64, P)
                    for mh in range(2):
                        PE([(ps[6 + mh][:, mm * sw:mm * sw + P], wmo[:, kc, (mh * 4 + mm) * 128:(mh * 4 + mm + 1) * 128], mixT[:, kc, 0:P],
                             kc == 0, kc == KC - 1) for mm in range(4) for kc in range(KC)], ["wmo", mk], [pk(6 + mh)])
                        yield
                        TT(xT[:, mh * 4:mh * 4 + 4, col0:col0 + P], xT[:, mh * 4:mh * 4 + 4, col0:col0 + P],
                           ps[6 + mh][:, 0:4 * sw].rearrange("p (m t) -> p m t", m=4)[:, :, 0:P], ALU.add, ["xT", pk(6 + mh)], ["xT"])

                def run(*gens):
                    gens = [(g if isinstance(g, tuple) else (g, 0)) for g in gens if g is not None]
                    while gens:
                        for it in list(gens):
                            CURQ[0] = it[1]
                            try:
                                next(it[0])
                            except StopIteration:
                                gens.remove(it)
                    CURQ[0] = 0

                make_h(TP - 64, 64, 0)
                PE([(ps[0][:, f * 4:f * 4 + 3], W()[:, kc, f * 128:(f + 1) * 128], hbs[0][:, kc, 61:64], kc == 0, kc == KC - 1)
                    for f in range(12) for kc in range(KC)], ["wmi", ("hb", 0)], [pk(0)])
                CP(halo[:], ps[0][:, 0:48].rearrange("p (f t) -> p f t", f=12)[:, :, 0:3], [pk(0)], ["halo"])
                T.dma("sp", "x_in", x_in[2].ap()[:, 256:292], halo[:].rearrange("p f t -> p (f t)"), reads=["halo"], writes=["d:x_in2"])
                chk(1)
                allgather(2)
                T.dma("sp", "xl", hsel[:], g8v[2][:, :, 256:292], reads=["d:x_g82"], writes=["hsel"])
                hf = halo[:].rearrange("p f t -> p (f t)")
                T.op("dve", lambda h: h.memset(hf, 0.0), ["halo"], ["halo"])
                for r in range(NCORES - 1):
                    STT(hf, hsel[:, r, :], C("ohp", 128, r, r + 1), hf, ALU.mult, ALU.add, ["hsel", "cst", "halo"], ["halo"])

                chk(2)
                T.op("dve", lambda h: h.memset(Tt[:], 0.0), ["state"], ["state"])
                for hh in range(4):
                    CP(Tt[:, hh, 0:128], C("ident"), ["cst", "state"], ["state"])
                order = {"halo": 0, "chain": 0}

                def p1_block(b, q):
                    while order["halo"] < b:
                        yield
                    make_h(b * 64, 64, q)
                    g = gdn_pre(64, "64", q)
                    for _ in g:
                        if order["halo"] == b and _ == "halo":
                            order["halo"] = b + 1
                        yield
                    order["halo"] = max(order["halo"], b + 1)
                    while order["chain"] < b:
                        yield
                    yield from gdn_loc(64, b)
                    yield from chain_step(64, Tt, 0, 256)
                    order["chain"] = b + 1
                    if b >= NB - 2:
                        yield from swa_proj(64, b, b % 3, q, kv_only=True)
                        hc = b - (NB - 2)
                        T.dma("sp", "x_in", x_in[2].ap()[hc * 64:(hc + 1) * 64, 0:128], qb[:, 8:10, :].rearrange("p h d -> p (h d)"),
                              reads=["qb"], writes=["d:x_in2"])
                        T.dma("sp", "x_in", x_in[2].ap()[hc * 64:(hc + 1) * 64, 128:256], vtok[:], reads=["vtok"], writes=["d:x_in2"])
                        T.dma("sp", "outk", o_kp[hc * 64:(hc + 1) * 64, li], qb[:, 8:10, :].rearrange("p h d -> p (h d)"), reads=["qb"])
                        T.dma("sp", "outk", o_vp[hc * 64:(hc + 1) * 64, li], vtok[:], reads=["vtok"])

                def p1_stream(q):
                    for b in range(q, NB, 2):
                        yield from p1_block(b, q)
                run((p1_stream(0), 0), (p1_stream(1), 1))
                T.dma("sp", "outc", o_convp[:, li], halo[:], reads=["halo"])
                chk(6)
                T.barrier()
                p1.close()
                SETS[1] = None
                p2 = ExitStack()
                wmi = sb("wmi", (128, KC, DIN), BF16, p2)
                wmo = sb("wmo", (128, KC, DM), BF16, p2)
                hb1 = sb("hb1b", (128, KC, 64), BF16, p2)
                hbs = [SETS[0].hb, hb1]
                zs = sb("zs", (128, 512), F32, p2)
                oa = sb("oa", (64, 512), BF16, p2)
                mixTs = [sb(f"mixT{i}", (128, 8, 64), BF16, p2) for i in range(2)]
                sco = sb("sco", (64, 4, 192), F32, p2)
                ssmA = sb("ssmA", (64, 8), F32, p2)
                pn = sb("pn", (64, 4, 192), BF16, p2)
                pT = sb("pT", (64, 4, 3, 64), BF16, p2)
                ssm = sb("ssm", (64, 32), F32, p2)
                kvh = sb("kvh", (128, 2, 128), F32, p2)
                ckb = sb("ckb", (128, 2, 128), F32, p2)
                S0 = SETS[0]
                S0.osb, S0.zst, S0.zs, S0.oa, S0.ssmA = osb, zst, zs, oa, ssmA
                S1 = _NS()
                S1.sqb = S0.cvb[:].rearrange("p f t -> p (f t)").bitcast(BF16)[:, 0:512].rearrange("p (k t) -> p k t", k=8)
                S1.rsb = S0.QpT[:, 0, :]
                S1.osb = S0.uw[:, 0:2, :].rearrange("p h w -> p (h w)")
                S1.zs = S0.uw[:, 2:4, :].rearrange("p h w -> p (h w)")
                S1.zst = S0.AT[:, 0:2, :].rearrange("p h d -> p (h d)")
                S1.oa = S0.kdec[:].rearrange("p h d -> p (h d)").bitcast(BF16)[:, 0:512]
                S1.ssmA = S0.sm_[0:64, 0:8]
                SETS[1] = S1
                hbA = S0.sq2
                hbB = S0.pre[:].rearrange("p f t -> p (f t)").bitcast(BF16)[:, 0:512].rearrange("p (k t) -> p k t", k=8)
                WS.update({"w": wmi, "gate": 2048, "kvb": 2568, "z": 1536, "qb": 2056})
                BKS[0] = {0: 0, 1: 1, 2: 2, 3: 3, 4: 4, 5: 5}
                BKS[1] = {0: 0, 1: 1, 2: 3, 3: 3, 4: 1, 5: 0}
                for kc in range(KC):
                    T.dma("pool", "wmi", wmi[:, kc, :], d_wmi[li, :, kc, :], writes=["wmi"])
                T.dma("pool", "wmo", wmo[:], d_wmo[li], writes=["wmo"])
                TR([(ps[0][:, hh * 128:(hh + 1) * 128], Tt[:, hh, 0:128]) for hh in range(4)], C("ident"), ["state", "cst"], [pk(0)])
                CP(B().uw[:, :, 0:128], ps[0][:, :].rearrange("p (h d) -> p h d", h=4), [pk(0)], ["uw"])
                CP(B().uw[:, :, 128:256], Tt[:, :, 128:256], ["state"], ["uw"])
                T.dma("sp", "x_in", x_in[0].ap(), B().uw[:, 0:2, :].rearrange("p h w -> p (h w)"), reads=["uw"], writes=["d:x_in0"])
                T.dma("sp", "x_in", x_in[1].ap(), B().uw[:, 2:4, :].rearrange("p h w -> p (h w)"), reads=["uw"], writes=["d:x_in1"])
                allgather(0)
                allgather(1)
                allgather(2)
                T.op("dve", lambda h: h.memset(Ssm[:], 0.0), ["Ssm"], ["Ssm"])
                T.op("dve", lambda h: h.memset(Tt[:, :, 128:256], 0.0), ["state"], ["state"])
                T.op("dve", lambda h: h.memset(kvh[:], 0.0), (), ["kvh"])
                for r in range(NCORES):
                    T.dma("sp", "xl", B().uw[:, 0:2, :].rearrange("p h w -> p (h w)"), g8v[0][:, r, :], reads=["d:x_g80"], writes=["uw"])
                    T.dma("sp", "xl", B().uw[:, 2:4, :].rearrange("p h w -> p (h w)"), g8v[1][:, r, :], reads=["d:x_g81"], writes=["uw"])
                    T.dma("sp", "xl2", zs[:, 0:256], g8v[2][:, r, 0:256], reads=["d:x_g82"], writes=["zs"])
                    if r < NCORES - 1:
                        STT(kvh[:].rearrange("p a b -> p (a b)"), zs[:, 0:256], C("ohp", 128, r, r + 1), kvh[:].rearrange("p a b -> p (a b)"),
                            ALU.mult, ALU.add, ["zs", "cst", "kvh"], ["kvh"])
                    if r < NCORES - 1:
                        PE([(ps[1][:, hh * 128:(hh + 1) * 128], B().uw[:, hh, 0:128], Ssm[:, hh, :], True, True) for hh in range(4)],
                           ["uw", "Ssm"], [pk(1)])
                        TT(B().AT[:], ps[1][:, :].rearrange("p (h d) -> p h d", h=4), B().uw[:, :, 128:256], ALU.add, [pk(1), "uw"], ["AT"])
                        TT(B().AT[:], B().AT[:], Ssm[:], ALU.subtract, ["AT", "Ssm"], ["AT"])
                        STT(Ssm[:], B().AT[:], C("mlt", 128, r, r + 1), Ssm[:], ALU.mult, ALU.add, ["AT", "cst", "Ssm"], ["Ssm"])
                    PE([(ps[2][:, hh * 128:(hh + 1) * 128], B().uw[:, hh, 0:128], Tt[:, hh, 128:256], True, True) for hh in range(4)],
                       ["uw", "state"], [pk(2)])
                    TT(Tt[:, :, 128:256], ps[2][:, :].rearrange("p (h d) -> p h d", h=4), B().uw[:, :, 128:256], ALU.add, [pk(2), "uw"], ["state"])
                T.dma("sp", "outd", o_deltap[:, li], Tt[:, :, 128:256], reads=["state"])
                CP(Tt[:, :, 128:256], Ssm[:], ["Ssm", "state"], ["state"])
                halo_kv_to_window = True
                if halo_kv_to_window:
                    for hc in range(2):
                        slot = 1 + hc
                        rows = slice(hc * 64, hc * 64 + 64)
                        def fn(h, rows=rows):
                            ins = None
                            for g in range(2):
                                ins = h.matmul(ps[7][0:64, 256 + g * 64:256 + g * 64 + 64], lhsT=kvh[rows, 0, g * 64:(g + 1) * 64],
                                               rhs=C("ident")[rows, rows], start=True, stop=True)
                            return ins
                        T.op("pe", fn, ["kvh", "cst"], [pk(7)])
                        CP(kTw[:, :, slot, :], ps[7][0:64, 256:384].rearrange("p (g t) -> p g t", g=2), [pk(7)], ["kTw"])
                        def fn2(h, rows=rows):
                            return h.matmul(ps[7][0:64, 384:512], lhsT=C("ident")[rows, rows], rhs=kvh[rows, 1, :], start=True, stop=True)
                        T.op("pe", fn2, ["kvh", "cst"], [pk(7)])
                        for g in range(2):
                            CP(Vw[:, slot, g, 0, 0:64], ps[7][0:64, 384 + g * 64:384 + g * 64 + 64], [pk(7)], ["Vw"])
                            CP(Vw[:, slot, g, 1, 64:128], ps[7][0:64, 384 + g * 64:384 + g * 64 + 64], [pk(7)], ["Vw"])

                chk(7)
                mixP = SETS[0].ctmp_full[:].bitcast(BF16).rearrange("p (q k t) -> p q k t", q=2, k=8)
                hbs = [S0.hb, hb1, hbA, hbB]
                prevB = None
                for b0 in range(0, NB, 2):
                    pp = (b0 // 2) % 2
                    mvp = mixP[:, pp]
                    mk = ("mixP", pp)

                    def genA(b, q, mvp=mvp, mk=mk):
                        make_h(b * 64, 64, b % 4)
                        yield
                        yield from gdn_corr(64, b, b % 4, mvp[:, :, (b % 2) * 64:(b % 2 + 1) * 64], mk)

                    def genB2(b0=b0, mvp=mvp, mk=mk):
                        for b in (b0, b0 + 1):
                            yield from swa_proj(64, b, b % 3, b % 4)
                            yield from swa_attn(64, [64, 64, 64], "mask0" if b == 0 else ("mask1" if b == 1 else None), b % 4,
                                                mvp[:, :, (b % 2) * 64:(b % 2 + 1) * 64], mk)
                        yield from out_proj(128, b0 * 64, 0, mvp, mk)
                    run((genA(b0, 0), 0), (genA(b0 + 1, 1), 1), prevB)
                    prevB = genB2()
                run(prevB)
                hbs = [S0.hb, hb1]
                chk(12)
                T.barrier()
                T.dma("sp", "ldc", halo[:], d_cc[:, li], writes=["halo"])
                T.dma("sp", "ldc", Tt[:, :, 128:256], d_st[:, li], writes=["state"])
                T.dma("sp", "ldc", ckb[:, 0, :], d_ck[:, li], writes=["ckb"])
                T.dma("sp", "ldc", ckb[:, 1, :], d_cv[:, li], writes=["ckb"])
                make_h(TP, 32, 0)
                run(gdn_pre(32, "32", 0))
                T.dma("sp", "outc", o_convs[:, li], halo[:], reads=["halo"])
                run(gdn_out(32, Tt, 128, TP, 0))
                run(chain_step(32, Tt, 128, 128))
                T.dma("sp", "outd", o_deltas[:, li], Tt[:, :, 128:256], reads=["state"])
                for hc in range(2):
                    rows = slice(hc * 64, hc * 64 + 64)
                    def fn(h, rows=rows):
                        ins = None
                        for g in range(2):
                            ins = h.matmul(ps[7][0:64, 256 + g * 64:256 + g * 64 + 64], lhsT=ckb[rows, 0, g * 64:(g + 1) * 64],
                                           rhs=C("ident")[rows, rows], start=True, stop=True)
                        return ins
                    T.op("pe", fn, ["ckb", "cst", "kTw"], [pk(7)])
                    CP(kTw[:, :, hc, :], ps[7][0:64, 256:384].rearrange("p (g t) -> p g t", g=2), [pk(7)], ["kTw"])
                    def fn2(h, rows=rows):
                        return h.matmul(ps[7][0:64, 384:512], lhsT=C("ident")[rows, rows], rhs=ckb[rows, 1, :], start=True, stop=True)
                    T.op("pe", fn2, ["ckb", "cst"], [pk(7)])
                    for g in range(2):
                        CP(Vw[:, hc, g, 0, 0:64], ps[7][0:64, 384 + g * 64:384 + g * 64 + 64], [pk(7)], ["Vw"])
                        CP(Vw[:, hc, g, 1, 64:128], ps[7][0:64, 384 + g * 64:384 + g * 64 + 64], [pk(7)], ["Vw"])
                run(swa_proj(32, 32, 2, 0, kout=o_ks[96:128, li], vout=o_vs[96:128, li]))
                T.dma("sp", "outk", o_ks[0:96, li], ckb[32:128, 0, :], reads=["ckb"])
                T.dma("sp", "outk", o_vs[0:96, li], ckb[32:128, 1, :], reads=["ckb"])
                run(swa_attn(32, [64, 64, 32], None, 0))
                run(out_proj(32, TP, 0))
                T.barrier()
                p2.close()
                T.kmap = None

        psb16 = [p[:].bitcast(BF16) for p in ps]

        with ExitStack() as st0:
            ztile = sb("ztile", (128, 512), F32, st0)
            T.op("dve", lambda h: h.memset(ztile[:], 0.0), (), ["ztile"])
            for k in range(3):
                T.dma("sp", "x_in", x_in[k].ap(), ztile[:], reads=["ztile"], writes=[f"d:x_in{k}"])
            T.barrier()
        _stage = _KSTAGE
        for li in range(nlayers):
            if _stage in ("all", "ffn"):
                ffn(li * 2, li * 3)
            if _stage in ("all", "mix") and not _STOPPED[0]:
                try:
                    mixer(li)
                except _Stop:
                    T.barrier()
            if _stage in ("all",) and not _STOPPED[0]:
                ffn(li * 2 + 1, li * 3 + 2)

        with ExitStack() as st:
          if not _STOPPED[0]:
              sq = sb("sqF", (128, KC, FT), BF16, st)
              rstd = sb("rstdF", (128, FT), F32, st)
              yb = sb("yb", (128, KC, FT), F32, st)
              T.barrier()
              for t in range(NFT):
                  cs = slice(t * FT, (t + 1) * FT)
                  ACT(sq[:], xT[:, :, cs], AF.Square, ["xT"], ["sqF"])
                  PE([(ps[7][:, 0:FT], onesb[:], sq[:, kc, :], kc == 0, kc == KC - 1) for kc in range(KC)], ["sqF", "onesb"], [pk(7)])
                  RSQRT(rstd[:], ps[7][:, 0:FT], 1.0 / DM, 0, [pk(7), "epst"], ["rstdF"])
                  for kc in range(KC):
                      STT(yb[:, kc, :], xT[:, kc, cs], nrm[:, L * 3, kc:kc + 1], rstd[:], ALU.mult, ALU.mult,
                          ["xT", "nrm", "rstdF"], ["yb"])
                  T.dma("sp", "outy", o_y[:, :, cs], yb[:], reads=["yb"])
              T.barrier()
        with nc.Block() as block:
            T.emit(block)
    return nc


_NC_CACHE = {}


def _layout_inputs(inp):
    f = np.float32
    g = lambda k: np.asarray(inp[k], dtype=f)
    shared = {}
    wfi = np.stack([g("ff1_w_in"), g("ff2_w_in")], 1).reshape(L * 2, KC, 128, 2 * DFF).transpose(0, 2, 1, 3)
    shared["wfi"] = np.ascontiguousarray(wfi if _KSTAGE != "mix" else wfi[:2])
    wfo = np.stack([g("ff1_w_out"), g("ff2_w_out")], 1).reshape(L * 2, HC, 128, DM).transpose(0, 2, 1, 3)
    shared["wfo"] = np.ascontiguousarray(wfo if _KSTAGE != "mix" else wfo[:2])
    shared["wmi"] = np.ascontiguousarray(g("w_mix_in").reshape(L, KC, 128, DIN).transpose(0, 2, 1, 3))
    shared["wmo"] = np.ascontiguousarray(g("w_mix_out").reshape(L, KC, 128, DM).transpose(0, 2, 1, 3))
    nrm = np.zeros((L * 3 + 1, DM), f)
    for l in range(L):
        nrm[l * 3 + 0] = g("norm_ff1")[l]
        nrm[l * 3 + 1] = g("norm_mix")[l]
        nrm[l * 3 + 2] = g("norm_ff2")[l]
    nrm[L * 3] = g("norm_final")
    shared["nrm"] = np.ascontiguousarray(nrm.reshape(L * 3 + 1, KC, 128).transpose(2, 0, 1))
    shared["cw"] = np.ascontiguousarray(g("conv_w").reshape(L, 4, 12, 128).transpose(3, 0, 2, 1))
    row = np.concatenate([g("a_log"), g("dt_bias"), g("sinks"), g("gnorm_w")], axis=1)
    shared["rowc"] = np.ascontiguousarray(np.broadcast_to(row[None], (128, L, 144)))
    maps = []
    xp = g("x_prompt")[0]
    xs = g("x_sample")
    for c in range(NCORES):
        m = dict(shared)
        xc = np.concatenate([xp[c * TP:(c + 1) * TP], xs[c]], 0)
        m["xT"] = np.ascontiguousarray(xc.T.reshape(KC, 128, NT).transpose(1, 0, 2))
        m["cconv"] = np.ascontiguousarray(g("cache_conv")[:, c].reshape(L, 3, 12, 128).transpose(3, 0, 2, 1))
        m["state"] = np.ascontiguousarray(g("state_delta")[:, c].transpose(2, 0, 1, 3))
        m["ck"] = np.ascontiguousarray(g("cache_k")[:, c].reshape(L, 128, 128).transpose(1, 0, 2))
        m["cv"] = np.ascontiguousarray(g("cache_v")[:, c].reshape(L, 128, 128).transpose(1, 0, 2))
        m["cst"] = _make_consts(c)
        maps.append(m)
    return maps


def kernel(**inputs):
    if "nc" not in _NC_CACHE:
        _NC_CACHE["nc"] = build_program()
    nc = _NC_CACHE["nc"]
    maps = _layout_inputs(inputs)
    res = run_bass_kernel_spmd(nc, maps, core_ids=list(range(NCORES)))
    R = res.results
    f = np.float32
    yp = np.zeros((1, NCORES * TP, DM), f)
    ys = np.zeros((NCORES, TS, DM), f)
    for c in range(NCORES):
        y = np.asarray(R[c]["yT"]).transpose(1, 0, 2).reshape(DM, NT).T
        yp[0, c * TP:(c + 1) * TP] = y[:TP]
        ys[c] = y[TP:]

    def conv_out(a):
        return np.ascontiguousarray(np.asarray(a).transpose(1, 3, 2, 0).reshape(L, 3, 1536))

    def delta_out(a):
        return np.ascontiguousarray(np.asarray(a).transpose(1, 2, 0, 3))

    def kv_out(a):
        return np.ascontiguousarray(np.asarray(a).transpose(1, 0, 2).reshape(L, 128, 2, 64))
    last = R[NCORES - 1]
    conv_p = conv_out(last["convp"])[:, None]
    delta_p = delta_out(last["deltap"])[:, None]
    k_p = kv_out(last["kp"])[:, None]
    v_p = kv_out(last["vp"])[:, None]
    conv_s = np.stack([conv_out(R[c]["convs"]) for c in range(NCORES)], 1)
    delta_s = np.stack([delta_out(R[c]["deltas"]) for c in range(NCORES)], 1)
    k_s = np.stack([kv_out(R[c]["ks"]) for c in range(NCORES)], 1)
    v_s = np.stack([kv_out(R[c]["vs"]) for c in range(NCORES)], 1)
    return (yp, ys, conv_p.astype(f), delta_p.astype(f), k_p.astype(f), v_p.astype(f),
            conv_s.astype(f), delta_s.astype(f), k_s.astype(f), v_s.astype(f))
```

```python
import numpy as np
from contextlib import ExitStack
import concourse.bass as bass
import concourse.mybir as mybir
from concourse.bass_utils import run_bass_kernel_spmd

F32 = mybir.dt.float32
BF16 = mybir.dt.bfloat16
AF = mybir.ActivationFunctionType
ALU = mybir.AluOpType
AX = mybir.AxisListType

NCORES = 8
L = 4
DM = 1024
KC = 8
TP = 2048
TS = 32
NT = TP + TS
DFF = 2816
HC = 22
DIN = 2824
EPS = 1e-6
NB = TP // 64
FT = 416
NFT = NT // FT
QS = 128.0 ** -0.5
NEG = -30000.0


_STOPPED = [False]


class _Eng:
    def __init__(self, name, sem):
        self.name = name
        self.sem = sem
        self.count = 0
        self.waited = {}
        self.prog = []


class _Slot:
    def __init__(self, name, sem):
        self.name = name
        self.sem = sem
        self.count = 0


class Tracker:
    def __init__(self, sems):
        self._sems = list(sems)
        self.eng = {n: _Eng(n, self._sems.pop()) for n in ("pe", "act", "dve", "pool", "sp")}
        self.last_write = {}
        self.readers = {}
        self.slots = {}

    def slot(self, name):
        if name not in self.slots:
            self.slots[name] = _Slot(name, self._sems.pop())
        return self.slots[name]

    def _wait(self, e, tok, same_ok):
        if tok is None:
            return
        if tok[0] == "e":
            _, en, seq = tok
            if en == e.name and same_ok:
                return
            if e.waited.get(en, 0) >= seq:
                return
            e.waited[en] = seq
            e.prog.append(("w", self.eng[en].sem, seq))
        else:
            sl = tok[1]
            key = "slot:" + sl.name
            if e.waited.get(key, 0) >= sl.count:
                return
            e.waited[key] = sl.count
            e.prog.append(("w", sl.sem, sl.count))

    def _deps(self, e, reads, writes):
        for k in reads:
            self._wait(e, self.last_write.get(k), False)
        for k in writes:
            self._wait(e, self.last_write.get(k), True)
            for t in self.readers.get(k, ()):
                self._wait(e, t, True)

    def _record(self, tok, reads, writes):
        for k in writes:
            self.last_write[k] = tok
            self.readers[k] = []
        for k in reads:
            lst = self.readers.setdefault(k, [])
            if tok[0] == "e":
                lst[:] = [t for t in lst if not (t[0] == "e" and t[1] == tok[1])]
            elif tok in lst:
                continue
            lst.append(tok)

    kmap = None

    def _km(self, keys):
        if self.kmap is None:
            return list(keys)
        return [self.kmap(k) for k in keys]

    def op(self, engname, fn, reads=(), writes=()):
        if _STOPPED[0]:
            return
        reads, writes = self._km(reads), self._km(writes)
        e = self.eng[engname]
        if engname in ("act", "dve"):
            writes = list(writes) + [("psl", k[1]) for k in list(reads) + list(writes)
                                     if isinstance(k, tuple) and k[0] == "ps"]
        self._deps(e, reads, writes)
        e.count += 1
        e.prog.append(("o", fn, e.sem, 1))
        self._record(("e", engname, e.count), reads, writes)

    def dma(self, qname, slotname, out, in_, reads=(), writes=(), **kw):
        if _STOPPED[0]:
            return
        reads, writes = self._km(reads), self._km(writes)
        e = self.eng[qname]
        sl = self.slot(slotname)
        self._deps(e, reads, writes)
        e.prog.append(("o", (lambda h, out=out, in_=in_, kw=kw: h.dma_start(out=out, in_=in_, **kw)), sl.sem, 16))
        sl.count += 16
        self._record(("d", sl), reads, writes)

    def custom(self, qname, slotname, fn, inc, reads=(), writes=()):
        if _STOPPED[0]:
            return
        e = self.eng[qname]
        sl = self.slot(slotname)
        self._deps(e, reads, writes)
        e.prog.append(("o", fn, sl.sem, inc))
        sl.count += inc
        self._record(("d", sl), reads, writes)

    def barrier(self):
        for e in self.eng.values():
            for en, o in self.eng.items():
                if o.count > 0 and en != e.name:
                    self._wait(e, ("e", en, o.count), False)
            for sl in self.slots.values():
                if sl.count > 0:
                    self._wait(e, ("d", sl), False)

    def emit(self, block):
        def run(e):
            def body(h):
                for it in e.prog:
                    if it[0] == "w":
                        h.wait_ge(it[1], it[2])
                    else:
                        it[1](h).then_inc(it[2], it[3])
            return body
        block.tensor(run(self.eng["pe"]))
        block.scalar(run(self.eng["act"]))
        block.vector(run(self.eng["dve"]))
        block.gpsimd(run(self.eng["pool"]))
        block.sync(run(self.eng["sp"]))


def _const_layout():
    off = {}
    n = 0
    for name, w in (("ident", 128), ("ones", 128),
                    ("triU64", 64), ("mL64", 64), ("mU64", 64), ("mI64", 64),
                    ("triU32", 64), ("mL32", 64), ("mU32", 64), ("mI32", 64),
                    ("sel64", 128), ("sel32", 128), ("mask0", 192), ("mask1", 192),
                    ("cos", 33 * 8), ("sin", 33 * 8), ("ohp", 8), ("mlt", 8)):
        off[name] = (n, w)
        n += w
    return off, n


COFF, CN = _const_layout()


def _make_consts(core):
    c = np.zeros((128, CN), np.float32)

    def put(name, arr):
        o, w = COFF[name]
        c[:arr.shape[0], o:o + arr.shape[1]] = arr
    put("ident", np.eye(128, dtype=np.float32))
    put("ones", np.ones((128, 128), np.float32))
    for sz, sfx in ((64, "64"), (32, "32")):
        j = np.arange(sz)[:, None]
        i = np.arange(sz)[None, :]
        put("triU" + sfx, (j <= i).astype(np.float32))
        put("mL" + sfx, -(j > i).astype(np.float32))
        put("mU" + sfx, -(j < i).astype(np.float32))
        put("mI" + sfx, (j <= i).astype(np.float32) * QS)
    s64 = np.zeros((64, 128), np.float32); s64[63, :] = 1.0
    s32 = np.zeros((32, 128), np.float32); s32[31, :] = 1.0
    put("sel64", s64)
    put("sel32", s32)
    hv = NEG if core == 0 else 0.0
    m0 = np.zeros((64, 192), np.float32); m0[:, 64:192] = hv
    m1 = np.zeros((64, 192), np.float32); m1[:, 128:192] = hv
    put("mask0", m0)
    put("mask1", m1)
    inv = np.power(np.float32(500000.0), -np.arange(0, 16, 2, dtype=np.float32) / np.float32(16)).astype(np.float32)
    cos = np.zeros((64, 33, 8), np.float32)
    sin = np.zeros((64, 33, 8), np.float32)
    for b in range(33):
        if b < 32:
            pos = (core * TP + b * 64 + np.arange(64)).astype(np.float32)
        else:
            pos = (4096 + np.arange(32)).astype(np.float32)
        ang = pos[:, None] * inv[None, :]
        cos[:len(pos), b] = np.cos(ang)
        sin[:len(pos), b] = np.sin(ang)
    put("cos", cos.reshape(64, -1))
    put("sin", sin.reshape(64, -1))
    ohp = np.zeros((128, 8), np.float32)
    if core > 0:
        ohp[:, core - 1] = 1.0
    mlt = np.zeros((128, 8), np.float32)
    mlt[:, :core] = 1.0
    put("ohp", ohp)
    put("mlt", mlt)
    return c


class _Stop(Exception):
    pass


import os as _os
_KSTOP = int(_os.environ.get("KSTOP", "0"))
_KSTAGE = _os.environ.get("KSTAGE", "all")


def chk(n):
    if _KSTOP == n:
        _STOPPED[0] = True


def build_program(nlayers=L):
    nc = bass.Bass("TRN2", target_bir_lowering=False, dynamic_dma_scratch_size=8192)

    def din(name, shape):
        return nc.dram_tensor(name, list(shape), F32, kind="ExternalInput").ap()

    def dout(name, shape):
        return nc.dram_tensor(name, list(shape), F32, kind="ExternalOutput").ap()

    d_x = din("xT", (128, KC, NT))
    LF = 1 if _KSTAGE == "mix" else L
    d_wfi = din("wfi", (LF * 2, 128, KC, 2 * DFF))
    d_wfo = din("wfo", (LF * 2, 128, HC, DM))
    d_wmi = din("wmi", (L, 128, KC, DIN))
    d_wmo = din("wmo", (L, 128, KC, DM))
    d_nrm = din("nrm", (128, L * 3 + 1, KC))
    d_cw = din("cw", (128, L, 12, 4))
    d_row = din("rowc", (128, L, 144))
    d_cc = din("cconv", (128, L, 12, 3))
    d_st = din("state", (128, L, 4, 128))
    d_ck = din("ck", (128, L, 128))
    d_cv = din("cv", (128, L, 128))
    d_cst = din("cst", (128, CN))
    o_y = dout("yT", (128, KC, NT))
    o_convp = dout("convp", (128, L, 12, 3))
    o_deltap = dout("deltap", (128, L, 4, 128))
    o_kp = dout("kp", (128, L, 128))
    o_vp = dout("vp", (128, L, 128))
    o_convs = dout("convs", (128, L, 12, 3))
    o_deltas = dout("deltas", (128, L, 4, 128))
    o_ks = dout("ks", (128, L, 128))
    o_vs = dout("vs", (128, L, 128))
    XW = 512
    x_in = [nc.dram_tensor(f"x_in{k}", [128, XW], F32) for k in range(3)]
    x_g4 = [nc.dram_tensor(f"x_g4{k}", [4 * 128, XW], F32) for k in range(3)]
    x_g8 = [nc.dram_tensor(f"x_g8{k}", [8 * 128, XW], F32) for k in range(3)]
    sp_o = nc.dram_tensor("sp_o", [NB, 64, 512], F32)
    sp_z = nc.dram_tensor("sp_z", [NB, 128, 256], F32)
    G4 = [[0, 1, 2, 3], [4, 5, 6, 7]]
    G2 = [[0, 4], [1, 5], [2, 6], [3, 7]]

    es = ExitStack()
    with es:
        _uid = [0]

        def sb(name, shape, dt=F32, stack=None):
            _uid[0] += 1
            return (stack or es).enter_context(nc.sbuf_tensor(f"s{_uid[0]}_{name}", list(shape), dt))

        sems = [es.enter_context(nc.semaphore(f"s{i}")) for i in range(60)]
        T = Tracker(sems)
        ps = [es.enter_context(nc.psum_tensor(f"ps{i}", [128, 512], F32)) for i in range(8)]

        def pk(i):
            return ("ps", i)

        xT = sb("xT", (128, KC, NT))
        cst = sb("cst", (128, CN))
        nrm = sb("nrm", (128, L * 3 + 1, KC))
        cwt = sb("cwt", (128, L, 12, 4))
        rowc = sb("rowc", (128, L, 144))
        identb = sb("identb", (128, 128), BF16)
        onesb = sb("onesb", (128, 128), BF16)
        ea = sb("ea", (128, L, 4))

        def C(name, rows=128, c0=0, c1=None):
            o, w = COFF[name]
            c1 = w if c1 is None else c1
            return cst[0:rows, o + c0:o + c1]

        T.dma("sp", "ld0", xT[:], d_x, writes=["xT"])
        T.dma("sp", "ld1", cst[:], d_cst, writes=["cst"])
        T.dma("sp", "ld1", nrm[:], d_nrm, writes=["nrm"])
        T.dma("sp", "ld1", cwt[:], d_cw, writes=["cwt"])
        T.dma("sp", "ld1", rowc[:], d_row, writes=["rowc"])
        T.op("dve", lambda h: h.tensor_copy(out=identb[:], in_=C("ident")), reads=["cst"], writes=["identb"])
        T.op("dve", lambda h: h.tensor_copy(out=onesb[:], in_=C("ones")), reads=["cst"], writes=["onesb"])
        T.op("act", lambda h: h.activation(out=ea[:], in_=rowc[:, :, 0:4], func=AF.Exp), reads=["rowc"], writes=["ea"])

        def PE(mms, reads, writes):
            def fn(h, mms=mms):
                ins = None
                for (o, l, r, st, sp_) in mms:
                    ins = h.matmul(o, lhsT=l, rhs=r, start=st, stop=sp_)
                return ins
            T.op("pe", fn, reads, writes)

        def TR(outs_ins, ident, reads, writes):
            def fn(h, oi=outs_ins, ident=ident):
                ins = None
                for (o, i) in oi:
                    ins = h.transpose(out=o, in_=i, identity=ident)
                return ins
            T.op("pe", fn, reads, writes)

        def ACT(out, in_, func, reads, writes, bias=None, scale=1.0, accum=None):
            def fn(h):
                kw = {}
                if bias is not None:
                    kw["bias"] = bias
                if accum is not None:
                    kw["accum_out"] = accum
                return h.activation(out=out, in_=in_, func=func, scale=scale, **kw)
            T.op("act", fn, reads, writes)

        def TT(out, in0, in1, op, reads, writes, eng="dve"):
            T.op(eng, lambda h: h.tensor_tensor(out=out, in0=in0, in1=in1, op=op), reads, writes)

        def TS_(out, in0, s1, s2, op0, op1, reads, writes):
            if s2 is None:
                T.op("dve", lambda h: h.tensor_scalar(out=out, in0=in0, scalar1=s1, scalar2=None, op0=op0), reads, writes)
            else:
                T.op("dve", lambda h: h.tensor_scalar(out=out, in0=in0, scalar1=s1, scalar2=s2, op0=op0, op1=op1), reads, writes)

        def STT(out, in0, scalar, in1, op0, op1, reads, writes):
            T.op("dve", lambda h: h.scalar_tensor_tensor(out=out, in0=in0, scalar=scalar, in1=in1, op0=op0, op1=op1), reads, writes)

        def CP(out, in_, reads, writes, eng="dve"):
            if eng == "act":
                ACT(out, in_, AF.Copy, reads, writes)
            else:
                T.op("dve", lambda h: h.tensor_scalar(out=out, in0=in_, scalar1=1.0, scalar2=None, op0=ALU.mult), reads, writes)

        def RSQRT(out, in_, scale, eps, reads, writes):
            ACT(out, in_, AF.Ln, reads, writes, bias=eps_ap(eps, out.shape[0]), scale=scale)
            ACT(out, out, AF.Exp, writes, writes, scale=-0.5)

        epst = sb("epst", (128, 3))
        T.op("dve", lambda h: h.memset(epst[:, 0:1], EPS), (), ["epst"])
        T.op("dve", lambda h: h.memset(epst[:, 1:2], 128.0 * EPS), ["epst"], ["epst"])
        T.op("dve", lambda h: h.memset(epst[:, 2:3], 1.0), ["epst"], ["epst"])

        def eps_ap(which, rows=128):
            return epst[0:rows, which:which + 1]

        def ffn(li, widx):
            with ExitStack() as st:
                hT = sb("hT", (128, KC, NT), BF16, st)
                actT = sb("actT", (128, 11, NT), BF16, st)
                sq = sb("sqf", (128, KC, FT), BF16, st)
                rstd = sb("rstdf", (128, FT), F32, st)
                sg = sb("sg", (128, 2, FT), F32, st)
                wgu = [sb(f"wgu{i}", (128, 2, KC, 128), BF16, st) for i in range(4)]
                wo = [sb(f"wo{i}", (128, 11, 128), BF16, st) for i in range(3)]
                T.barrier()
                for t in range(NFT):
                    cs = slice(t * FT, (t + 1) * FT)
                    ACT(sq[:], xT[:, :, cs], AF.Square, ["xT"], ["sqf"])
                    PE([(ps[7][:, 0:FT], onesb[:], sq[:, kc, :], kc == 0, kc == KC - 1) for kc in range(KC)],
                       ["sqf", "onesb"], [pk(7)])
                    RSQRT(rstd[:], ps[7][:, 0:FT], 1.0 / DM, 0, [pk(7), "epst"], ["rstdf"])
                    for kc in range(KC):
                        STT(hT[:, kc, cs], xT[:, kc, cs], nrm[:, widx, kc:kc + 1], rstd[:],
                            ALU.mult, ALU.mult, ["xT", "nrm", "rstdf"], [("hT", t)])
                cnt = {"a": 0, "b": 0, "ga": 0, "gb": 0}

                def loadA(j):
                    s = cnt["a"] % 4
                    cnt["a"] += 1
                    T.dma("pool", f"wgu{s}", wgu[s][:, 0], d_wfi[li, :, :, j * 128:(j + 1) * 128], writes=[("wgu", s)])
                    T.dma("pool", f"wgu{s}", wgu[s][:, 1], d_wfi[li, :, :, DFF + j * 128:DFF + (j + 1) * 128], writes=[("wgu", s)])
                    return s

                def loadB(j0, m):
                    s = cnt["b"] % 3
                    cnt["b"] += 1
                    T.dma("pool", f"wo{s}", wo[s][:], d_wfo[li, :, j0:j0 + 11, m * 128:(m + 1) * 128], writes=[("wo", s)])
                    return s

                for piece in range(2):
                    j0 = piece * 11
                    pend = [loadA(j0 + q) for q in range(3)]
                    for jj in range(11):
                        if jj + 3 < 11:
                            pend.append(loadA(j0 + jj + 3))
                        s = pend.pop(0)
                        for t in range(NFT):
                            cs = slice(t * FT, (t + 1) * FT)
                            n = cnt["ga"] % 2
                            cnt["ga"] += 1
                            bg, bu = 2 * n, 2 * n + 1
                            PE([(ps[bg][:, 0:FT], wgu[s][:, 0, kc, :], hT[:, kc, cs], kc == 0, kc == KC - 1) for kc in range(KC)] +
                               [(ps[bu][:, 0:FT], wgu[s][:, 1, kc, :], hT[:, kc, cs], kc == 0, kc == KC - 1) for kc in range(KC)],
                               [("wgu", s), ("hT", t)], [pk(bg), pk(bu)])
                            ACT(sg[:, n, :], ps[bg][:, 0:FT], AF.Silu, [pk(bg)], [("sg", n)])
                            TT(actT[:, jj, cs], sg[:, n, :], ps[bu][:, 0:FT], ALU.mult, [("sg", n), pk(bu)], [("actT", t)])
                    pendb = [loadB(j0, m) for m in range(2)]
                    for m in range(KC):
                        if m + 2 < KC:
                            pendb.append(loadB(j0, m + 2))
                        s = pendb.pop(0)
                        for t in range(NFT):
                            cs = slice(t * FT, (t + 1) * FT)
                            n = 4 + cnt["gb"] % 3
                            cnt["gb"] += 1
                            PE([(ps[n][:, 0:FT], wo[s][:, q, :], actT[:, q, cs], q == 0, q == 10) for q in range(11)],
                               [("wo", s), ("actT", t)], [pk(n)])
                            STT(xT[:, m, cs], ps[n][:, 0:FT], 0.5, xT[:, m, cs], ALU.mult, ALU.add, [pk(n), "xT"], ["xT"])
                T.barrier()

        def allgather(k):
            T.custom("pool", "cc", lambda h, k=k: h.collective_compute(
                "AllGather", ALU.bypass, replica_groups=G4, ins=[x_in[k].ap().opt()], outs=[x_g4[k].ap().opt()]), 1,
                reads=[f"d:x_in{k}"], writes=[f"d:x_g4{k}"])
            T.custom("pool", "cc", lambda h, k=k: h.collective_compute(
                "AllGather", ALU.bypass, replica_groups=G2, ins=[x_g4[k].ap().opt()], outs=[x_g8[k].ap().opt()]), 1,
                reads=[f"d:x_g4{k}"], writes=[f"d:x_g8{k}"])

        g8v = [x_g8[k].ap().rearrange("(r p) f -> p r f", p=128) for k in range(3)]

        def mixer(li):
            with ExitStack() as st:
                CURQ = [0]
                LOCALKEYS = {"sqb", "rsb", "pre", "cvb", "ctmp", "sq2", "rs2", "qtok", "small", "dGB", "DU", "DL", "t1", "t2",
                             "Pm0", "Pm1", "Qm0", "Qm1", "Rm0", "Rm1", "Mqk", "nMqk", "uw", "kdec", "dE", "AT", "gta", "QpT",
                             "osb", "zs", "oa", "ssmA", "zst"}
                T.kmap = lambda k: (k, CURQ[0]) if (isinstance(k, str) and k in LOCALKEYS) else k
                BKS = [{0: 0, 1: 1, 2: 2, 3: 3, 4: 4, 5: 5}, {0: 0, 1: 1, 2: 2, 3: 3, 4: 4, 5: 5}]

                def PSB(i):
                    return ps[BKS[CURQ[0]][i]]

                def pkA(i):
                    return ("ps", BKS[CURQ[0]][i])

                class _NS:
                    pass

                def mkset(q, stk):
                    S = _NS()
                    S.hb = sb(f"hb{q}", (128, KC, 64), BF16, stk)
                    S.sqb = sb(f"sqb{q}", (128, KC, 64), BF16, stk)
                    S.rsb = sb(f"rsb{q}", (128, 64), F32, stk)
                    S.pre = sb(f"pre{q}", (128, 12, 67), F32, stk)
                    S.cvb = sb(f"cvb{q}", (128, 12, 64), F32, stk)
                    S.ctmp_full = sb(f"ctmp{q}", (128, 1024), F32, stk)
                    S.ctmp = S.ctmp_full[:, 0:768].rearrange("p (f t) -> p f t", f=12)
                    S.rhs_t = S.ctmp_full[0:64, :].rearrange("p (h w) -> p h w", h=4)
                    S.sq2 = sb(f"sq2{q}", (128, 8, 64), BF16, stk)
                    S.rs2 = sb(f"rs2{q}", (128, 8, 64), F32, stk)
                    S.qtok = sb(f"qtok{q}", (64, 4, 128), F32, stk)
                    S.sm_ = sb(f"small{q}", (128, 96), F32, stk)
                    S.dGB = sb(f"dGB{q}", (64, 4, 2, 64), F32, stk)
                    S.DU = sb(f"DU{q}", (64, 4, 64), F32, stk)
                    S.DL = sb(f"DL{q}", (64, 4, 64), F32, stk)
                    S.t1 = sb(f"t1{q}", (64, 4, 64), F32, stk)
                    S.t2 = sb(f"t2{q}", (64, 4, 64), F32, stk)
                    S.Pm = [sb(f"Pm{i}{q}", (64, 4, 64), F32, stk) for i in range(2)]
                    S.Qm = [sb(f"Qm{i}{q}", (64, 4, 64), F32, stk) for i in range(2)]
                    S.Rm = [sb(f"Rm{i}{q}", (64, 4, 64), F32, stk) for i in range(2)]
                    S.Mqk = sb(f"Mqk{q}", (64, 4, 64), F32, stk)
                    S.nMqk = sb(f"nMqk{q}", (64, 4, 64), F32, stk)
                    S.uw = sb(f"uw{q}", (128, 4, 256), F32, stk)
                    S.kdec = sb(f"kdec{q}", (64, 4, 128), F32, stk)
                    S.dE = sb(f"dE{q}", (64, 4, 64), F32, stk)
                    S.AT = sb(f"AT{q}", (128, 4, 128), F32, stk)
                    S.QpT = sb(f"QpT{q}", (128, 4, 64), F32, stk)
                    S.gta = sb(f"gta{q}", (128, 4), F32, stk)
                    return S

                SETS = [mkset(0, st), None]

                def B():
                    return SETS[CURQ[0]]

                hbs = [SETS[0].hb, None]
                Tt = sb("Tt", (128, 4, 256), F32, st)
                Ssm = sb("Ssm", (128, 4, 128), F32, st)
                halo = sb("halo", (128, 12, 3), F32, st)
                osb = sb("osb", (64, 512), F32, st)
                zst = sb("zst", (128, 256), F32, st)
                qb = sb("qb", (64, 10, 64), F32, st)
                rt = sb("rt", (64, 4, 10, 8), F32, st)
                vtok = sb("vtok", (64, 128), F32, st)
                qr = sb("qr", (64, 10, 64), BF16, st)
                Vw = sb("Vw", (64, 3, 2, 2, 128), BF16, st)
                kTw = sb("kTw", (64, 2, 3, 64), BF16, st)
                qT = sb("qT", (64, 8, 64), BF16, st)
                p1 = ExitStack()
                wmi1 = sb("wmi1", (128, KC, 1800), BF16, p1)
                hsel = sb("hsel", (128, 8, 36), F32, p1)
                SETS[1] = mkset(1, p1)
                hbs = [SETS[0].hb, SETS[1].hb]
                BKS[0] = {0: 0, 1: 1, 2: 2, 3: 0, 4: 0, 5: 3}
                BKS[1] = {0: 4, 1: 5, 2: 6, 3: 4, 4: 4, 5: 7}
                WS = {"w": wmi1, "gate": 1536, "kvb": 1544, "z": None, "qb": None}

                def W():
                    return WS["w"]

                T.barrier()
                for kc in range(KC):
                    T.dma("pool", "wmi", wmi1[:, kc, 0:1536], d_wmi[li, :, kc, 0:1536], writes=["wmi"])
                    T.dma("pool", "wmi", wmi1[:, kc, 1536:1544], d_wmi[li, :, kc, 2048:2056], writes=["wmi"])
                    T.dma("pool", "wmi", wmi1[:, kc, 1544:1800], d_wmi[li, :, kc, 2568:2824], writes=["wmi"])
                T.op("dve", lambda h: h.memset(Vw[:], 0.0), (), ["Vw"])

                a_dt = rowc[:, li, 4:8]
                sinks = rowc[:, li, 8:16]
                gnw = rowc[:, li, 16:144]

                def make_h(col0, P, par=0):
                    hb = hbs[par]
                    cs = slice(col0, col0 + P)
                    ACT(B().sqb[:, :, 0:P], xT[:, :, cs], AF.Square, ["xT"], ["sqb"])
                    PE([(PSB(5)[:, 0:P], onesb[:], B().sqb[:, kc, 0:P], kc == 0, kc == KC - 1) for kc in range(KC)],
                       ["sqb", "onesb"], [pkA(5)])
                    RSQRT(B().rsb[:, 0:P], PSB(5)[:, 0:P], 1.0 / DM, 0, [pkA(5), "epst"], ["rsb"])
                    for kc in range(KC):
                        STT(hb[:, kc, 0:P], xT[:, kc, cs], nrm[:, li * 3 + 1, kc:kc + 1], B().rsb[:, 0:P],
                            ALU.mult, ALU.mult, ["xT", "nrm", "rsb"], [("hb", par)])

                def proj_fm(P, ncols_from, nchunks, banks, par=0):
                    hb = hbs[par]
                    for g0 in range(0, nchunks, 4):
                        bnk = banks[g0 // 4]
                        mms = []
                        for f in range(g0, min(g0 + 4, nchunks)):
                            c0 = ncols_from + f * 128
                            for kc in range(KC):
                                mms.append((PSB(bnk)[:, (f - g0) * 128:(f - g0) * 128 + P], W()[:, kc, c0:c0 + 128],
                                            hb[:, kc, 0:P], kc == 0, kc == KC - 1))
                        PE(mms, ["wmi", ("hb", par)], [pkA(bnk)])

                def gdn_pre(P, sfx, par=0):
                    hb = hbs[par]
                    proj_fm(P, 0, 12, [0, 1, 2], par)
                    yield
                    for g0 in range(3):
                        CP(B().pre[:, g0 * 4:(g0 + 1) * 4, 3:3 + P],
                           PSB(g0)[:, 0:512].rearrange("p (f t) -> p f t", f=4)[:, :, 0:P], [pkA(g0)], ["pre"], eng="act")
                    CP(B().pre[:, :, 0:3], halo[:], ["halo"], ["pre"])
                    for j in range(4):
                        dst = B().cvb if j == 0 else B().ctmp
                        TT(dst[:, :, 0:P], B().pre[:, :, j:j + P], cwt[:, li, :, j:j + 1].broadcast_to([128, 12, P]), ALU.mult,
                           ["pre", "cwt"], ["cvb" if j == 0 else "ctmp"], eng="pool")
                        if j > 0:
                            TT(B().cvb[:, :, 0:P], B().cvb[:, :, 0:P], B().ctmp[:, :, 0:P], ALU.add, ["cvb", "ctmp"], ["cvb"], eng="pool")
                    chk(41)
                    yield
                    CP(halo[:], B().pre[:, :, P:P + 3], ["pre"], ["halo"])
                    yield "halo"
                    ACT(B().ctmp[:, :, 0:P], B().cvb[:, :, 0:P], AF.Exp, ["cvb"], ["ctmp"], scale=-1.0)
                    ACT(B().ctmp[:, :, 0:P], B().ctmp[:, :, 0:P], AF.Ln, ["ctmp", "epst"], ["ctmp"], bias=epst[:, 2:3])
                    ACT(B().ctmp[:, :, 0:P], B().ctmp[:, :, 0:P], AF.Exp, ["ctmp"], ["ctmp"], scale=-1.0)
                    TT(B().cvb[:, :, 0:P], B().cvb[:, :, 0:P], B().ctmp[:, :, 0:P], ALU.mult, ["cvb", "ctmp"], ["cvb"])
                    ACT(B().sq2[:, :, 0:P], B().cvb[:, 0:8, 0:P], AF.Square, ["cvb"], ["sq2"])
                    if P == 64:
                        PE([(PSB(3)[:, 0:512], onesb[:], B().sq2[:].rearrange("p f t -> p (f t)"), True, True)], ["sq2", "onesb"], [pkA(3)])
                    else:
                        PE([(PSB(3)[:, f * 64:f * 64 + P], onesb[:], B().sq2[:, f, 0:P], True, True) for f in range(8)],
                           ["sq2", "onesb"], [pkA(3)])
                    RSQRT(B().rs2[:, :, 0:P], PSB(3)[:, 0:512].rearrange("p (f t) -> p f t", f=8)[:, :, 0:P], 1.0, 0,
                          [pkA(3), "epst"], ["rs2"])
                    TT(B().cvb[:, 0:8, 0:P], B().cvb[:, 0:8, 0:P], B().rs2[:, :, 0:P], ALU.mult, ["cvb", "rs2"], ["cvb"])
                    chk(42)
                    yield
                    PE([(PSB(4)[0:P, 0:8], hb[:, kc, 0:P], W()[:, kc, WS["gate"]:WS["gate"] + 8], kc == 0, kc == KC - 1) for kc in range(KC)],
                       [("hb", par), "wmi"], [pkA(4)])
                    beta = B().sm_[0:P, 0:4]
                    gg = B().sm_[0:P, 4:8]
                    Gc = B().sm_[0:P, 8:12]
                    Gl = B().sm_[0:P, 12:16]
                    ex3 = B().sm_[0:P, 16:28]
                    bG = B().sm_[0:P, 28:32]
                    ACT(beta, PSB(4)[0:P, 0:4], AF.Exp, [pkA(4)], ["small"], scale=-1.0)
                    ACT(beta, beta, AF.Ln, ["small", "epst"], ["small"], bias=epst[0:P, 2:3])
                    ACT(beta, beta, AF.Exp, ["small"], ["small"], scale=-1.0)
                    TT(gg, PSB(4)[0:P, 4:8], a_dt[0:P], ALU.add, [pkA(4), "rowc"], ["small"])
                    ACT(gg, gg, AF.Exp, ["small"], ["small"])
                    ACT(gg, gg, AF.Ln, ["small", "epst"], ["small"], bias=epst[0:P, 2:3])
                    STT(gg, gg, -1.0, ea[0:P, li, :], ALU.mult, ALU.mult, ["small", "ea"], ["small"])
                    PE([(PSB(4)[0:P, 16:20], C("triU" + sfx, P, 0, P), gg, True, True),
                        (PSB(4)[0:P, 20:24], C("ones", P, 0, P), gg, True, True),
                        (PSB(4)[:, 24:28], C("ones", P, 0, 128), gg, True, True)],
                       ["small", "cst"], [pkA(4)])
                    CP(B().sm_[0:P, 8:16], PSB(4)[0:P, 16:24], [pkA(4)], ["small"])
                    CP(B().sm_[0:P, 32:36], Gc, ["small"], ["small"])
                    TT(B().sm_[0:P, 36:40], Gl, Gc, ALU.subtract, ["small"], ["small"])
                    CP(B().sm_[0:P, 40:44], Gl, ["small"], ["small"])
                    ACT(ex3, B().sm_[0:P, 32:44], AF.Exp, ["small"], ["small"])
                    ACT(B().gta[:], PSB(4)[:, 24:28], AF.Exp, [pkA(4)], ["gta"])
                    eG = B().sm_[0:P, 16:20]
                    eGlG = B().sm_[0:P, 20:24]
                    TT(bG, beta, eG, ALU.mult, ["small"], ["small"])
                    chk(43)
                    yield
                    idf = C("ident")
                    TR([(PSB(0)[0:P, f * 128:(f + 1) * 128], B().cvb[:, f, 0:P]) for f in range(4)], idf, ["cvb", "cst", "pre"], [pkA(0)])
                    TR([(PSB(1)[0:P, f * 128:(f + 1) * 128], B().cvb[:, 4 + f, 0:P]) for f in range(4)], idf, ["cvb", "cst"], [pkA(1)])
                    TR([(PSB(2)[0:P, f * 128:(f + 1) * 128], B().cvb[:, 8 + f, 0:P]) for f in range(4)], idf, ["cvb", "cst"], [pkA(2)])
                    CP(B().qtok[0:P], PSB(0)[0:P, :].rearrange("p (h d) -> p h d", h=4), [pkA(0)], ["qtok"], eng="act")
                    k3 = PSB(1)[0:P, :].rearrange("p (h d) -> p h d", h=4)
                    v3 = PSB(2)[0:P, :].rearrange("p (h d) -> p h d", h=4)
                    TT(B().rhs_t[0:P, :, 128:256], k3, bG.unsqueeze(2).broadcast_to([P, 4, 128]), ALU.mult, [pkA(1), "small"], ["ctmp"])
                    TT(B().kdec[0:P], k3, eGlG.unsqueeze(2).broadcast_to([P, 4, 128]), ALU.mult, [pkA(1), "small"], ["kdec"])
                    TT(B().rhs_t[0:P, :, 0:128], v3, beta.unsqueeze(2).broadcast_to([P, 4, 128]), ALU.mult, [pkA(2), "small"], ["ctmp"])
                    chk(44)
                    yield
                    for hh in range(4):
                        TS_(B().dGB[0:P, hh, 0, 0:P], C("ident", P, 0, P), Gc[:, hh:hh + 1], None, ALU.mult, None, ["cst", "small"], ["dGB"])
                        TS_(B().dGB[0:P, hh, 1, 0:P], C("ident", P, 0, P), beta[:, hh:hh + 1], None, ALU.mult, None, ["cst", "small"], ["dGB"])
                    if P == 64:
                        PE([(PSB(5)[0:P, 0:512], C("ones", P, 0, P), B().dGB[:].rearrange("p h w t -> p (h w t)"), True, True)],
                           ["dGB", "cst", "qtok"], [pkA(5)])
                    else:
                        PE([(PSB(5)[0:P, hh * 128 + w * 64: hh * 128 + w * 64 + P], C("ones", P, 0, P), B().dGB[0:P, hh, w, 0:P], True, True)
                            for hh in range(4) for w in range(2)], ["dGB", "cst", "qtok"], [pkA(5)])
                    GB = PSB(5)[0:P, :].rearrange("p (h w t) -> p h w t", h=4, w=2)
                    Grow = GB[:, :, 0, 0:P]
                    brow = GB[:, :, 1, 0:P]
                    PE([(PSB(3)[0:P, hh * 64:hh * 64 + P], B().cvb[:, 4 + hh, 0:P], B().cvb[:, 4 + hh, 0:P], True, True) for hh in range(4)] +
                       [(PSB(3)[0:P, 256 + hh * 64:256 + hh * 64 + P], B().cvb[:, 4 + hh, 0:P], B().cvb[:, hh, 0:P], True, True) for hh in range(4)],
                       ["cvb", "kdec", "ctmp", "rs2"], [pkA(3)])
                    yield
                    kk = PSB(3)[0:P, 0:256].rearrange("p (h t) -> p h t", h=4)[:, :, 0:P]
                    qkT = PSB(3)[0:P, 256:512].rearrange("p (h t) -> p h t", h=4)[:, :, 0:P]
                    Gb = Gc.unsqueeze(2).broadcast_to([P, 4, P])
                    TT(B().t1[0:P, :, 0:P], Grow, Gb, ALU.subtract, [pkA(5), "small"], ["t1"])
                    TS_(B().DU[0:P, :, 0:P], B().t1[0:P, :, 0:P], 0.0, None, ALU.min, None, ["t1"], ["DU"])
                    TS_(B().DL[0:P, :, 0:P], B().t1[0:P, :, 0:P], 0.0, None, ALU.max, None, ["t1"], ["DL"])
                    ACT(B().DU[0:P, :, 0:P], B().DU[0:P, :, 0:P], AF.Exp, ["DU"], ["DU"])
                    ACT(B().DL[0:P, :, 0:P], B().DL[0:P, :, 0:P], AF.Exp, ["DL"], ["DL"], scale=-1.0)
                    yield
                    mU = C("mU" + sfx, P, 0, P).unsqueeze(1).broadcast_to([P, 4, P])
                    mL = C("mL" + sfx, P, 0, P).unsqueeze(1).broadcast_to([P, 4, P])
                    mI = C("mI" + sfx, P, 0, P).unsqueeze(1).broadcast_to([P, 4, P])
                    TT(B().t1[0:P, :, 0:P], kk, B().DU[0:P, :, 0:P], ALU.mult, [pkA(3), "DU"], ["t1"])
                    TT(B().t2[0:P, :, 0:P], brow, mU, ALU.mult, [pkA(5), "cst"], ["t2"])
                    TT(B().Pm[0][0:P, :, 0:P], B().t1[0:P, :, 0:P], B().t2[0:P, :, 0:P], ALU.mult, ["t1", "t2"], ["Pm0"])
                    TT(B().t1[0:P, :, 0:P], kk, B().DL[0:P, :, 0:P], ALU.mult, [pkA(3), "DL"], ["t1"])
                    TT(B().t2[0:P, :, 0:P], mL, beta.unsqueeze(2).broadcast_to([P, 4, P]), ALU.mult, ["cst", "small"], ["t2"])
                    TT(B().Qm[0][0:P, :, 0:P], B().t1[0:P, :, 0:P], B().t2[0:P, :, 0:P], ALU.mult, ["t1", "t2"], ["Qm0"])
                    TT(B().t1[0:P, :, 0:P], qkT, B().DU[0:P, :, 0:P], ALU.mult, [pkA(3), "DU"], ["t1"])
                    TT(B().Mqk[0:P, :, 0:P], B().t1[0:P, :, 0:P], mI, ALU.mult, ["t1", "cst"], ["Mqk"])
                    TS_(B().nMqk[0:P, :, 0:P], B().Mqk[0:P, :, 0:P], -1.0, None, ALU.mult, None, ["Mqk"], ["nMqk"])
                    for hh in range(4):
                        TS_(B().dE[0:P, hh, 0:P], C("ident", P, 0, P), eG[:, hh:hh + 1], QS, ALU.mult, ALU.mult, ["cst", "small"], ["dE"])
                    chk(45)
                    TT(B().Rm[0][0:P, :, 0:P], B().Pm[0][0:P, :, 0:P], C("ident", P, 0, P).unsqueeze(1).broadcast_to([P, 4, P]), ALU.add,
                       ["Pm0", "cst"], ["Rm0"])
                    nr = 5 if P == 64 else 4
                    cur = 0
                    rc = 0
                    yield
                    for r in range(nr + 1):
                        nx = 1 - cur
                        sq_ = r < nr
                        upd = r >= 1
                        mms = []
                        if sq_:
                            if r < nr - 1:
                                mms += [(PSB(5)[0:P, hh * 64:hh * 64 + P], B().Qm[cur][0:P, hh, 0:P], B().Pm[cur][0:P, hh, 0:P], True, True) for hh in range(4)]
                            mms += [(PSB(5)[0:P, 256 + hh * 64:256 + hh * 64 + P], B().Pm[cur][0:P, hh, 0:P], B().Qm[cur][0:P, hh, 0:P], True, True) for hh in range(4)]
                        wr = [pkA(5)] if sq_ else []
                        if upd:
                            mms += [(PSB(4)[0:P, hh * 64:hh * 64 + P], B().Qm[cur][0:P, hh, 0:P], B().Rm[rc][0:P, hh, 0:P], True, True) for hh in range(4)]
                            wr.append(pkA(4))
                        PE(mms, [f"Pm{cur}", f"Qm{cur}", f"Rm{rc}", "t1", "t2", "Mqk", "gta", "small"], wr)
                        pq = PSB(5)[0:P, :].rearrange("p (w h t) -> p w h t", w=2, h=4)
                        if sq_:
                            if r < nr - 1:
                                CP(B().Pm[nx][0:P, :, 0:P], pq[:, 0, :, 0:P], [pkA(5)], [f"Pm{nx}"], eng="act")
                            CP(B().Qm[nx][0:P, :, 0:P], pq[:, 1, :, 0:P], [pkA(5)], [f"Qm{nx}"], eng="act")
                        if upd:
                            TT(B().Rm[1 - rc][0:P, :, 0:P], B().Rm[rc][0:P, :, 0:P],
                               PSB(4)[0:P, 0:256].rearrange("p (h t) -> p h t", h=4)[:, :, 0:P], ALU.add, [f"Rm{rc}", pkA(4)], [f"Rm{1 - rc}"])
                            rc = 1 - rc
                        cur = nx
                        yield
                    cur = rc
                    chk(46)
                    PE([(PSB((0, 1)[hh // 2])[0:P, (hh % 2) * 256:(hh % 2) * 256 + 256], B().Rm[cur][0:P, hh, 0:P], B().rhs_t[0:P, hh, :], True, True)
                        for hh in range(4)], [f"Rm{cur}", "ctmp"], [pkA(0), pkA(1)])
                    CP(B().uw[0:P, 0:2, :], PSB(0)[0:P, :].rearrange("p (h w) -> p h w", h=2), [pkA(0)], ["uw"], eng="act")
                    CP(B().uw[0:P, 2:4, :], PSB(1)[0:P, :].rearrange("p (h w) -> p h w", h=2), [pkA(1)], ["uw"])
                    yield
                    PE([(PSB(2)[:, hh * 128:(hh + 1) * 128], B().uw[0:P, hh, 128:256], B().kdec[0:P, hh, :], True, True) for hh in range(4)],
                       ["uw", "kdec"], [pkA(2)])
                    for hh in range(4):
                        STT(B().AT[:, hh, :], C("ident"), B().gta[:, hh:hh + 1], PSB(2)[:, hh * 128:(hh + 1) * 128], ALU.mult, ALU.subtract,
                            ["cst", "gta", pkA(2)], ["AT"])

                def chain_step(P, state, c0, width):
                    mms = []
                    for hh in range(4):
                        bnk = (0, 1)[hh // 2]
                        o0 = (hh % 2) * 256
                        mms.append((PSB(bnk)[:, o0:o0 + width], B().AT[:, hh, :], state[:, hh, c0:c0 + width], True, False))
                        mms.append((PSB(bnk)[:, o0 + width - 128:o0 + width], B().kdec[0:P, hh, :], B().uw[0:P, hh, 0:128], False, True))
                    PE(mms, ["AT", "state", "kdec", "uw"], [pkA(0), pkA(1)])
                    yield
                    for hp in range(2):
                        CP(state[:, 2 * hp:2 * hp + 2, c0:c0 + width],
                           PSB(hp)[:, :].rearrange("p (h w) -> p h w", h=2)[:, :, 0:width], [pkA(hp)], ["state"],
                           eng=("act" if hp == 0 else "dve"))

                def gdn_qp(P):
                    PE([m for hh in range(4) for m in (
                        (PSB(3)[:, hh * 64:hh * 64 + P], B().qtok[0:P, hh, :], B().dE[0:P, hh, 0:P], True, False),
                        (PSB(3)[:, hh * 64:hh * 64 + P], B().uw[0:P, hh, 128:256], B().nMqk[0:P, hh, 0:P], False, True))],
                       ["qtok", "dE", "uw", "nMqk"], [pkA(3)])
                    yield
                    CP(B().QpT[:, :, 0:P], PSB(3)[:, 0:256].rearrange("p (h t) -> p h t", h=4)[:, :, 0:P], [pkA(3)], ["QpT"], eng="act")

                def gdn_fin(P, par, mv=None, mk=None):
                    mixT = mixTs[par] if mv is None else mv
                    mk = ("mixT", par) if mk is None else mk
                    for hh in range(4):
                        ACT(B().oa[0:P, hh * 128:(hh + 1) * 128], B().osb[0:P, hh * 128:(hh + 1) * 128], AF.Square, ["osb"], ["oa", "ssmA"],
                            accum=B().ssmA[0:P, hh:hh + 1])
                    yield
                    RSQRT(B().ssmA[0:P, 4:8], B().ssmA[0:P, 0:4], 1.0 / 128, 0, ["ssmA", "epst"], ["ssmA"])
                    o3 = B().osb[0:P, :].rearrange("p (h d) -> p h d", h=4)
                    TT(o3, o3, B().ssmA[0:P, 4:8].unsqueeze(2).broadcast_to([P, 4, 128]), ALU.mult, ["osb", "ssmA"], ["osb"])
                    TT(o3, o3, gnw[0:P].unsqueeze(1).broadcast_to([P, 4, 128]), ALU.mult, ["osb", "rowc"], ["osb"])
                    TT(B().oa[0:P, :], B().osb[0:P, :], B().zs[0:P, :], ALU.mult, ["osb", "zs"], ["oa"])
                    TRb([(pbf(BKS[CURQ[0]][2])[:, hh * 64:hh * 64 + P], B().oa[0:P, hh * 128:(hh + 1) * 128]) for hh in range(4)],
                        ["oa", "identb"], [pkA(2)])
                    yield
                    CP(mixT[:, 0:4, 0:P], pbf(BKS[CURQ[0]][2])[:, 0:256].rearrange("p (h t) -> p h t", h=4)[:, :, 0:P], [pkA(2)], [mk], eng="act")

                def zproj(P, par):
                    hb = hbs[par]
                    PE([(PSB(5)[0:P, :], hb[:, kc, 0:P], W()[:, kc, WS["z"]:WS["z"] + 512], kc == 0, kc == KC - 1) for kc in range(KC)],
                       [("hb", par), "wmi"], [pkA(5)])
                    yield
                    ACT(B().zs[0:P, :], PSB(5)[0:P, :], AF.Exp, [pkA(5)], ["zs"], scale=-1.0)
                    ACT(B().zs[0:P, :], B().zs[0:P, :], AF.Ln, ["zs", "epst"], ["zs"], bias=epst[0:P, 2:3])
                    ACT(B().zs[0:P, :], B().zs[0:P, :], AF.Exp, ["zs"], ["zs"], scale=-1.0)
                    TT(B().zs[0:P, :], B().zs[0:P, :], PSB(5)[0:P, :], ALU.mult, ["zs", pkA(5)], ["zs"])

                def gdn_out(P, state, c0, col0, par=0):
                    yield from gdn_qp(P)
                    PE([m for hh in range(4) for m in (
                        (PSB(4)[0:P, hh * 128:(hh + 1) * 128], B().QpT[:, hh, 0:P], state[:, hh, c0:c0 + 128], True, False),
                        (PSB(4)[0:P, hh * 128:(hh + 1) * 128], B().Mqk[0:P, hh, 0:P], B().uw[0:P, hh, 0:128], False, True))],
                       ["QpT", "state", "Mqk", "uw"], [pkA(4)])
                    yield from zproj(P, par)
                    CP(B().osb[0:P, :], PSB(4)[0:P, :], [pkA(4)], ["osb"])
                    yield from gdn_fin(P, par)

                def gdn_loc(P, b):
                    yield from gdn_qp(P)
                    PE([(PSB(5)[:, hh * 64:hh * 64 + P], Tt[:, hh, 0:128], B().QpT[:, hh, 0:P], True, True) for hh in range(4)] +
                       [m for hh in range(4) for m in (
                        (PSB(4)[0:P, hh * 128:(hh + 1) * 128], B().QpT[:, hh, 0:P], Tt[:, hh, 128:256], True, False),
                        (PSB(4)[0:P, hh * 128:(hh + 1) * 128], B().Mqk[0:P, hh, 0:P], B().uw[0:P, hh, 0:128], False, True))],
                       ["QpT", "state", "Mqk", "uw"], [pkA(5), pkA(4)])
                    yield
                    CP(zst[:], PSB(5)[:, 0:256], [pkA(5)], ["zstS"], eng="act")
                    CP(osb[0:P, :], PSB(4)[0:P, :], [pkA(4)], ["osbS"])
                    T.dma("sp", "spill", sp_z.ap()[b], zst[:], reads=["zstS"], writes=[("d:spz", b)])
                    T.dma("sp", "spill", sp_o.ap()[b], osb[0:P, :], reads=["osbS"], writes=[("d:spo", b)])

                def gdn_corr(P, b, par, mv=None, mk=None):
                    T.dma("sp", "fill", B().zst[:], sp_z.ap()[b], reads=[("d:spz", b)], writes=["zst"])
                    T.dma("sp", "fill", B().osb[0:P, :], sp_o.ap()[b], reads=[("d:spo", b)], writes=["osb"])
                    yield from zproj(P, par)
                    PE([(PSB(4)[0:P, hh * 128:(hh + 1) * 128], B().zst[:, hh * 64:hh * 64 + P], Tt[:, hh, 128:256], True, True) for hh in range(4)],
                       ["zst", "state"], [pkA(4)])
                    yield
                    TT(B().osb[0:P, :], B().osb[0:P, :], PSB(4)[0:P, :], ALU.add, ["osb", pkA(4)], ["osb"])
                    yield from gdn_fin(P, par, mv, mk)

                def pbf(i):
                    return psb16[i]

                def TRb(oi, reads, writes):
                    def fn(h, oi=oi):
                        ins = None
                        for (o, i) in oi:
                            ins = h.transpose(out=o, in_=i, identity=identb[0:i.shape[0], 0:i.shape[0]])
                        return ins
                    T.op("pe", fn, reads, writes)

                def swa_proj(P, bidx, slot, par=0, kout=None, vout=None, kv_only=False):
                    hb = hbs[par]
                    if kv_only:
                        PE([(ps[7][0:P, 0:256], hb[:, kc, 0:P], W()[:, kc, WS["kvb"]:WS["kvb"] + 256], kc == 0, kc == KC - 1) for kc in range(KC)],
                           [("hb", par), "wmi"], [pk(7)])
                        yield
                        CP(qb[0:P, 8:10, :], ps[7][0:P, 0:128].rearrange("p (h d) -> p h d", h=2), [pk(7)], ["qb"], eng="act")
                        CP(vtok[0:P, :], ps[7][0:P, 128:256], [pk(7)], ["vtok"], eng="act")
                        co, _ = COFF["cos"]
                        so, _ = COFF["sin"]
                        cosb = cst[0:P, co + bidx * 8:co + bidx * 8 + 8].unsqueeze(1).broadcast_to([P, 2, 8])
                        sinb = cst[0:P, so + bidx * 8:so + bidx * 8 + 8].unsqueeze(1).broadcast_to([P, 2, 8])
                        x1 = qb[0:P, 8:10, 0:8]
                        x2 = qb[0:P, 8:10, 8:16]
                        TT(rt[0:P, 0, 8:10], x1, cosb, ALU.mult, ["qb", "cst"], ["rt"])
                        TT(rt[0:P, 1, 8:10], x2, sinb, ALU.mult, ["qb", "cst"], ["rt"])
                        TT(rt[0:P, 2, 8:10], x2, cosb, ALU.mult, ["qb", "cst"], ["rt"])
                        TT(rt[0:P, 3, 8:10], x1, sinb, ALU.mult, ["qb", "cst"], ["rt"])
                        TT(x1, rt[0:P, 0, 8:10], rt[0:P, 1, 8:10], ALU.subtract, ["rt"], ["qb"])
                        TT(x2, rt[0:P, 2, 8:10], rt[0:P, 3, 8:10], ALU.add, ["rt"], ["qb"])
                        return
                    PE([(ps[6][0:P, :], hb[:, kc, 0:P], W()[:, kc, WS["qb"]:WS["qb"] + 512], kc == 0, kc == KC - 1) for kc in range(KC)] +
                       [(ps[7][0:P, 0:256], hb[:, kc, 0:P], W()[:, kc, WS["kvb"]:WS["kvb"] + 256], kc == 0, kc == KC - 1) for kc in range(KC)],
                       [("hb", par), "wmi"], [pk(6), pk(7)])
                    yield
                    CP(qb[0:P, 0:8, :], ps[6][0:P, :].rearrange("p (h d) -> p h d", h=8), [pk(6)], ["qb"], eng="act")
                    CP(qb[0:P, 8:10, :], ps[7][0:P, 0:128].rearrange("p (h d) -> p h d", h=2), [pk(7)], ["qb"], eng="act")
                    CP(vtok[0:P, :], ps[7][0:P, 128:256], [pk(7)], ["vtok"], eng="act")
                    yield
                    co, _ = COFF["cos"]
                    so, _ = COFF["sin"]
                    cosb = cst[0:P, co + bidx * 8:co + bidx * 8 + 8].unsqueeze(1).broadcast_to([P, 10, 8])
                    sinb = cst[0:P, so + bidx * 8:so + bidx * 8 + 8].unsqueeze(1).broadcast_to([P, 10, 8])
                    x1 = qb[0:P, :, 0:8]
                    x2 = qb[0:P, :, 8:16]
                    TT(rt[0:P, 0], x1, cosb, ALU.mult, ["qb", "cst"], ["rt"])
                    TT(rt[0:P, 1], x2, sinb, ALU.mult, ["qb", "cst"], ["rt"])
                    TT(rt[0:P, 2], x2, cosb, ALU.mult, ["qb", "cst"], ["rt"])
                    TT(rt[0:P, 3], x1, sinb, ALU.mult, ["qb", "cst"], ["rt"])
                    TT(x1, rt[0:P, 0], rt[0:P, 1], ALU.subtract, ["rt"], ["qb"])
                    TT(x2, rt[0:P, 2], rt[0:P, 3], ALU.add, ["rt"], ["qb"])
                    CP(qr[0:P], qb[0:P], ["qb"], ["qr"])
                    for g in range(2):
                        CP(Vw[0:P, slot, g, 0, 0:64], vtok[0:P, g * 64:(g + 1) * 64], ["vtok"], ["Vw"])
                        CP(Vw[0:P, slot, g, 1, 64:128], vtok[0:P, g * 64:(g + 1) * 64], ["vtok"], ["Vw"])
                    yield
                    TRb([(pbf(6)[0:64, hh * 64:hh * 64 + P], qr[0:P, hh, :]) for hh in range(10)], ["qr", "identb"], [pk(6)])
                    yield
                    CP(qT[:, :, 0:P], pbf(6)[0:64, 0:512].rearrange("p (h t) -> p h t", h=8)[:, :, 0:P], [pk(6)], ["qT"], eng="act")
                    CP(kTw[:, :, slot, 0:P], pbf(6)[0:64, 512:640].rearrange("p (h t) -> p h t", h=2)[:, :, 0:P], [pk(6)], ["kTw"], eng="act")
                    if kout is not None:
                        T.dma("sp", "outk", kout, qb[0:P, 8:10, :].rearrange("p h d -> p (h d)"), reads=["qb"])
                        T.dma("sp", "outk", vout, vtok[0:P, :], reads=["vtok"])

                def swa_attn(P, nkeys, maskname, par=0, mv=None, mk=None):
                    mixT = mixTs[par] if mv is None else mv
                    mk = ("mixT", par) if mk is None else mk
                    NK = 192
                    for hg in range(2):
                        PE([(ps[6 + (hq // 2)][0:P, (hq % 2) * 192 + s * 64: (hq % 2) * 192 + s * 64 + nkeys[s]], qT[:, hg * 4 + hq, 0:P],
                             kTw[:, hg, s, 0:nkeys[s]], True, True) for hq in range(4) for s in range(3)],
                           ["qT", "kTw"], [pk(6), pk(7)])
                        yield
                        for hp in range(2):
                            src = ps[6 + hp][0:P, 0:384].rearrange("p (h k) -> p h k", h=2)
                            if maskname is None:
                                nkw = 128 + nkeys[2]
                                TS_(sco[0:P, 2 * hp:2 * hp + 2, 0:nkw], src[:, :, 0:nkw], 0.125, None, ALU.mult, None, [pk(6 + hp)], ["sco"])
                            else:
                                STT(sco[0:P, 2 * hp:2 * hp + 2, :], src, 0.125,
                                    C(maskname, P).unsqueeze(1).broadcast_to([P, 2, 192]), ALU.mult, ALU.add,
                                    [pk(6 + hp), "cst"], ["sco"])
                        if nkeys[2] < 64 or nkeys[1] < 64 or nkeys[0] < 64:
                            for s in range(3):
                                if nkeys[s] < 64:
                                    T.op("dve", lambda h, s=s: h.memset(sco[0:P, :, s * 64 + nkeys[s]:(s + 1) * 64], NEG), ["sco"], ["sco"])
                        mx = ssm[0:P, 8:12]
                        T.op("dve", lambda h: h.tensor_reduce(out=mx, in_=sco[0:P], axis=AX.X, op=ALU.max), ["sco"], ["ssm"])
                        TT(mx, mx, sinks[0:P, hg * 4:hg * 4 + 4], ALU.max, ["ssm", "rowc"], ["ssm"])
                        TS_(ssm[0:P, 12:16], mx, -1.0, None, ALU.mult, None, ["ssm"], ["ssm"])
                        yield
                        for hq in range(4):
                            ACT(sco[0:P, hq, :], sco[0:P, hq, :], AF.Exp, ["sco", "ssm"], ["sco", "ssm2"],
                                bias=ssm[0:P, 12 + hq:13 + hq], accum=ssm[0:P, 16 + hq:17 + hq])
                        yield
                        TT(ssm[0:P, 20:24], sinks[0:P, hg * 4:hg * 4 + 4], mx, ALU.subtract, ["ssm", "rowc"], ["ssm"])
                        ACT(ssm[0:P, 20:24], ssm[0:P, 20:24], AF.Exp, ["ssm"], ["ssm"])
                        TT(ssm[0:P, 24:28], ssm[0:P, 20:24], ssm[0:P, 16:20], ALU.add, ["ssm", "ssm2"], ["ssm"])
                        T.op("dve", lambda h: h.reciprocal(out=ssm[0:P, 24:28], in_=ssm[0:P, 24:28]), ["ssm"], ["ssm"])
                        TT(pn[0:P], sco[0:P], ssm[0:P, 24:28].unsqueeze(2).broadcast_to([P, 4, 192]), ALU.mult, ["sco", "ssm"], ["pn"])
                        yield
                        TRb([(pbf(6)[0:nkeys[s], (hq * 3 + s) * 64:(hq * 3 + s) * 64 + P], pn[0:P, hq, s * 64:s * 64 + nkeys[s]])
                             for hq in range(4) for s in range(3)], ["pn", "identb", "qT", "kTw"], [pk(6)])
                        yield
                        CP(pT[:, :, :, 0:P], pbf(6)[0:64, 0:768].rearrange("p (h s t) -> p h s t", h=4, s=3)[:, :, :, 0:P], [pk(6)], ["pT"],
                           eng="act")
                        yield
                        mms = []
                        for pr in range(2):
                            k = 0
                            for par_ in range(2):
                                for s in range(3):
                                    mms.append((ps[7][:, pr * 64:pr * 64 + P], Vw[0:nkeys[s], s, hg, par_, :],
                                                pT[0:nkeys[s], pr * 2 + par_, s, 0:P], k == 0, k == 5))
                                    k += 1
                        PE(mms, ["Vw", "pT"], [pk(7)])
                        yield
                        CP(mixT[:, 4 + hg * 2:6 + hg * 2, 0:P], ps[7][:, 0:128].rearrange("p (h t) -> p h t", h=2)[:, :, 0:P], [pk(7)], [mk],
                           eng="act")

                def out_proj(P, col0, par=0, mv=None, mk=None):
                    mixT = mixTs[par] if mv is None else mv
                    mk = ("mixT", par) if mk is None else mk
                    sw = max(64, P)
                    for mh in range(2):
                        PE([(ps[6 + mh][:, mm * sw:mm * sw + P], wmo[:, kc, (mh * 4 + mm) * 128:(mh * 4 + mm + 1) * 128], mixT[:, kc, 0:P],
                             kc == 0, kc == KC - 1) for mm in range(4) for kc in range(KC)], ["wmo", mk], [pk(6 + mh)])
                        yield
                        TT(xT[:, mh * 4:mh * 4 + 4, col0:col0 + P], xT[:, mh * 4:mh * 4 + 4, col0:col0 + P],
                           ps[6 + mh][:, 0:4 * sw].rearrange("p (m t) -> p m t", m=4)[:, :, 0:P], ALU.add, ["xT", pk(6 + mh)], ["xT"])

                def run(*gens):
                    gens = [(g if isinstance(g, tuple) else (g, 0)) for g in gens if g is not None]
                    while gens:
                        for it in list(gens):
                            CURQ[0] = it[1]
                            try:
                                next(it[0])
                            except StopIteration:
                                gens.remove(it)
                    CURQ[0] = 0

                make_h(TP - 64, 64, 0)
                PE([(ps[0][:, f * 4:f * 4 + 3], W()[:, kc, f * 128:(f + 1) * 128], hbs[0][:, kc, 61:64], kc == 0, kc == KC - 1)
                    for f in range(12) for kc in range(KC)], ["wmi", ("hb", 0)], [pk(0)])
                CP(halo[:], ps[0][:, 0:48].rearrange("p (f t) -> p f t", f=12)[:, :, 0:3], [pk(0)], ["halo"])
                T.dma("sp", "x_in", x_in[2].ap()[:, 256:292], halo[:].rearrange("p f t -> p (f t)"), reads=["halo"], writes=["d:x_in2"])
                chk(1)
                allgather(2)
                T.dma("sp", "xl", hsel[:], g8v[2][:, :, 256:292], reads=["d:x_g82"], writes=["hsel"])
                hf = halo[:].rearrange("p f t -> p (f t)")
                T.op("dve", lambda h: h.memset(hf, 0.0), ["halo"], ["halo"])
                for r in range(NCORES - 1):
                    STT(hf, hsel[:, r, :], C("ohp", 128, r, r + 1), hf, ALU.mult, ALU.add, ["hsel", "cst", "halo"], ["halo"])

                chk(2)
                T.op("dve", lambda h: h.memset(Tt[:], 0.0), ["state"], ["state"])
                for hh in range(4):
                    CP(Tt[:, hh, 0:128], C("ident"), ["cst", "state"], ["state"])
                order = {"halo": 0, "chain": 0}

                def p1_block(b, q):
                    while order["halo"] < b:
                        yield
                    make_h(b * 64, 64, q)
                    g = gdn_pre(64, "64", q)
                    for _ in g:
                        if order["halo"] == b and _ == "halo":
                            order["halo"] = b + 1
                        yield
                    order["halo"] = max(order["halo"], b + 1)
                    while order["chain"] < b:
                        yield
                    yield from gdn_loc(64, b)
                    yield from chain_step(64, Tt, 0, 256)
                    order["chain"] = b + 1
                    if b >= NB - 2:
                        yield from swa_proj(64, b, b % 3, q, kv_only=True)
                        hc = b - (NB - 2)
                        T.dma("sp", "x_in", x_in[2].ap()[hc * 64:(hc + 1) * 64, 0:128], qb[:, 8:10, :].rearrange("p h d -> p (h d)"),
                              reads=["qb"], writes=["d:x_in2"])
                        T.dma("sp", "x_in", x_in[2].ap()[hc * 64:(hc + 1) * 64, 128:256], vtok[:], reads=["vtok"], writes=["d:x_in2"])
                        T.dma("sp", "outk", o_kp[hc * 64:(hc + 1) * 64, li], qb[:, 8:10, :].rearrange("p h d -> p (h d)"), reads=["qb"])
                        T.dma("sp", "outk", o_vp[hc * 64:(hc + 1) * 64, li], vtok[:], reads=["vtok"])

                def p1_stream(q):
                    for b in range(q, NB, 2):
                        yield from p1_block(b, q)
                run((p1_stream(0), 0), (p1_stream(1), 1))
                T.dma("sp", "outc", o_convp[:, li], halo[:], reads=["halo"])
                chk(6)
                T.barrier()
                p1.close()
                SETS[1] = None
                p2 = ExitStack()
                wmi = sb("wmi", (128, KC, DIN), BF16, p2)
                wmo = sb("wmo", (128, KC, DM), BF16, p2)
                hb1 = sb("hb1b", (128, KC, 64), BF16, p2)
                hbs = [SETS[0].hb, hb1]
                zs = sb("zs", (128, 512), F32, p2)
                oa = sb("oa", (64, 512), BF16, p2)
                mixTs = [sb(f"mixT{i}", (128, 8, 64), BF16, p2) for i in range(2)]
                sco = sb("sco", (64, 4, 192), F32, p2)
                ssmA = sb("ssmA", (64, 8), F32, p2)
                pn = sb("pn", (64, 4, 192), BF16, p2)
                pT = sb("pT", (64, 4, 3, 64), BF16, p2)
                ssm = sb("ssm", (64, 32), F32, p2)
                kvh = sb("kvh", (128, 2, 128), F32, p2)
                ckb = sb("ckb", (128, 2, 128), F32, p2)
                S0 = SETS[0]
                S0.osb, S0.zst, S0.zs, S0.oa, S0.ssmA = osb, zst, zs, oa, ssmA
                S1 = _NS()
                S1.sqb = S0.cvb[:].rearrange("p f t -> p (f t)").bitcast(BF16)[:, 0:512].rearrange("p (k t) -> p k t", k=8)
                S1.rsb = S0.QpT[:, 0, :]
                S1.osb = S0.uw[:, 0:2, :].rearrange("p h w -> p (h w)")
                S1.zs = S0.uw[:, 2:4, :].rearrange("p h w -> p (h w)")
                S1.zst = S0.AT[:, 0:2, :].rearrange("p h d -> p (h d)")
                S1.oa = S0.kdec[:].rearrange("p h d -> p (h d)").bitcast(BF16)[:, 0:512]
                S1.ssmA = S0.sm_[0:64, 0:8]
                SETS[1] = S1
                hbA = S0.sq2
                hbB = S0.pre[:].rearrange("p f t -> p (f t)").bitcast(BF16)[:, 0:512].rearrange("p (k t) -> p k t", k=8)
                WS.update({"w": wmi, "gate": 2048, "kvb": 2568, "z": 1536, "qb": 2056})
                BKS[0] = {0: 0, 1: 1, 2: 2, 3: 3, 4: 4, 5: 5}
                BKS[1] = {0: 0, 1: 1, 2: 3, 3: 3, 4: 1, 5: 0}
                for kc in range(KC):
                    T.dma("pool", "wmi", wmi[:, kc, :], d_wmi[li, :, kc, :], writes=["wmi"])
                T.dma("pool", "wmo", wmo[:], d_wmo[li], writes=["wmo"])
                TR([(ps[0][:, hh * 128:(hh + 1) * 128], Tt[:, hh, 0:128]) for hh in range(4)], C("ident"), ["state", "cst"], [pk(0)])
                CP(B().uw[:, :, 0:128], ps[0][:, :].rearrange("p (h d) -> p h d", h=4), [pk(0)], ["uw"])
                CP(B().uw[:, :, 128:256], Tt[:, :, 128:256], ["state"], ["uw"])
                T.dma("sp", "x_in", x_in[0].ap(), B().uw[:, 0:2, :].rearrange("p h w -> p (h w)"), reads=["uw"], writes=["d:x_in0"])
                T.dma("sp", "x_in", x_in[1].ap(), B().uw[:, 2:4, :].rearrange("p h w -> p (h w)"), reads=["uw"], writes=["d:x_in1"])
                allgather(0)
                allgather(1)
                allgather(2)
                T.op("dve", lambda h: h.memset(Ssm[:], 0.0), ["Ssm"], ["Ssm"])
                T.op("dve", lambda h: h.memset(Tt[:, :, 128:256], 0.0), ["state"], ["state"])
                T.op("dve", lambda h: h.memset(kvh[:], 0.0), (), ["kvh"])
                for r in range(NCORES):
                    T.dma("sp", "xl", B().uw[:, 0:2, :].rearrange("p h w -> p (h w)"), g8v[0][:, r, :], reads=["d:x_g80"], writes=["uw"])
                    T.dma("sp", "xl", B().uw[:, 2:4, :].rearrange("p h w -> p (h w)"), g8v[1][:, r, :], reads=["d:x_g81"], writes=["uw"])
                    T.dma("sp", "xl2", zs[:, 0:256], g8v[2][:, r, 0:256], reads=["d:x_g82"], writes=["zs"])
                    if r < NCORES - 1:
                        STT(kvh[:].rearrange("p a b -> p (a b)"), zs[:, 0:256], C("ohp", 128, r, r + 1), kvh[:].rearrange("p a b -> p (a b)"),
                            ALU.mult, ALU.add, ["zs", "cst", "kvh"], ["kvh"])
                    if r < NCORES - 1:
                        PE([(ps[1][:, hh * 128:(hh + 1) * 128], B().uw[:, hh, 0:128], Ssm[:, hh, :], True, True) for hh in range(4)],
                           ["uw", "Ssm"], [pk(1)])
                        TT(B().AT[:], ps[1][:, :].rearrange("p (h d) -> p h d", h=4), B().uw[:, :, 128:256], ALU.add, [pk(1), "uw"], ["AT"])
                        TT(B().AT[:], B().AT[:], Ssm[:], ALU.subtract, ["AT", "Ssm"], ["AT"])
                        STT(Ssm[:], B().AT[:], C("mlt", 128, r, r + 1), Ssm[:], ALU.mult, ALU.add, ["AT", "cst", "Ssm"], ["Ssm"])
                    PE([(ps[2][:, hh * 128:(hh + 1) * 128], B().uw[:, hh, 0:128], Tt[:, hh, 128:256], True, True) for hh in range(4)],
                       ["uw", "state"], [pk(2)])
                    TT(Tt[:, :, 128:256], ps[2][:, :].rearrange("p (h d) -> p h d", h=4), B().uw[:, :, 128:256], ALU.add, [pk(2), "uw"], ["state"])
                T.dma("sp", "outd", o_deltap[:, li], Tt[:, :, 128:256], reads=["state"])
                CP(Tt[:, :, 128:256], Ssm[:], ["Ssm", "state"], ["state"])
                halo_kv_to_window = True
                if halo_kv_to_window:
                    for hc in range(2):
                        slot = 1 + hc
                        rows = slice(hc * 64, hc * 64 + 64)
                        def fn(h, rows=rows):
                            ins = None
                            for g in range(2):
                                ins = h.matmul(ps[7][0:64, 256 + g * 64:256 + g * 64 + 64], lhsT=kvh[rows, 0, g * 64:(g + 1) * 64],
                                               rhs=C("ident")[rows, rows], start=True, stop=True)
                            return ins
                        T.op("pe", fn, ["kvh", "cst"], [pk(7)])
                        CP(kTw[:, :, slot, :], ps[7][0:64, 256:384].rearrange("p (g t) -> p g t", g=2), [pk(7)], ["kTw"])
                        def fn2(h, rows=rows):
                            return h.matmul(ps[7][0:64, 384:512], lhsT=C("ident")[rows, rows], rhs=kvh[rows, 1, :], start=True, stop=True)
                        T.op("pe", fn2, ["kvh", "cst"], [pk(7)])
                        for g in range(2):
                            CP(Vw[:, slot, g, 0, 0:64], ps[7][0:64, 384 + g * 64:384 + g * 64 + 64], [pk(7)], ["Vw"])
                            CP(Vw[:, slot, g, 1, 64:128], ps[7][0:64, 384 + g * 64:384 + g * 64 + 64], [pk(7)], ["Vw"])

                chk(7)
                mixP = SETS[0].ctmp_full[:].bitcast(BF16).rearrange("p (q k t) -> p q k t", q=2, k=8)
                hbs = [S0.hb, hb1, hbA, hbB]
                prevB = None
                for b0 in range(0, NB, 2):
                    pp = (b0 // 2) % 2
                    mvp = mixP[:, pp]
                    mk = ("mixP", pp)

                    def genA(b, q, mvp=mvp, mk=mk):
                        make_h(b * 64, 64, b % 4)
                        yield
                        yield from gdn_corr(64, b, b % 4, mvp[:, :, (b % 2) * 64:(b % 2 + 1) * 64], mk)

                    def genB2(b0=b0, mvp=mvp, mk=mk):
                        for b in (b0, b0 + 1):
                            yield from swa_proj(64, b, b % 3, b % 4)
                            yield from swa_attn(64, [64, 64, 64], "mask0" if b == 0 else ("mask1" if b == 1 else None), b % 4,
                                                mvp[:, :, (b % 2) * 64:(b % 2 + 1) * 64], mk)
                        yield from out_proj(128, b0 * 64, 0, mvp, mk)
                    run((genA(b0, 0), 0), (genA(b0 + 1, 1), 1), prevB)
                    prevB = genB2()
                run(prevB)
                hbs = [S0.hb, hb1]
                chk(12)
                T.barrier()
                T.dma("sp", "ldc", halo[:], d_cc[:, li], writes=["halo"])
                T.dma("sp", "ldc", Tt[:, :, 128:256], d_st[:, li], writes=["state"])
                T.dma("sp", "ldc", ckb[:, 0, :], d_ck[:, li], writes=["ckb"])
                T.dma("sp", "ldc", ckb[:, 1, :], d_cv[:, li], writes=["ckb"])
                make_h(TP, 32, 0)
                run(gdn_pre(32, "32", 0))
                T.dma("sp", "outc", o_convs[:, li], halo[:], reads=["halo"])
                run(gdn_out(32, Tt, 128, TP, 0))
                run(chain_step(32, Tt, 128, 128))
                T.dma("sp", "outd", o_deltas[:, li], Tt[:, :, 128:256], reads=["state"])
                for hc in range(2):
                    rows = slice(hc * 64, hc * 64 + 64)
                    def fn(h, rows=rows):
                        ins = None
                        for g in range(2):
                            ins = h.matmul(ps[7][0:64, 256 + g * 64:256 + g * 64 + 64], lhsT=ckb[rows, 0, g * 64:(g + 1) * 64],
                                           rhs=C("ident")[rows, rows], start=True, stop=True)
                        return ins
                    T.op("pe", fn, ["ckb", "cst", "kTw"], [pk(7)])
                    CP(kTw[:, :, hc, :], ps[7][0:64, 256:384].rearrange("p (g t) -> p g t", g=2), [pk(7)], ["kTw"])
                    def fn2(h, rows=rows):
                        return h.matmul(ps[7][0:64, 384:512], lhsT=C("ident")[rows, rows], rhs=ckb[rows, 1, :], start=True, stop=True)
                    T.op("pe", fn2, ["ckb", "cst"], [pk(7)])
                    for g in range(2):
                        CP(Vw[:, hc, g, 0, 0:64], ps[7][0:64, 384 + g * 64:384 + g * 64 + 64], [pk(7)], ["Vw"])
                        CP(Vw[:, hc, g, 1, 64:128], ps[7][0:64, 384 + g * 64:384 + g * 64 + 64], [pk(7)], ["Vw"])
                run(swa_proj(32, 32, 2, 0, kout=o_ks[96:128, li], vout=o_vs[96:128, li]))
                T.dma("sp", "outk", o_ks[0:96, li], ckb[32:128, 0, :], reads=["ckb"])
                T.dma("sp", "outk", o_vs[0:96, li], ckb[32:128, 1, :], reads=["ckb"])
                run(swa_attn(32, [64, 64, 32], None, 0))
                run(out_proj(32, TP, 0))
                T.barrier()
                p2.close()
                T.kmap = None

        psb16 = [p[:].bitcast(BF16) for p in ps]

        with ExitStack() as st0:
            ztile = sb("ztile", (128, 512), F32, st0)
            T.op("dve", lambda h: h.memset(ztile[:], 0.0), (), ["ztile"])
            for k in range(3):
                T.dma("sp", "x_in", x_in[k].ap(), ztile[:], reads=["ztile"], writes=[f"d:x_in{k}"])
            T.barrier()
        _stage = _KSTAGE
        for li in range(nlayers):
            if _stage in ("all", "ffn"):
                ffn(li * 2, li * 3)
            if _stage in ("all", "mix") and not _STOPPED[0]:
                try:
                    mixer(li)
                except _Stop:
                    T.barrier()
            if _stage in ("all",) and not _STOPPED[0]:
                ffn(li * 2 + 1, li * 3 + 2)

        with ExitStack() as st:
          if not _STOPPED[0]:
              sq = sb("sqF", (128, KC, FT), BF16, st)
              rstd = sb("rstdF", (128, FT), F32, st)
              yb = sb("yb", (128, KC, FT), F32, st)
              T.barrier()
              for t in range(NFT):
                  cs = slice(t * FT, (t + 1) * FT)
                  ACT(sq[:], xT[:, :, cs], AF.Square, ["xT"], ["sqF"])
                  PE([(ps[7][:, 0:FT], onesb[:], sq[:, kc, :], kc == 0, kc == KC - 1) for kc in range(KC)], ["sqF", "onesb"], [pk(7)])
                  RSQRT(rstd[:], ps[7][:, 0:FT], 1.0 / DM, 0, [pk(7), "epst"], ["rstdF"])
                  for kc in range(KC):
                      STT(yb[:, kc, :], xT[:, kc, cs], nrm[:, L * 3, kc:kc + 1], rstd[:], ALU.mult, ALU.mult,
                          ["xT", "nrm", "rstdF"], ["yb"])
                  T.dma("sp", "outy", o_y[:, :, cs], yb[:], reads=["yb"])
              T.barrier()
        with nc.Block() as block:
            T.emit(block)
    return nc


_NC_CACHE = {}


def _layout_inputs(inp):
    f = np.float32
    g = lambda k: np.asarray(inp[k], dtype=f)
    shared = {}
    wfi = np.stack([g("ff1_w_in"), g("ff2_w_in")], 1).reshape(L * 2, KC, 128, 2 * DFF).transpose(0, 2, 1, 3)
    shared["wfi"] = np.ascontiguousarray(wfi if _KSTAGE != "mix" else wfi[:2])
    wfo = np.stack([g("ff1_w_out"), g("ff2_w_out")], 1).reshape(L * 2, HC, 128, DM).transpose(0, 2, 1, 3)
    shared["wfo"] = np.ascontiguousarray(wfo if _KSTAGE != "mix" else wfo[:2])
    shared["wmi"] = np.ascontiguousarray(g("w_mix_in").reshape(L, KC, 128, DIN).transpose(0, 2, 1, 3))
    shared["wmo"] = np.ascontiguousarray(g("w_mix_out").reshape(L, KC, 128, DM).transpose(0, 2, 1, 3))
    nrm = np.zeros((L * 3 + 1, DM), f)
    for l in range(L):
        nrm[l * 3 + 0] = g("norm_ff1")[l]
        nrm[l * 3 + 1] = g("norm_mix")[l]
        nrm[l * 3 + 2] = g("norm_ff2")[l]
    nrm[L * 3] = g("norm_final")
    shared["nrm"] = np.ascontiguousarray(nrm.reshape(L * 3 + 1, KC, 128).transpose(2, 0, 1))
    shared["cw"] = np.ascontiguousarray(g("conv_w").reshape(L, 4, 12, 128).transpose(3, 0, 2, 1))
    row = np.concatenate([g("a_log"), g("dt_bias"), g("sinks"), g("gnorm_w")], axis=1)
    shared["rowc"] = np.ascontiguousarray(np.broadcast_to(row[None], (128, L, 144)))
    maps = []
    xp = g("x_prompt")[0]
    xs = g("x_sample")
    for c in range(NCORES):
        m = dict(shared)
        xc = np.concatenate([xp[c * TP:(c + 1) * TP], xs[c]], 0)
        m["xT"] = np.ascontiguousarray(xc.T.reshape(KC, 128, NT).transpose(1, 0, 2))
        m["cconv"] = np.ascontiguousarray(g("cache_conv")[:, c].reshape(L, 3, 12, 128).transpose(3, 0, 2, 1))
        m["state"] = np.ascontiguousarray(g("state_delta")[:, c].transpose(2, 0, 1, 3))
        m["ck"] = np.ascontiguousarray(g("cache_k")[:, c].reshape(L, 128, 128).transpose(1, 0, 2))
        m["cv"] = np.ascontiguousarray(g("cache_v")[:, c].reshape(L, 128, 128).transpose(1, 0, 2))
        m["cst"] = _make_consts(c)
        maps.append(m)
    return maps


def kernel(**inputs):
    if "nc" not in _NC_CACHE:
        _NC_CACHE["nc"] = build_program()
    nc = _NC_CACHE["nc"]
    maps = _layout_inputs(inputs)
    res = run_bass_kernel_spmd(nc, maps, core_ids=list(range(NCORES)))
    R = res.results
    f = np.float32
    yp = np.zeros((1, NCORES * TP, DM), f)
    ys = np.zeros((NCORES, TS, DM), f)
    for c in range(NCORES):
        y = np.asarray(R[c]["yT"]).transpose(1, 0, 2).reshape(DM, NT).T
        yp[0, c * TP:(c + 1) * TP] = y[:TP]
        ys[c] = y[TP:]

    def conv_out(a):
        return np.ascontiguousarray(np.asarray(a).transpose(1, 3, 2, 0).reshape(L, 3, 1536))

    def delta_out(a):
        return np.ascontiguousarray(np.asarray(a).transpose(1, 2, 0, 3))

    def kv_out(a):
        return np.ascontiguousarray(np.asarray(a).transpose(1, 0, 2).reshape(L, 128, 2, 64))
    last = R[NCORES - 1]
    conv_p = conv_out(last["convp"])[:, None]
    delta_p = delta_out(last["deltap"])[:, None]
    k_p = kv_out(last["kp"])[:, None]
    v_p = kv_out(last["vp"])[:, None]
    conv_s = np.stack([conv_out(R[c]["convs"]) for c in range(NCORES)], 1)
    delta_s = np.stack([delta_out(R[c]["deltas"]) for c in range(NCORES)], 1)
    k_s = np.stack([kv_out(R[c]["ks"]) for c in range(NCORES)], 1)
    v_s = np.stack([kv_out(R[c]["vs"]) for c in range(NCORES)], 1)
    return (yp, ys, conv_p.astype(f), delta_p.astype(f), k_p.astype(f), v_p.astype(f),
            conv_s.astype(f), delta_s.astype(f), k_s.astype(f), v_s.astype(f))
```

```python
import numpy as np
from contextlib import ExitStack
import concourse.bass as bass
import concourse.mybir as mybir
from concourse.bass_utils import run_bass_kernel_spmd

F32 = mybir.dt.float32
BF16 = mybir.dt.bfloat16
AF = mybir.ActivationFunctionType
ALU = mybir.AluOpType
AX = mybir.AxisListType

NCORES = 8
L = 4
DM = 1024
KC = 8
TP = 2048
TS = 32
NT = TP + TS
DFF = 2816
HC = 22
DIN = 2824
EPS = 1e-6
NB = TP // 64
FT = 416
NFT = NT // FT
QS = 128.0 ** -0.5
NEG = -30000.0


_STOPPED = [False]


class _Eng:
    def __init__(self, name, sem):
        self.name = name
        self.sem = sem
        self.count = 0
        self.waited = {}
        self.prog = []


class _Slot:
    def __init__(self, name, sem):
        self.name = name
        self.sem = sem
        self.count = 0


class Tracker:
    def __init__(self, sems):
        self._sems = list(sems)
        self.eng = {n: _Eng(n, self._sems.pop()) for n in ("pe", "act", "dve", "pool", "sp")}
        self.last_write = {}
        self.readers = {}
        self.slots = {}

    def slot(self, name):
        if name not in self.slots:
            self.slots[name] = _Slot(name, self._sems.pop())
        return self.slots[name]

    def _wait(self, e, tok, same_ok):
        if tok is None:
            return
        if tok[0] == "e":
            _, en, seq = tok
            if en == e.name and same_ok:
                return
            if e.waited.get(en, 0) >= seq:
                return
            e.waited[en] = seq
            e.prog.append(("w", self.eng[en].sem, seq))
        else:
            sl = tok[1]
            key = "slot:" + sl.name
            if e.waited.get(key, 0) >= sl.count:
                return
            e.waited[key] = sl.count
            e.prog.append(("w", sl.sem, sl.count))

    def _deps(self, e, reads, writes):
        for k in reads:
            self._wait(e, self.last_write.get(k), False)
        for k in writes:
            self._wait(e, self.last_write.get(k), True)
            for t in self.readers.get(k, ()):
                self._wait(e, t, True)

    def _record(self, tok, reads, writes):
        for k in writes:
            self.last_write[k] = tok
            self.readers[k] = []
        for k in reads:
            lst = self.readers.setdefault(k, [])
            if tok[0] == "e":
                lst[:] = [t for t in lst if not (t[0] == "e" and t[1] == tok[1])]
            elif tok in lst:
                continue
            lst.append(tok)

    kmap = None

    def _km(self, keys):
        if self.kmap is None:
            return list(keys)
        return [self.kmap(k) for k in keys]

    def op(self, engname, fn, reads=(), writes=()):
        if _STOPPED[0]:
            return
        reads, writes = self._km(reads), self._km(writes)
        e = self.eng[engname]
        if engname in ("act", "dve"):
            writes = list(writes) + [("psl", k[1]) for k in list(reads) + list(writes)
                                     if isinstance(k, tuple) and k[0] == "ps"]
        self._deps(e, reads, writes)
        e.count += 1
        e.prog.append(("o", fn, e.sem, 1))
        self._record(("e", engname, e.count), reads, writes)

    def dma(self, qname, slotname, out, in_, reads=(), writes=(), **kw):
        if _STOPPED[0]:
            return
        reads, writes = self._km(reads), self._km(writes)
        e = self.eng[qname]
        sl = self.slot(slotname)
        self._deps(e, reads, writes)
        e.prog.append(("o", (lambda h, out=out, in_=in_, kw=kw: h.dma_start(out=out, in_=in_, **kw)), sl.sem, 16))
        sl.count += 16
        self._record(("d", sl), reads, writes)

    def custom(self, qname, slotname, fn, inc, reads=(), writes=()):
        if _STOPPED[0]:
            return
        e = self.eng[qname]
        sl = self.slot(slotname)
        self._deps(e, reads, writes)
        e.prog.append(("o", fn, sl.sem, inc))
        sl.count += inc
        self._record(("d", sl), reads, writes)

    def barrier(self):
        for e in self.eng.values():
            for en, o in self.eng.items():
                if o.count > 0 and en != e.name:
                    self._wait(e, ("e", en, o.count), False)
            for sl in self.slots.values():
                if sl.count > 0:
                    self._wait(e, ("d", sl), False)

    def emit(self, block):
        def run(e):
            def body(h):
                for it in e.prog:
                    if it[0] == "w":
                        h.wait_ge(it[1], it[2])
                    else:
                        it[1](h).then_inc(it[2], it[3])
            return body
        block.tensor(run(self.eng["pe"]))
        block.scalar(run(self.eng["act"]))
        block.vector(run(self.eng["dve"]))
        block.gpsimd(run(self.eng["pool"]))
        block.sync(run(self.eng["sp"]))


def _const_layout():
    off = {}
    n = 0
    for name, w in (("ident", 128), ("ones", 128),
                    ("triU64", 64), ("mL64", 64), ("mU64", 64), ("mI64", 64),
                    ("triU32", 64), ("mL32", 64), ("mU32", 64), ("mI32", 64),
                    ("sel64", 128), ("sel32", 128), ("mask0", 192), ("mask1", 192),
                    ("cos", 33 * 8), ("sin", 33 * 8), ("ohp", 8), ("mlt", 8)):
        off[name] = (n, w)
        n += w
    return off, n


COFF, CN = _const_layout()


def _make_consts(core):
    c = np.zeros((128, CN), np.float32)

    def put(name, arr):
        o, w = COFF[name]
        c[:arr.shape[0], o:o + arr.shape[1]] = arr
    put("ident", np.eye(128, dtype=np.float32))
    put("ones", np.ones((128, 128), np.float32))
    for sz, sfx in ((64, "64"), (32, "32")):
        j = np.arange(sz)[:, None]
        i = np.arange(sz)[None, :]
        put("triU" + sfx, (j <= i).astype(np.float32))
        put("mL" + sfx, -(j > i).astype(np.float32))
        put("mU" + sfx, -(j < i).astype(np.float32))
        put("mI" + sfx, (j <= i).astype(np.float32) * QS)
    s64 = np.zeros((64, 128), np.float32); s64[63, :] = 1.0
    s32 = np.zeros((32, 128), np.float32); s32[31, :] = 1.0
    put("sel64", s64)
    put("sel32", s32)
    hv = NEG if core == 0 else 0.0
    m0 = np.zeros((64, 192), np.float32); m0[:, 64:192] = hv
    m1 = np.zeros((64, 192), np.float32); m1[:, 128:192] = hv
    put("mask0", m0)
    put("mask1", m1)
    inv = np.power(np.float32(500000.0), -np.arange(0, 16, 2, dtype=np.float32) / np.float32(16)).astype(np.float32)
    cos = np.zeros((64, 33, 8), np.float32)
    sin = np.zeros((64, 33, 8), np.float32)
    for b in range(33):
        if b < 32:
            pos = (core * TP + b * 64 + np.arange(64)).astype(np.float32)
        else:
            pos = (4096 + np.arange(32)).astype(np.float32)
        ang = pos[:, None] * inv[None, :]
        cos[:len(pos), b] = np.cos(ang)
        sin[:len(pos), b] = np.sin(ang)
    put("cos", cos.reshape(64, -1))
    put("sin", sin.reshape(64, -1))
    ohp = np.zeros((128, 8), np.float32)
    if core > 0:
        ohp[:, core - 1] = 1.0
    mlt = np.zeros((128, 8), np.float32)
    mlt[:, :core] = 1.0
    put("ohp", ohp)
    put("mlt", mlt)
    return c


class _Stop(Exception):
    pass


import os as _os
_KSTOP = int(_os.environ.get("KSTOP", "0"))
_KSTAGE = _os.environ.get("KSTAGE", "all")


def chk(n):
    if _KSTOP == n:
        _STOPPED[0] = True


def build_program(nlayers=L):
    nc = bass.Bass("TRN2", target_bir_lowering=False, dynamic_dma_scratch_size=8192)

    def din(name, shape):
        return nc.dram_tensor(name, list(shape), F32, kind="ExternalInput").ap()

    def dout(name, shape):
        return nc.dram_tensor(name, list(shape), F32, kind="ExternalOutput").ap()

    d_x = din("xT", (128, KC, NT))
    LF = 1 if _KSTAGE == "mix" else L
    d_wfi = din("wfi", (LF * 2, 128, KC, 2 * DFF))
    d_wfo = din("wfo", (LF * 2, 128, HC, DM))
    d_wmi = din("wmi", (L, 128, KC, DIN))
    d_wmo = din("wmo", (L, 128, KC, DM))
    d_nrm = din("nrm", (128, L * 3 + 1, KC))
    d_cw = din("cw", (128, L, 12, 4))
    d_row = din("rowc", (128, L, 144))
    d_cc = din("cconv", (128, L, 12, 3))
    d_st = din("state", (128, L, 4, 128))
    d_ck = din("ck", (128, L, 128))
    d_cv = din("cv", (128, L, 128))
    d_cst = din("cst", (128, CN))
    o_y = dout("yT", (128, KC, NT))
    o_convp = dout("convp", (128, L, 12, 3))
    o_deltap = dout("deltap", (128, L, 4, 128))
    o_kp = dout("kp", (128, L, 128))
    o_vp = dout("vp", (128, L, 128))
    o_convs = dout("convs", (128, L, 12, 3))
    o_deltas = dout("deltas", (128, L, 4, 128))
    o_ks = dout("ks", (128, L, 128))
    o_vs = dout("vs", (128, L, 128))
    XW = 512
    x_in = [nc.dram_tensor(f"x_in{k}", [128, XW], F32) for k in range(3)]
    x_g4 = [nc.dram_tensor(f"x_g4{k}", [4 * 128, XW], F32) for k in range(3)]
    x_g8 = [nc.dram_tensor(f"x_g8{k}", [8 * 128, XW], F32) for k in range(3)]
    sp_o = nc.dram_tensor("sp_o", [NB, 64, 512], F32)
    sp_z = nc.dram_tensor("sp_z", [NB, 128, 256], F32)
    G4 = [[0, 1, 2, 3], [4, 5, 6, 7]]
    G2 = [[0, 4], [1, 5], [2, 6], [3, 7]]

    es = ExitStack()
    with es:
        _uid = [0]

        def sb(name, shape, dt=F32, stack=None):
            _uid[0] += 1
            return (stack or es).enter_context(nc.sbuf_tensor(f"s{_uid[0]}_{name}", list(shape), dt))

        sems = [es.enter_context(nc.semaphore(f"s{i}")) for i in range(60)]
        T = Tracker(sems)
        ps = [es.enter_context(nc.psum_tensor(f"ps{i}", [128, 512], F32)) for i in range(8)]

        def pk(i):
            return ("ps", i)

        xT = sb("xT", (128, KC, NT))
        cst = sb("cst", (128, CN))
        nrm = sb("nrm", (128, L * 3 + 1, KC))
        cwt = sb("cwt", (128, L, 12, 4))
        rowc = sb("rowc", (128, L, 144))
        identb = sb("identb", (128, 128), BF16)
        onesb = sb("onesb", (128, 128), BF16)
        ea = sb("ea", (128, L, 4))

        def C(name, rows=128, c0=0, c1=None):
            o, w = COFF[name]
            c1 = w if c1 is None else c1
            return cst[0:rows, o + c0:o + c1]

        T.dma("sp", "ld0", xT[:], d_x, writes=["xT"])
        T.dma("sp", "ld1", cst[:], d_cst, writes=["cst"])
        T.dma("sp", "ld1", nrm[:], d_nrm, writes=["nrm"])
        T.dma("sp", "ld1", cwt[:], d_cw, writes=["cwt"])
        T.dma("sp", "ld1", rowc[:], d_row, writes=["rowc"])
        T.op("dve", lambda h: h.tensor_copy(out=identb[:], in_=C("ident")), reads=["cst"], writes=["identb"])
        T.op("dve", lambda h: h.tensor_copy(out=onesb[:], in_=C("ones")), reads=["cst"], writes=["onesb"])
        T.op("act", lambda h: h.activation(out=ea[:], in_=rowc[:, :, 0:4], func=AF.Exp), reads=["rowc"], writes=["ea"])

        def PE(mms, reads, writes):
            def fn(h, mms=mms):
                ins = None
                for (o, l, r, st, sp_) in mms:
                    ins = h.matmul(o, lhsT=l, rhs=r, start=st, stop=sp_)
                return ins
            T.op("pe", fn, reads, writes)

        def TR(outs_ins, ident, reads, writes):
            def fn(h, oi=outs_ins, ident=ident):
                ins = None
                for (o, i) in oi:
                    ins = h.transpose(out=o, in_=i, identity=ident)
                return ins
            T.op("pe", fn, reads, writes)

        def ACT(out, in_, func, reads, writes, bias=None, scale=1.0, accum=None):
            def fn(h):
                kw = {}
                if bias is not None:
                    kw["bias"] = bias
                if accum is not None:
                    kw["accum_out"] = accum
                return h.activation(out=out, in_=in_, func=func, scale=scale, **kw)
            T.op("act", fn, reads, writes)

        def TT(out, in0, in1, op, reads, writes, eng="dve"):
            T.op(eng, lambda h: h.tensor_tensor(out=out, in0=in0, in1=in1, op=op), reads, writes)

        def TS_(out, in0, s1, s2, op0, op1, reads, writes):
            if s2 is None:
                T.op("dve", lambda h: h.tensor_scalar(out=out, in0=in0, scalar1=s1, scalar2=None, op0=op0), reads, writes)
            else:
                T.op("dve", lambda h: h.tensor_scalar(out=out, in0=in0, scalar1=s1, scalar2=s2, op0=op0, op1=op1), reads, writes)

        def STT(out, in0, scalar, in1, op0, op1, reads, writes):
            T.op("dve", lambda h: h.scalar_tensor_tensor(out=out, in0=in0, scalar=scalar, in1=in1, op0=op0, op1=op1), reads, writes)

        def CP(out, in_, reads, writes, eng="dve"):
            if eng == "act":
                ACT(out, in_, AF.Copy, reads, writes)
            else:
                T.op("dve", lambda h: h.tensor_scalar(out=out, in0=in_, scalar1=1.0, scalar2=None, op0=ALU.mult), reads, writes)

        def RSQRT(out, in_, scale, eps, reads, writes):
            ACT(out, in_, AF.Ln, reads, writes, bias=eps_ap(eps, out.shape[0]), scale=scale)
            ACT(out, out, AF.Exp, writes, writes, scale=-0.5)

        epst = sb("epst", (128, 3))
        T.op("dve", lambda h: h.memset(epst[:, 0:1], EPS), (), ["epst"])
        T.op("dve", lambda h: h.memset(epst[:, 1:2], 128.0 * EPS), ["epst"], ["epst"])
        T.op("dve", lambda h: h.memset(epst[:, 2:3], 1.0), ["epst"], ["epst"])

        def eps_ap(which, rows=128):
            return epst[0:rows, which:which + 1]

        def ffn(li, widx):
            with ExitStack() as st:
                hT = sb("hT", (128, KC, NT), BF16, st)
                actT = sb("actT", (128, 11, NT), BF16, st)
                sq = sb("sqf", (128, KC, FT), BF16, st)
                rstd = sb("rstdf", (128, FT), F32, st)
                sg = sb("sg", (128, 2, FT), F32, st)
                wgu = [sb(f"wgu{i}", (128, 2, KC, 128), BF16, st) for i in range(4)]
                wo = [sb(f"wo{i}", (128, 11, 128), BF16, st) for i in range(3)]
                T.barrier()
                for t in range(NFT):
                    cs = slice(t * FT, (t + 1) * FT)
                    ACT(sq[:], xT[:, :, cs], AF.Square, ["xT"], ["sqf"])
                    PE([(ps[7][:, 0:FT], onesb[:], sq[:, kc, :], kc == 0, kc == KC - 1) for kc in range(KC)],
                       ["sqf", "onesb"], [pk(7)])
                    RSQRT(rstd[:], ps[7][:, 0:FT], 1.0 / DM, 0, [pk(7), "epst"], ["rstdf"])
                    for kc in range(KC):
                        STT(hT[:, kc, cs], xT[:, kc, cs], nrm[:, widx, kc:kc + 1], rstd[:],
                            ALU.mult, ALU.mult, ["xT", "nrm", "rstdf"], [("hT", t)])
                cnt = {"a": 0, "b": 0, "ga": 0, "gb": 0}

                def loadA(j):
                    s = cnt["a"] % 4
                    cnt["a"] += 1
                    T.dma("pool", f"wgu{s}", wgu[s][:, 0], d_wfi[li, :, :, j * 128:(j + 1) * 128], writes=[("wgu", s)])
                    T.dma("pool", f"wgu{s}", wgu[s][:, 1], d_wfi[li, :, :, DFF + j * 128:DFF + (j + 1) * 128], writes=[("wgu", s)])
                    return s

                def loadB(j0, m):
                    s = cnt["b"] % 3
                    cnt["b"] += 1
                    T.dma("pool", f"wo{s}", wo[s][:], d_wfo[li, :, j0:j0 + 11, m * 128:(m + 1) * 128], writes=[("wo", s)])
                    return s

                for piece in range(2):
                    j0 = piece * 11
                    pend = [loadA(j0 + q) for q in range(3)]
                    for jj in range(11):
                        if jj + 3 < 11:
                            pend.append(loadA(j0 + jj + 3))
                        s = pend.pop(0)
                        for t in range(NFT):
                            cs = slice(t * FT, (t + 1) * FT)
                            n = cnt["ga"] % 2
                            cnt["ga"] += 1
                            bg, bu = 2 * n, 2 * n + 1
                            PE([(ps[bg][:, 0:FT], wgu[s][:, 0, kc, :], hT[:, kc, cs], kc == 0, kc == KC - 1) for kc in range(KC)] +
                               [(ps[bu][:, 0:FT], wgu[s][:, 1, kc, :], hT[:, kc, cs], kc == 0, kc == KC - 1) for kc in range(KC)],
                               [("wgu", s), ("hT", t)], [pk(bg), pk(bu)])
                            ACT(sg[:, n, :], ps[bg][:, 0:FT], AF.Silu, [pk(bg)], [("sg", n)])
                            TT(actT[:, jj, cs], sg[:, n, :], ps[bu][:, 0:FT], ALU.mult, [("sg", n), pk(bu)], [("actT", t)])
                    pendb = [loadB(j0, m) for m in range(2)]
                    for m in range(KC):
                        if m + 2 < KC:
                            pendb.append(loadB(j0, m + 2))
                        s = pendb.pop(0)
                        for t in range(NFT):
                            cs = slice(t * FT, (t + 1) * FT)
                            n = 4 + cnt["gb"] % 3
                            cnt["gb"] += 1
                            PE([(ps[n][:, 0:FT], wo[s][:, q, :], actT[:, q, cs], q == 0, q == 10) for q in range(11)],
                               [("wo", s), ("actT", t)], [pk(n)])
                            STT(xT[:, m, cs], ps[n][:, 0:FT], 0.5, xT[:, m, cs], ALU.mult, ALU.add, [pk(n), "xT"], ["xT"])
                T.barrier()

        def allgather(k):
            T.custom("pool", "cc", lambda h, k=k: h.collective_compute(
                "AllGather", ALU.bypass, replica_groups=G4, ins=[x_in[k].ap().opt()], outs=[x_g4[k].ap().opt()]), 1,
                reads=[f"d:x_in{k}"], writes=[f"d:x_g4{k}"])
            T.custom("pool", "cc", lambda h, k=k: h.collective_compute(
                "AllGather", ALU.bypass, replica_groups=G2, ins=[x_g4[k].ap().opt()], outs=[x_g8[k].ap().opt()]), 1,
                reads=[f"d:x_g4{k}"], writes=[f"d:x_g8{k}"])

        g8v = [x_g8[k].ap().rearrange("(r p) f -> p r f", p=128) for k in range(3)]

        def mixer(li):
            with ExitStack() as st:
                CURQ = [0]
                LOCALKEYS = {"sqb", "rsb", "pre", "cvb", "ctmp", "sq2", "rs2", "qtok", "small", "dGB", "DU", "DL", "t1", "t2",
                             "Pm0", "Pm1", "Qm0", "Qm1", "Rm0", "Rm1", "Mqk", "nMqk", "uw", "kdec", "dE", "AT", "gta", "QpT",
                             "osb", "zs", "oa", "ssmA", "zst"}
                T.kmap = lambda k: (k, CURQ[0]) if (isinstance(k, str) and k in LOCALKEYS) else k
                BKS = [{0: 0, 1: 1, 2: 2, 3: 3, 4: 4, 5: 5}, {0: 0, 1: 1, 2: 2, 3: 3, 4: 4, 5: 5}]

                def PSB(i):
                    return ps[BKS[CURQ[0]][i]]

                def pkA(i):
                    return ("ps", BKS[CURQ[0]][i])

                class _NS:
                    pass

                def mkset(q, stk):
                    S = _NS()
                    S.hb = sb(f"hb{q}", (128, KC, 64), BF16, stk)
                    S.sqb = sb(f"sqb{q}", (128, KC, 64), BF16, stk)
                    S.rsb = sb(f"rsb{q}", (128, 64), F32, stk)
                    S.pre = sb(f"pre{q}", (128, 12, 67), F32, stk)
                    S.cvb = sb(f"cvb{q}", (128, 12, 64), F32, stk)
                    S.ctmp_full = sb(f"ctmp{q}", (128, 1024), F32, stk)
                    S.ctmp = S.ctmp_full[:, 0:768].rearrange("p (f t) -> p f t", f=12)
                    S.rhs_t = S.ctmp_full[0:64, :].rearrange("p (h w) -> p h w", h=4)
                    S.sq2 = sb(f"sq2{q}", (128, 8, 64), BF16, stk)
                    S.rs2 = sb(f"rs2{q}", (128, 8, 64), F32, stk)
                    S.qtok = sb(f"qtok{q}", (64, 4, 128), F32, stk)
                    S.sm_ = sb(f"small{q}", (128, 96), F32, stk)
                    S.dGB = sb(f"dGB{q}", (64, 4, 2, 64), F32, stk)
                    S.DU = sb(f"DU{q}", (64, 4, 64), F32, stk)
                    S.DL = sb(f"DL{q}", (64, 4, 64), F32, stk)
                    S.t1 = sb(f"t1{q}", (64, 4, 64), F32, stk)
                    S.t2 = sb(f"t2{q}", (64, 4, 64), F32, stk)
                    S.Pm = [sb(f"Pm{i}{q}", (64, 4, 64), F32, stk) for i in range(2)]
                    S.Qm = [sb(f"Qm{i}{q}", (64, 4, 64), F32, stk) for i in range(2)]
                    S.Rm = [sb(f"Rm{i}{q}", (64, 4, 64), F32, stk) for i in range(2)]
                    S.Mqk = sb(f"Mqk{q}", (64, 4, 64), F32, stk)
                    S.nMqk = sb(f"nMqk{q}", (64, 4, 64), F32, stk)
                    S.uw = sb(f"uw{q}", (128, 4, 256), F32, stk)
                    S.kdec = sb(f"kdec{q}", (64, 4, 128), F32, stk)
                    S.dE = sb(f"dE{q}", (64, 4, 64), F32, stk)
                    S.AT = sb(f"AT{q}", (128, 4, 128), F32, stk)
                    S.QpT = sb(f"QpT{q}", (128, 4, 64), F32, stk)
                    S.gta = sb(f"gta{q}", (128, 4), F32, stk)
                    return S

                SETS = [mkset(0, st), None]

                def B():
                    return SETS[CURQ[0]]

                hbs = [SETS[0].hb, None]
                Tt = sb("Tt", (128, 4, 256), F32, st)
                Ssm = sb("Ssm", (128, 4, 128), F32, st)
                halo = sb("halo", (128, 12, 3), F32, st)
                osb = sb("osb", (64, 512), F32, st)
                zst = sb("zst", (128, 256), F32, st)
                qb = sb("qb", (64, 10, 64), F32, st)
                rt = sb("rt", (64, 4, 10, 8), F32, st)
                vtok = sb("vtok", (64, 128), F32, st)
                qr = sb("qr", (64, 10, 64), BF16, st)
                Vw = sb("Vw", (64, 3, 2, 2, 128), BF16, st)
                kTw = sb("kTw", (64, 2, 3, 64), BF16, st)
                qT = sb("qT", (64, 8, 64), BF16, st)
                p1 = ExitStack()
                wmi1 = sb("wmi1", (128, KC, 1800), BF16, p1)
                hsel = sb("hsel", (128, 8, 36), F32, p1)
                SETS[1] = mkset(1, p1)
                hbs = [SETS[0].hb, SETS[1].hb]
                BKS[0] = {0: 0, 1: 1, 2: 2, 3: 0, 4: 0, 5: 3}
                BKS[1] = {0: 4, 1: 5, 2: 6, 3: 4, 4: 4, 5: 7}
                WS = {"w": wmi1, "gate": 1536, "kvb": 1544, "z": None, "qb": None}

                def W():
                    return WS["w"]

                T.barrier()
                for kc in range(KC):
                    T.dma("pool", "wmi", wmi1[:, kc, 0:1536], d_wmi[li, :, kc, 0:1536], writes=["wmi"])
                    T.dma("pool", "wmi", wmi1[:, kc, 1536:1544], d_wmi[li, :, kc, 2048:2056], writes=["wmi"])
                    T.dma("pool", "wmi", wmi1[:, kc, 1544:1800], d_wmi[li, :, kc, 2568:2824], writes=["wmi"])
                T.op("dve", lambda h: h.memset(Vw[:], 0.0), (), ["Vw"])

                a_dt = rowc[:, li, 4:8]
                sinks = rowc[:, li, 8:16]
                gnw = rowc[:, li, 16:144]

                def make_h(col0, P, par=0):
                    hb = hbs[par]
                    cs = slice(col0, col0 + P)
                    ACT(B().sqb[:, :, 0:P], xT[:, :, cs], AF.Square, ["xT"], ["sqb"])
                    PE([(PSB(5)[:, 0:P], onesb[:], B().sqb[:, kc, 0:P], kc == 0, kc == KC - 1) for kc in range(KC)],
                       ["sqb", "onesb"], [pkA(5)])
                    RSQRT(B().rsb[:, 0:P], PSB(5)[:, 0:P], 1.0 / DM, 0, [pkA(5), "epst"], ["rsb"])
                    for kc in range(KC):
                        STT(hb[:, kc, 0:P], xT[:, kc, cs], nrm[:, li * 3 + 1, kc:kc + 1], B().rsb[:, 0:P],
                            ALU.mult, ALU.mult, ["xT", "nrm", "rsb"], [("hb", par)])

                def proj_fm(P, ncols_from, nchunks, banks, par=0):
                    hb = hbs[par]
                    for g0 in range(0, nchunks, 4):
                        bnk = banks[g0 // 4]
                        mms = []
                        for f in range(g0, min(g0 + 4, nchunks)):
                            c0 = ncols_from + f * 128
                            for kc in range(KC):
                                mms.append((PSB(bnk)[:, (f - g0) * 128:(f - g0) * 128 + P], W()[:, kc, c0:c0 + 128],
                                            hb[:, kc, 0:P], kc == 0, kc == KC - 1))
                        PE(mms, ["wmi", ("hb", par)], [pkA(bnk)])

                def gdn_pre(P, sfx, par=0):
                    hb = hbs[par]
                    proj_fm(P, 0, 12, [0, 1, 2], par)
                    yield
                    for g0 in range(3):
                        CP(B().pre[:, g0 * 4:(g0 + 1) * 4, 3:3 + P],
                           PSB(g0)[:, 0:512].rearrange("p (f t) -> p f t", f=4)[:, :, 0:P], [pkA(g0)], ["pre"], eng="act")
                    CP(B().pre[:, :, 0:3], halo[:], ["halo"], ["pre"])
                    for j in range(4):
                        dst = B().cvb if j == 0 else B().ctmp
                        TT(dst[:, :, 0:P], B().pre[:, :, j:j + P], cwt[:, li, :, j:j + 1].broadcast_to([128, 12, P]), ALU.mult,
                           ["pre", "cwt"], ["cvb" if j == 0 else "ctmp"])
                        if j > 0:
                            TT(B().cvb[:, :, 0:P], B().cvb[:, :, 0:P], B().ctmp[:, :, 0:P], ALU.add, ["cvb", "ctmp"], ["cvb"])
                    chk(41)
                    yield
                    CP(halo[:], B().pre[:, :, P:P + 3], ["pre"], ["halo"])
                    yield "halo"
                    ACT(B().ctmp[:, :, 0:P], B().cvb[:, :, 0:P], AF.Exp, ["cvb"], ["ctmp"], scale=-1.0)
                    ACT(B().ctmp[:, :, 0:P], B().ctmp[:, :, 0:P], AF.Ln, ["ctmp", "epst"], ["ctmp"], bias=epst[:, 2:3])
                    ACT(B().ctmp[:, :, 0:P], B().ctmp[:, :, 0:P], AF.Exp, ["ctmp"], ["ctmp"], scale=-1.0)
                    TT(B().cvb[:, :, 0:P], B().cvb[:, :, 0:P], B().ctmp[:, :, 0:P], ALU.mult, ["cvb", "ctmp"], ["cvb"])
                    ACT(B().sq2[:, :, 0:P], B().cvb[:, 0:8, 0:P], AF.Square, ["cvb"], ["sq2"])
                    if P == 64:
                        PE([(PSB(3)[:, 0:512], onesb[:], B().sq2[:].rearrange("p f t -> p (f t)"), True, True)], ["sq2", "onesb"], [pkA(3)])
                    else:
                        PE([(PSB(3)[:, f * 64:f * 64 + P], onesb[:], B().sq2[:, f, 0:P], True, True) for f in range(8)],
                           ["sq2", "onesb"], [pkA(3)])
                    RSQRT(B().rs2[:, :, 0:P], PSB(3)[:, 0:512].rearrange("p (f t) -> p f t", f=8)[:, :, 0:P], 1.0, 0,
                          [pkA(3), "epst"], ["rs2"])
                    TT(B().cvb[:, 0:8, 0:P], B().cvb[:, 0:8, 0:P], B().rs2[:, :, 0:P], ALU.mult, ["cvb", "rs2"], ["cvb"])
                    chk(42)
                    yield
                    PE([(PSB(4)[0:P, 0:8], hb[:, kc, 0:P], W()[:, kc, WS["gate"]:WS["gate"] + 8], kc == 0, kc == KC - 1) for kc in range(KC)],
                       [("hb", par), "wmi"], [pkA(4)])
                    beta = B().sm_[0:P, 0:4]
                    gg = B().sm_[0:P, 4:8]
                    Gc = B().sm_[0:P, 8:12]
                    Gl = B().sm_[0:P, 12:16]
                    ex3 = B().sm_[0:P, 16:28]
                    bG = B().sm_[0:P, 28:32]
                    ACT(beta, PSB(4)[0:P, 0:4], AF.Exp, [pkA(4)], ["small"], scale=-1.0)
                    ACT(beta, beta, AF.Ln, ["small", "epst"], ["small"], bias=epst[0:P, 2:3])
                    ACT(beta, beta, AF.Exp, ["small"], ["small"], scale=-1.0)
                    TT(gg, PSB(4)[0:P, 4:8], a_dt[0:P], ALU.add, [pkA(4), "rowc"], ["small"])
                    ACT(gg, gg, AF.Exp, ["small"], ["small"])
                    ACT(gg, gg, AF.Ln, ["small", "epst"], ["small"], bias=epst[0:P, 2:3])
                    STT(gg, gg, -1.0, ea[0:P, li, :], ALU.mult, ALU.mult, ["small", "ea"], ["small"])
                    PE([(PSB(4)[0:P, 16:20], C("triU" + sfx, P, 0, P), gg, True, True),
                        (PSB(4)[0:P, 20:24], C("ones", P, 0, P), gg, True, True),
                        (PSB(4)[:, 24:28], C("ones", P, 0, 128), gg, True, True)],
                       ["small", "cst"], [pkA(4)])
                    CP(B().sm_[0:P, 8:16], PSB(4)[0:P, 16:24], [pkA(4)], ["small"])
                    CP(B().sm_[0:P, 32:36], Gc, ["small"], ["small"])
                    TT(B().sm_[0:P, 36:40], Gl, Gc, ALU.subtract, ["small"], ["small"])
                    CP(B().sm_[0:P, 40:44], Gl, ["small"], ["small"])
                    ACT(ex3, B().sm_[0:P, 32:44], AF.Exp, ["small"], ["small"])
                    ACT(B().gta[:], PSB(4)[:, 24:28], AF.Exp, [pkA(4)], ["gta"])
                    eG = B().sm_[0:P, 16:20]
                    eGlG = B().sm_[0:P, 20:24]
                    TT(bG, beta, eG, ALU.mult, ["small"], ["small"])
                    chk(43)
                    yield
                    idf = C("ident")
                    TR([(PSB(0)[0:P, f * 128:(f + 1) * 128], B().cvb[:, f, 0:P]) for f in range(4)], idf, ["cvb", "cst", "pre"], [pkA(0)])
                    TR([(PSB(1)[0:P, f * 128:(f + 1) * 128], B().cvb[:, 4 + f, 0:P]) for f in range(4)], idf, ["cvb", "cst"], [pkA(1)])
                    TR([(PSB(2)[0:P, f * 128:(f + 1) * 128], B().cvb[:, 8 + f, 0:P]) for f in range(4)], idf, ["cvb", "cst"], [pkA(2)])
                    CP(B().qtok[0:P], PSB(0)[0:P, :].rearrange("p (h d) -> p h d", h=4), [pkA(0)], ["qtok"], eng="act")
                    k3 = PSB(1)[0:P, :].rearrange("p (h d) -> p h d", h=4)
                    v3 = PSB(2)[0:P, :].rearrange("p (h d) -> p h d", h=4)
                    TT(B().rhs_t[0:P, :, 128:256], k3, bG.unsqueeze(2).broadcast_to([P, 4, 128]), ALU.mult, [pkA(1), "small"], ["ctmp"])
                    TT(B().kdec[0:P], k3, eGlG.unsqueeze(2).broadcast_to([P, 4, 128]), ALU.mult, [pkA(1), "small"], ["kdec"])
                    TT(B().rhs_t[0:P, :, 0:128], v3, beta.unsqueeze(2).broadcast_to([P, 4, 128]), ALU.mult, [pkA(2), "small"], ["ctmp"])
                    chk(44)
                    yield
                    for hh in range(4):
                        TS_(B().dGB[0:P, hh, 0, 0:P], C("ident", P, 0, P), Gc[:, hh:hh + 1], None, ALU.mult, None, ["cst", "small"], ["dGB"])
                        TS_(B().dGB[0:P, hh, 1, 0:P], C("ident", P, 0, P), beta[:, hh:hh + 1], None, ALU.mult, None, ["cst", "small"], ["dGB"])
                    if P == 64:
                        PE([(PSB(5)[0:P, 0:512], C("ones", P, 0, P), B().dGB[:].rearrange("p h w t -> p (h w t)"), True, True)],
                           ["dGB", "cst", "qtok"], [pkA(5)])
                    else:
                        PE([(PSB(5)[0:P, hh * 128 + w * 64: hh * 128 + w * 64 + P], C("ones", P, 0, P), B().dGB[0:P, hh, w, 0:P], True, True)
                            for hh in range(4) for w in range(2)], ["dGB", "cst", "qtok"], [pkA(5)])
                    GB = PSB(5)[0:P, :].rearrange("p (h w t) -> p h w t", h=4, w=2)
                    Grow = GB[:, :, 0, 0:P]
                    brow = GB[:, :, 1, 0:P]
                    PE([(PSB(3)[0:P, hh * 64:hh * 64 + P], B().cvb[:, 4 + hh, 0:P], B().cvb[:, 4 + hh, 0:P], True, True) for hh in range(4)] +
                       [(PSB(3)[0:P, 256 + hh * 64:256 + hh * 64 + P], B().cvb[:, 4 + hh, 0:P], B().cvb[:, hh, 0:P], True, True) for hh in range(4)],
                       ["cvb", "kdec", "ctmp", "rs2"], [pkA(3)])
                    yield
                    kk = PSB(3)[0:P, 0:256].rearrange("p (h t) -> p h t", h=4)[:, :, 0:P]
                    qkT = PSB(3)[0:P, 256:512].rearrange("p (h t) -> p h t", h=4)[:, :, 0:P]
                    Gb = Gc.unsqueeze(2).broadcast_to([P, 4, P])
                    TT(B().t1[0:P, :, 0:P], Grow, Gb, ALU.subtract, [pkA(5), "small"], ["t1"])
                    TS_(B().DU[0:P, :, 0:P], B().t1[0:P, :, 0:P], 0.0, None, ALU.min, None, ["t1"], ["DU"])
                    TS_(B().DL[0:P, :, 0:P], B().t1[0:P, :, 0:P], 0.0, None, ALU.max, None, ["t1"], ["DL"])
                    ACT(B().DU[0:P, :, 0:P], B().DU[0:P, :, 0:P], AF.Exp, ["DU"], ["DU"])
                    ACT(B().DL[0:P, :, 0:P], B().DL[0:P, :, 0:P], AF.Exp, ["DL"], ["DL"], scale=-1.0)
                    yield
                    mU = C("mU" + sfx, P, 0, P).unsqueeze(1).broadcast_to([P, 4, P])
                    mL = C("mL" + sfx, P, 0, P).unsqueeze(1).broadcast_to([P, 4, P])
                    mI = C("mI" + sfx, P, 0, P).unsqueeze(1).broadcast_to([P, 4, P])
                    TT(B().t1[0:P, :, 0:P], kk, B().DU[0:P, :, 0:P], ALU.mult, [pkA(3), "DU"], ["t1"])
                    TT(B().t2[0:P, :, 0:P], brow, mU, ALU.mult, [pkA(5), "cst"], ["t2"])
                    TT(B().Pm[0][0:P, :, 0:P], B().t1[0:P, :, 0:P], B().t2[0:P, :, 0:P], ALU.mult, ["t1", "t2"], ["Pm0"])
                    TT(B().t1[0:P, :, 0:P], kk, B().DL[0:P, :, 0:P], ALU.mult, [pkA(3), "DL"], ["t1"])
                    TT(B().t2[0:P, :, 0:P], mL, beta.unsqueeze(2).broadcast_to([P, 4, P]), ALU.mult, ["cst", "small"], ["t2"])
                    TT(B().Qm[0][0:P, :, 0:P], B().t1[0:P, :, 0:P], B().t2[0:P, :, 0:P], ALU.mult, ["t1", "t2"], ["Qm0"])
                    TT(B().t1[0:P, :, 0:P], qkT, B().DU[0:P, :, 0:P], ALU.mult, [pkA(3), "DU"], ["t1"])
                    TT(B().Mqk[0:P, :, 0:P], B().t1[0:P, :, 0:P], mI, ALU.mult, ["t1", "cst"], ["Mqk"])
                    TS_(B().nMqk[0:P, :, 0:P], B().Mqk[0:P, :, 0:P], -1.0, None, ALU.mult, None, ["Mqk"], ["nMqk"])
                    for hh in range(4):
                        TS_(B().dE[0:P, hh, 0:P], C("ident", P, 0, P), eG[:, hh:hh + 1], QS, ALU.mult, ALU.mult, ["cst", "small"], ["dE"])
                    chk(45)
                    TT(B().Rm[0][0:P, :, 0:P], B().Pm[0][0:P, :, 0:P], C("ident", P, 0, P).unsqueeze(1).broadcast_to([P, 4, P]), ALU.add,
                       ["Pm0", "cst"], ["Rm0"])
                    nr = 5 if P == 64 else 4
                    cur = 0
                    rc = 0
                    yield
                    for r in range(nr + 1):
                        nx = 1 - cur
                        sq_ = r < nr
                        upd = r >= 1
                        mms = []
                        if sq_:
                            if r < nr - 1:
                                mms += [(PSB(5)[0:P, hh * 64:hh * 64 + P], B().Qm[cur][0:P, hh, 0:P], B().Pm[cur][0:P, hh, 0:P], True, True) for hh in range(4)]
                            mms += [(PSB(5)[0:P, 256 + hh * 64:256 + hh * 64 + P], B().Pm[cur][0:P, hh, 0:P], B().Qm[cur][0:P, hh, 0:P], True, True) for hh in range(4)]
                        wr = [pkA(5)] if sq_ else []
                        if upd:
                            mms += [(PSB(4)[0:P, hh * 64:hh * 64 + P], B().Qm[cur][0:P, hh, 0:P], B().Rm[rc][0:P, hh, 0:P], True, True) for hh in range(4)]
                            wr.append(pkA(4))
                        PE(mms, [f"Pm{cur}", f"Qm{cur}", f"Rm{rc}", "t1", "t2", "Mqk", "gta", "small"], wr)
                        pq = PSB(5)[0:P, :].rearrange("p (w h t) -> p w h t", w=2, h=4)
                        if sq_:
                            if r < nr - 1:
                                CP(B().Pm[nx][0:P, :, 0:P], pq[:, 0, :, 0:P], [pkA(5)], [f"Pm{nx}"], eng="act")
                            CP(B().Qm[nx][0:P, :, 0:P], pq[:, 1, :, 0:P], [pkA(5)], [f"Qm{nx}"], eng="act")
                        if upd:
                            TT(B().Rm[1 - rc][0:P, :, 0:P], B().Rm[rc][0:P, :, 0:P],
                               PSB(4)[0:P, 0:256].rearrange("p (h t) -> p h t", h=4)[:, :, 0:P], ALU.add, [f"Rm{rc}", pkA(4)], [f"Rm{1 - rc}"])
                            rc = 1 - rc
                        cur = nx
                        yield
                    cur = rc
                    chk(46)
                    PE([(PSB((0, 1)[hh // 2])[0:P, (hh % 2) * 256:(hh % 2) * 256 + 256], B().Rm[cur][0:P, hh, 0:P], B().rhs_t[0:P, hh, :], True, True)
                        for hh in range(4)], [f"Rm{cur}", "ctmp"], [pkA(0), pkA(1)])
                    CP(B().uw[0:P, 0:2, :], PSB(0)[0:P, :].rearrange("p (h w) -> p h w", h=2), [pkA(0)], ["uw"], eng="act")
                    CP(B().uw[0:P, 2:4, :], PSB(1)[0:P, :].rearrange("p (h w) -> p h w", h=2), [pkA(1)], ["uw"])
                    yield
                    PE([(PSB(2)[:, hh * 128:(hh + 1) * 128], B().uw[0:P, hh, 128:256], B().kdec[0:P, hh, :], True, True) for hh in range(4)],
                       ["uw", "kdec"], [pkA(2)])
                    for hh in range(4):
                        STT(B().AT[:, hh, :], C("ident"), B().gta[:, hh:hh + 1], PSB(2)[:, hh * 128:(hh + 1) * 128], ALU.mult, ALU.subtract,
                            ["cst", "gta", pkA(2)], ["AT"])

                def chain_step(P, state, c0, width):
                    mms = []
                    for hh in range(4):
                        bnk = (0, 1)[hh // 2]
                        o0 = (hh % 2) * 256
                        mms.append((PSB(bnk)[:, o0:o0 + width], B().AT[:, hh, :], state[:, hh, c0:c0 + width], True, False))
                        mms.append((PSB(bnk)[:, o0 + width - 128:o0 + width], B().kdec[0:P, hh, :], B().uw[0:P, hh, 0:128], False, True))
                    PE(mms, ["AT", "state", "kdec", "uw"], [pkA(0), pkA(1)])
                    yield
                    for hp in range(2):
                        CP(state[:, 2 * hp:2 * hp + 2, c0:c0 + width],
                           PSB(hp)[:, :].rearrange("p (h w) -> p h w", h=2)[:, :, 0:width], [pkA(hp)], ["state"],
                           eng=("act" if hp == 0 else "dve"))

                def gdn_qp(P):
                    PE([m for hh in range(4) for m in (
                        (PSB(3)[:, hh * 64:hh * 64 + P], B().qtok[0:P, hh, :], B().dE[0:P, hh, 0:P], True, False),
                        (PSB(3)[:, hh * 64:hh * 64 + P], B().uw[0:P, hh, 128:256], B().nMqk[0:P, hh, 0:P], False, True))],
                       ["qtok", "dE", "uw", "nMqk"], [pkA(3)])
                    yield
                    CP(B().QpT[:, :, 0:P], PSB(3)[:, 0:256].rearrange("p (h t) -> p h t", h=4)[:, :, 0:P], [pkA(3)], ["QpT"], eng="act")

                def gdn_fin(P, par, mv=None, mk=None):
                    mixT = mixTs[par] if mv is None else mv
                    mk = ("mixT", par) if mk is None else mk
                    for hh in range(4):
                        ACT(B().oa[0:P, hh * 128:(hh + 1) * 128], B().osb[0:P, hh * 128:(hh + 1) * 128], AF.Square, ["osb"], ["oa", "ssmA"],
                            accum=B().ssmA[0:P, hh:hh + 1])
                    yield
                    RSQRT(B().ssmA[0:P, 4:8], B().ssmA[0:P, 0:4], 1.0 / 128, 0, ["ssmA", "epst"], ["ssmA"])
                    o3 = B().osb[0:P, :].rearrange("p (h d) -> p h d", h=4)
                    TT(o3, o3, B().ssmA[0:P, 4:8].unsqueeze(2).broadcast_to([P, 4, 128]), ALU.mult, ["osb", "ssmA"], ["osb"])
                    TT(o3, o3, gnw[0:P].unsqueeze(1).broadcast_to([P, 4, 128]), ALU.mult, ["osb", "rowc"], ["osb"])
                    TT(B().oa[0:P, :], B().osb[0:P, :], B().zs[0:P, :], ALU.mult, ["osb", "zs"], ["oa"])
                    TRb([(pbf(BKS[CURQ[0]][2])[:, hh * 64:hh * 64 + P], B().oa[0:P, hh * 128:(hh + 1) * 128]) for hh in range(4)],
                        ["oa", "identb"], [pkA(2)])
                    yield
                    CP(mixT[:, 0:4, 0:P], pbf(BKS[CURQ[0]][2])[:, 0:256].rearrange("p (h t) -> p h t", h=4)[:, :, 0:P], [pkA(2)], [mk], eng="act")

                def zproj(P, par):
                    hb = hbs[par]
                    PE([(PSB(5)[0:P, :], hb[:, kc, 0:P], W()[:, kc, WS["z"]:WS["z"] + 512], kc == 0, kc == KC - 1) for kc in range(KC)],
                       [("hb", par), "wmi"], [pkA(5)])
                    yield
                    ACT(B().zs[0:P, :], PSB(5)[0:P, :], AF.Exp, [pkA(5)], ["zs"], scale=-1.0)
                    ACT(B().zs[0:P, :], B().zs[0:P, :], AF.Ln, ["zs", "epst"], ["zs"], bias=epst[0:P, 2:3])
                    ACT(B().zs[0:P, :], B().zs[0:P, :], AF.Exp, ["zs"], ["zs"], scale=-1.0)
                    TT(B().zs[0:P, :], B().zs[0:P, :], PSB(5)[0:P, :], ALU.mult, ["zs", pkA(5)], ["zs"])

                def gdn_out(P, state, c0, col0, par=0):
                    yield from gdn_qp(P)
                    PE([m for hh in range(4) for m in (
                        (PSB(4)[0:P, hh * 128:(hh + 1) * 128], B().QpT[:, hh, 0:P], state[:, hh, c0:c0 + 128], True, False),
                        (PSB(4)[0:P, hh * 128:(hh + 1) * 128], B().Mqk[0:P, hh, 0:P], B().uw[0:P, hh, 0:128], False, True))],
                       ["QpT", "state", "Mqk", "uw"], [pkA(4)])
                    yield from zproj(P, par)
                    CP(B().osb[0:P, :], PSB(4)[0:P, :], [pkA(4)], ["osb"])
                    yield from gdn_fin(P, par)

                def gdn_loc(P, b):
                    yield from gdn_qp(P)
                    PE([(PSB(5)[:, hh * 64:hh * 64 + P], Tt[:, hh, 0:128], B().QpT[:, hh, 0:P], True, True) for hh in range(4)] +
                       [m for hh in range(4) for m in (
                        (PSB(4)[0:P, hh * 128:(hh + 1) * 128], B().QpT[:, hh, 0:P], Tt[:, hh, 128:256], True, False),
                        (PSB(4)[0:P, hh * 128:(hh + 1) * 128], B().Mqk[0:P, hh, 0:P], B().uw[0:P, hh, 0:128], False, True))],
                       ["QpT", "state", "Mqk", "uw"], [pkA(5), pkA(4)])
                    yield
                    CP(zst[:], PSB(5)[:, 0:256], [pkA(5)], ["zstS"], eng="act")
                    CP(osb[0:P, :], PSB(4)[0:P, :], [pkA(4)], ["osbS"])
                    T.dma("sp", "spill", sp_z.ap()[b], zst[:], reads=["zstS"], writes=[("d:spz", b)])
                    T.dma("sp", "spill", sp_o.ap()[b], osb[0:P, :], reads=["osbS"], writes=[("d:spo", b)])

                def gdn_corr(P, b, par, mv=None, mk=None):
                    T.dma("sp", "fill", B().zst[:], sp_z.ap()[b], reads=[("d:spz", b)], writes=["zst"])
                    T.dma("sp", "fill", B().osb[0:P, :], sp_o.ap()[b], reads=[("d:spo", b)], writes=["osb"])
                    yield from zproj(P, par)
                    PE([(PSB(4)[0:P, hh * 128:(hh + 1) * 128], B().zst[:, hh * 64:hh * 64 + P], Tt[:, hh, 128:256], True, True) for hh in range(4)],
                       ["zst", "state"], [pkA(4)])
                    yield
                    TT(B().osb[0:P, :], B().osb[0:P, :], PSB(4)[0:P, :], ALU.add, ["osb", pkA(4)], ["osb"])
                    yield from gdn_fin(P, par, mv, mk)

                def pbf(i):
                    return psb16[i]

                def TRb(oi, reads, writes):
                    def fn(h, oi=oi):
                        ins = None
                        for (o, i) in oi:
                            ins = h.transpose(out=o, in_=i, identity=identb[0:i.shape[0], 0:i.shape[0]])
                        return ins
                    T.op("pe", fn, reads, writes)

                def swa_proj(P, bidx, slot, par=0, kout=None, vout=None, kv_only=False):
                    hb = hbs[par]
                    if kv_only:
                        PE([(ps[7][0:P, 0:256], hb[:, kc, 0:P], W()[:, kc, WS["kvb"]:WS["kvb"] + 256], kc == 0, kc == KC - 1) for kc in range(KC)],
                           [("hb", par), "wmi"], [pk(7)])
                        yield
                        CP(qb[0:P, 8:10, :], ps[7][0:P, 0:128].rearrange("p (h d) -> p h d", h=2), [pk(7)], ["qb"], eng="act")
                        CP(vtok[0:P, :], ps[7][0:P, 128:256], [pk(7)], ["vtok"], eng="act")
                        co, _ = COFF["cos"]
                        so, _ = COFF["sin"]
                        cosb = cst[0:P, co + bidx * 8:co + bidx * 8 + 8].unsqueeze(1).broadcast_to([P, 2, 8])
                        sinb = cst[0:P, so + bidx * 8:so + bidx * 8 + 8].unsqueeze(1).broadcast_to([P, 2, 8])
                        x1 = qb[0:P, 8:10, 0:8]
                        x2 = qb[0:P, 8:10, 8:16]
                        TT(rt[0:P, 0, 8:10], x1, cosb, ALU.mult, ["qb", "cst"], ["rt"])
                        TT(rt[0:P, 1, 8:10], x2, sinb, ALU.mult, ["qb", "cst"], ["rt"])
                        TT(rt[0:P, 2, 8:10], x2, cosb, ALU.mult, ["qb", "cst"], ["rt"])
                        TT(rt[0:P, 3, 8:10], x1, sinb, ALU.mult, ["qb", "cst"], ["rt"])
                        TT(x1, rt[0:P, 0, 8:10], rt[0:P, 1, 8:10], ALU.subtract, ["rt"], ["qb"])
                        TT(x2, rt[0:P, 2, 8:10], rt[0:P, 3, 8:10], ALU.add, ["rt"], ["qb"])
                        return
                    PE([(ps[6][0:P, :], hb[:, kc, 0:P], W()[:, kc, WS["qb"]:WS["qb"] + 512], kc == 0, kc == KC - 1) for kc in range(KC)] +
                       [(ps[7][0:P, 0:256], hb[:, kc, 0:P], W()[:, kc, WS["kvb"]:WS["kvb"] + 256], kc == 0, kc == KC - 1) for kc in range(KC)],
                       [("hb", par), "wmi"], [pk(6), pk(7)])
                    yield
                    CP(qb[0:P, 0:8, :], ps[6][0:P, :].rearrange("p (h d) -> p h d", h=8), [pk(6)], ["qb"], eng="act")
                    CP(qb[0:P, 8:10, :], ps[7][0:P, 0:128].rearrange("p (h d) -> p h d", h=2), [pk(7)], ["qb"], eng="act")
                    CP(vtok[0:P, :], ps[7][0:P, 128:256], [pk(7)], ["vtok"], eng="act")
                    yield
                    co, _ = COFF["cos"]
                    so, _ = COFF["sin"]
                    cosb = cst[0:P, co + bidx * 8:co + bidx * 8 + 8].unsqueeze(1).broadcast_to([P, 10, 8])
                    sinb = cst[0:P, so + bidx * 8:so + bidx * 8 + 8].unsqueeze(1).broadcast_to([P, 10, 8])
                    x1 = qb[0:P, :, 0:8]
                    x2 = qb[0:P, :, 8:16]
                    TT(rt[0:P, 0], x1, cosb, ALU.mult, ["qb", "cst"], ["rt"])
                    TT(rt[0:P, 1], x2, sinb, ALU.mult, ["qb", "cst"], ["rt"])
                    TT(rt[0:P, 2], x2, cosb, ALU.mult, ["qb", "cst"], ["rt"])
                    TT(rt[0:P, 3], x1, sinb, ALU.mult, ["qb", "cst"], ["rt"])
                    TT(x1, rt[0:P, 0], rt[0:P, 1], ALU.subtract, ["rt"], ["qb"])
                    TT(x2, rt[0:P, 2], rt[0:P, 3], ALU.add, ["rt"], ["qb"])
                    CP(qr[0:P], qb[0:P], ["qb"], ["qr"])
                    for g in range(2):
                        CP(Vw[0:P, slot, g, 0, 0:64], vtok[0:P, g * 64:(g + 1) * 64], ["vtok"], ["Vw"])
                        CP(Vw[0:P, slot, g, 1, 64:128], vtok[0:P, g * 64:(g + 1) * 64], ["vtok"], ["Vw"])
                    yield
                    TRb([(pbf(6)[0:64, hh * 64:hh * 64 + P], qr[0:P, hh, :]) for hh in range(10)], ["qr", "identb"], [pk(6)])
                    yield
                    CP(qT[:, :, 0:P], pbf(6)[0:64, 0:512].rearrange("p (h t) -> p h t", h=8)[:, :, 0:P], [pk(6)], ["qT"], eng="act")
                    CP(kTw[:, :, slot, 0:P], pbf(6)[0:64, 512:640].rearrange("p (h t) -> p h t", h=2)[:, :, 0:P], [pk(6)], ["kTw"], eng="act")
                    if kout is not None:
                        T.dma("sp", "outk", kout, qb[0:P, 8:10, :].rearrange("p h d -> p (h d)"), reads=["qb"])
                        T.dma("sp", "outk", vout, vtok[0:P, :], reads=["vtok"])

                def swa_attn(P, nkeys, maskname, par=0, mv=None, mk=None):
                    mixT = mixTs[par] if mv is None else mv
                    mk = ("mixT", par) if mk is None else mk
                    NK = 192
                    for hg in range(2):
                        PE([(ps[6 + (hq // 2)][0:P, (hq % 2) * 192 + s * 64: (hq % 2) * 192 + s * 64 + nkeys[s]], qT[:, hg * 4 + hq, 0:P],
                             kTw[:, hg, s, 0:nkeys[s]], True, True) for hq in range(4) for s in range(3)],
                           ["qT", "kTw"], [pk(6), pk(7)])
                        yield
                        for hp in range(2):
                            src = ps[6 + hp][0:P, 0:384].rearrange("p (h k) -> p h k", h=2)
                            if maskname is None:
                                nkw = 128 + nkeys[2]
                                TS_(sco[0:P, 2 * hp:2 * hp + 2, 0:nkw], src[:, :, 0:nkw], 0.125, None, ALU.mult, None, [pk(6 + hp)], ["sco"])
                            else:
                                STT(sco[0:P, 2 * hp:2 * hp + 2, :], src, 0.125,
                                    C(maskname, P).unsqueeze(1).broadcast_to([P, 2, 192]), ALU.mult, ALU.add,
                                    [pk(6 + hp), "cst"], ["sco"])
                        if nkeys[2] < 64 or nkeys[1] < 64 or nkeys[0] < 64:
                            for s in range(3):
                                if nkeys[s] < 64:
                                    T.op("dve", lambda h, s=s: h.memset(sco[0:P, :, s * 64 + nkeys[s]:(s + 1) * 64], NEG), ["sco"], ["sco"])
                        mx = ssm[0:P, 8:12]
                        T.op("dve", lambda h: h.tensor_reduce(out=mx, in_=sco[0:P], axis=AX.X, op=ALU.max), ["sco"], ["ssm"])
                        TT(mx, mx, sinks[0:P, hg * 4:hg * 4 + 4], ALU.max, ["ssm", "rowc"], ["ssm"])
                        TS_(ssm[0:P, 12:16], mx, -1.0, None, ALU.mult, None, ["ssm"], ["ssm"])
                        yield
                        for hq in range(4):
                            ACT(sco[0:P, hq, :], sco[0:P, hq, :], AF.Exp, ["sco", "ssm"], ["sco", "ssm2"],
                                bias=ssm[0:P, 12 + hq:13 + hq], accum=ssm[0:P, 16 + hq:17 + hq])
                        yield
                        TT(ssm[0:P, 20:24], sinks[0:P, hg * 4:hg * 4 + 4], mx, ALU.subtract, ["ssm", "rowc"], ["ssm"])
                        ACT(ssm[0:P, 20:24], ssm[0:P, 20:24], AF.Exp, ["ssm"], ["ssm"])
                        TT(ssm[0:P, 24:28], ssm[0:P, 20:24], ssm[0:P, 16:20], ALU.add, ["ssm", "ssm2"], ["ssm"])
                        T.op("dve", lambda h: h.reciprocal(out=ssm[0:P, 24:28], in_=ssm[0:P, 24:28]), ["ssm"], ["ssm"])
                        TT(pn[0:P], sco[0:P], ssm[0:P, 24:28].unsqueeze(2).broadcast_to([P, 4, 192]), ALU.mult, ["sco", "ssm"], ["pn"])
                        yield
                        TRb([(pbf(6)[0:nkeys[s], (hq * 3 + s) * 64:(hq * 3 + s) * 64 + P], pn[0:P, hq, s * 64:s * 64 + nkeys[s]])
                             for hq in range(4) for s in range(3)], ["pn", "identb", "qT", "kTw"], [pk(6)])
                        yield
                        CP(pT[:, :, :, 0:P], pbf(6)[0:64, 0:768].rearrange("p (h s t) -> p h s t", h=4, s=3)[:, :, :, 0:P], [pk(6)], ["pT"],
                           eng="act")
                        yield
                        mms = []
                        for pr in range(2):
                            k = 0
                            for par_ in range(2):
                                for s in range(3):
                                    mms.append((ps[7][:, pr * 64:pr * 64 + P], Vw[0:nkeys[s], s, hg, par_, :],
                                                pT[0:nkeys[s], pr * 2 + par_, s, 0:P], k == 0, k == 5))
                                    k += 1
                        PE(mms, ["Vw", "pT"], [pk(7)])
                        yield
                        CP(mixT[:, 4 + hg * 2:6 + hg * 2, 0:P], ps[7][:, 0:128].rearrange("p (h t) -> p h t", h=2)[:, :, 0:P], [pk(7)], [mk],
                           eng="act")

                def out_proj(P, col0, par=0, mv=None, mk=None):
                    mixT = mixTs[par] if mv is None else mv
                    mk = ("mixT", par) if mk is None else mk
                    sw = max(64, P)
                    for mh in range(2):
                        PE([(ps[6 + mh][:, mm * sw:mm * sw + P], wmo[:, kc, (mh * 4 + mm) * 128:(mh * 4 + mm + 1) * 128], mixT[:, kc, 0:P],
                             kc == 0, kc == KC - 1) for mm in range(4) for kc in range(KC)], ["wmo", mk], [pk(6 + mh)])
                        yield
                        TT(xT[:, mh * 4:mh * 4 + 4, col0:col0 + P], xT[:, mh * 4:mh * 4 + 4, col0:col0 + P],
                           ps[6 + mh][:, 0:4 * sw].rearrange("p (m t) -> p m t", m=4)[:, :, 0:P], ALU.add, ["xT", pk(6 + mh)], ["xT"])

                def run(*gens):
                    gens = [(g if isinstance(g, tuple) else (g, 0)) for g in gens if g is not None]
                    while gens:
                        for it in list(gens):
                            CURQ[0] = it[1]
                            try:
                                next(it[0])
                            except StopIteration:
                                gens.remove(it)
                    CURQ[0] = 0

                make_h(TP - 64, 64, 0)
                PE([(ps[0][:, f * 4:f * 4 + 3], W()[:, kc, f * 128:(f + 1) * 128], hbs[0][:, kc, 61:64], kc == 0, kc == KC - 1)
                    for f in range(12) for kc in range(KC)], ["wmi", ("hb", 0)], [pk(0)])
                CP(halo[:], ps[0][:, 0:48].rearrange("p (f t) -> p f t", f=12)[:, :, 0:3], [pk(0)], ["halo"])
                T.dma("sp", "x_in", x_in[2].ap()[:, 256:292], halo[:].rearrange("p f t -> p (f t)"), reads=["halo"], writes=["d:x_in2"])
                chk(1)
                allgather(2)
                make_h(0, 64, 0)
                CURQ[0] = 1
                make_h(64, 64, 1)
                CURQ[0] = 0
                T.dma("sp", "xl", hsel[:], g8v[2][:, :, 256:292], reads=["d:x_g82"], writes=["hsel"])
                hf = halo[:].rearrange("p f t -> p (f t)")
                T.op("dve", lambda h: h.memset(hf, 0.0), ["halo"], ["halo"])
                for r in range(NCORES - 1):
                    STT(hf, hsel[:, r, :], C("ohp", 128, r, r + 1), hf, ALU.mult, ALU.add, ["hsel", "cst", "halo"], ["halo"])

                chk(2)
                T.op("dve", lambda h: h.memset(Tt[:], 0.0), ["state"], ["state"])
                for hh in range(4):
                    CP(Tt[:, hh, 0:128], C("ident"), ["cst", "state"], ["state"])
                order = {"halo": 0, "chain": 0}

                def p1_block(b, q):
                    while order["halo"] < b:
                        yield
                    if b >= 2:
                        make_h(b * 64, 64, q)
                    g = gdn_pre(64, "64", q)
                    for _ in g:
                        if order["halo"] == b and _ == "halo":
                            order["halo"] = b + 1
                        yield
                    order["halo"] = max(order["halo"], b + 1)
                    while order["chain"] < b:
                        yield
                    yield from gdn_loc(64, b)
                    yield from chain_step(64, Tt, 0, 256)
                    order["chain"] = b + 1
                    if b >= NB - 2:
                        yield from swa_proj(64, b, b % 3, q, kv_only=True)
                        hc = b - (NB - 2)
                        T.dma("sp", "x_in", x_in[2].ap()[hc * 64:(hc + 1) * 64, 0:128], qb[:, 8:10, :].rearrange("p h d -> p (h d)"),
                              reads=["qb"], writes=["d:x_in2"])
                        T.dma("sp", "x_in", x_in[2].ap()[hc * 64:(hc + 1) * 64, 128:256], vtok[:], reads=["vtok"], writes=["d:x_in2"])
                        T.dma("sp", "outk", o_kp[hc * 64:(hc + 1) * 64, li], qb[:, 8:10, :].rearrange("p h d -> p (h d)"), reads=["qb"])
                        T.dma("sp", "outk", o_vp[hc * 64:(hc + 1) * 64, li], vtok[:], reads=["vtok"])

                def p1_stream(q):
                    for b in range(q, NB, 2):
                        yield from p1_block(b, q)
                run((p1_stream(0), 0), (p1_stream(1), 1))
                T.dma("sp", "outc", o_convp[:, li], halo[:], reads=["halo"])
                chk(6)
                T.barrier()
                p1.close()
                SETS[1] = None
                p2 = ExitStack()
                wmi = sb("wmi", (128, KC, DIN), BF16, p2)
                wmo = sb("wmo", (128, KC, DM), BF16, p2)
                hb1 = sb("hb1b", (128, KC, 64), BF16, p2)
                hbs = [SETS[0].hb, hb1]
                zs = sb("zs", (128, 512), F32, p2)
                oa = sb("oa", (64, 512), BF16, p2)
                mixTs = [sb(f"mixT{i}", (128, 8, 64), BF16, p2) for i in range(2)]
                sco = sb("sco", (64, 4, 192), F32, p2)
                ssmA = sb("ssmA", (64, 8), F32, p2)
                pn = sb("pn", (64, 4, 192), BF16, p2)
                pT = sb("pT", (64, 4, 3, 64), BF16, p2)
                ssm = sb("ssm", (64, 32), F32, p2)
                kvh = sb("kvh", (128, 2, 128), F32, p2)
                ckb = sb("ckb", (128, 2, 128), F32, p2)
                S0 = SETS[0]
                S0.osb, S0.zst, S0.zs, S0.oa, S0.ssmA = osb, zst, zs, oa, ssmA
                S1 = _NS()
                S1.sqb = S0.cvb[:].rearrange("p f t -> p (f t)").bitcast(BF16)[:, 0:512].rearrange("p (k t) -> p k t", k=8)
                S1.rsb = S0.QpT[:, 0, :]
                S1.osb = S0.uw[:, 0:2, :].rearrange("p h w -> p (h w)")
                S1.zs = S0.uw[:, 2:4, :].rearrange("p h w -> p (h w)")
                S1.zst = S0.AT[:, 0:2, :].rearrange("p h d -> p (h d)")
                S1.oa = S0.kdec[:].rearrange("p h d -> p (h d)").bitcast(BF16)[:, 0:512]
                S1.ssmA = S0.sm_[0:64, 0:8]
                SETS[1] = S1
                hbA = S0.sq2
                hbB = S0.pre[:].rearrange("p f t -> p (f t)").bitcast(BF16)[:, 0:512].rearrange("p (k t) -> p k t", k=8)
                WS.update({"w": wmi, "gate": 2048, "kvb": 2568, "z": 1536, "qb": 2056})
                BKS[0] = {0: 0, 1: 1, 2: 2, 3: 3, 4: 4, 5: 5}
                BKS[1] = {0: 0, 1: 1, 2: 3, 3: 3, 4: 1, 5: 0}
                for kc in range(KC):
                    T.dma("pool", "wmi", wmi[:, kc, :], d_wmi[li, :, kc, :], writes=["wmi"])
                T.dma("pool", "wmo", wmo[:], d_wmo[li], writes=["wmo"])
                TR([(ps[0][:, hh * 128:(hh + 1) * 128], Tt[:, hh, 0:128]) for hh in range(4)], C("ident"), ["state", "cst"], [pk(0)])
                CP(B().uw[:, :, 0:128], ps[0][:, :].rearrange("p (h d) -> p h d", h=4), [pk(0)], ["uw"])
                CP(B().uw[:, :, 128:256], Tt[:, :, 128:256], ["state"], ["uw"])
                T.dma("sp", "x_in", x_in[0].ap(), B().uw[:, 0:2, :].rearrange("p h w -> p (h w)"), reads=["uw"], writes=["d:x_in0"])
                T.dma("sp", "x_in", x_in[1].ap(), B().uw[:, 2:4, :].rearrange("p h w -> p (h w)"), reads=["uw"], writes=["d:x_in1"])
                allgather(0)
                allgather(1)
                allgather(2)
                T.op("dve", lambda h: h.memset(Ssm[:], 0.0), ["Ssm"], ["Ssm"])
                T.op("dve", lambda h: h.memset(Tt[:, :, 128:256], 0.0), ["state"], ["state"])
                T.op("dve", lambda h: h.memset(kvh[:], 0.0), (), ["kvh"])
                for r in range(NCORES):
                    T.dma("sp", "xl", B().uw[:, 0:2, :].rearrange("p h w -> p (h w)"), g8v[0][:, r, :], reads=["d:x_g80"], writes=["uw"])
                    T.dma("sp", "xl", B().uw[:, 2:4, :].rearrange("p h w -> p (h w)"), g8v[1][:, r, :], reads=["d:x_g81"], writes=["uw"])
                    T.dma("sp", "xl2", zs[:, 0:256], g8v[2][:, r, 0:256], reads=["d:x_g82"], writes=["zs"])
                    if r < NCORES - 1:
                        STT(kvh[:].rearrange("p a b -> p (a b)"), zs[:, 0:256], C("ohp", 128, r, r + 1), kvh[:].rearrange("p a b -> p (a b)"),
                            ALU.mult, ALU.add, ["zs", "cst", "kvh"], ["kvh"])
                    if r < NCORES - 1:
                        PE([(ps[1][:, hh * 128:(hh + 1) * 128], B().uw[:, hh, 0:128], Ssm[:, hh, :], True, True) for hh in range(4)],
                           ["uw", "Ssm"], [pk(1)])
                        TT(B().AT[:], ps[1][:, :].rearrange("p (h d) -> p h d", h=4), B().uw[:, :, 128:256], ALU.add, [pk(1), "uw"], ["AT"])
                        TT(B().AT[:], B().AT[:], Ssm[:], ALU.subtract, ["AT", "Ssm"], ["AT"])
                        STT(Ssm[:], B().AT[:], C("mlt", 128, r, r + 1), Ssm[:], ALU.mult, ALU.add, ["AT", "cst", "Ssm"], ["Ssm"])
                    PE([(ps[2][:, hh * 128:(hh + 1) * 128], B().uw[:, hh, 0:128], Tt[:, hh, 128:256], True, True) for hh in range(4)],
                       ["uw", "state"], [pk(2)])
                    TT(Tt[:, :, 128:256], ps[2][:, :].rearrange("p (h d) -> p h d", h=4), B().uw[:, :, 128:256], ALU.add, [pk(2), "uw"], ["state"])
                T.dma("sp", "outd", o_deltap[:, li], Tt[:, :, 128:256], reads=["state"])
                CP(Tt[:, :, 128:256], Ssm[:], ["Ssm", "state"], ["state"])
                halo_kv_to_window = True
                if halo_kv_to_window:
                    for hc in range(2):
                        slot = 1 + hc
                        rows = slice(hc * 64, hc * 64 + 64)
                        def fn(h, rows=rows):
                            ins = None
                            for g in range(2):
                                ins = h.matmul(ps[7][0:64, 256 + g * 64:256 + g * 64 + 64], lhsT=kvh[rows, 0, g * 64:(g + 1) * 64],
                                               rhs=C("ident")[rows, rows], start=True, stop=True)
                            return ins
                        T.op("pe", fn, ["kvh", "cst"], [pk(7)])
                        CP(kTw[:, :, slot, :], ps[7][0:64, 256:384].rearrange("p (g t) -> p g t", g=2), [pk(7)], ["kTw"])
                        def fn2(h, rows=rows):
                            return h.matmul(ps[7][0:64, 384:512], lhsT=C("ident")[rows, rows], rhs=kvh[rows, 1, :], start=True, stop=True)
                        T.op("pe", fn2, ["kvh", "cst"], [pk(7)])
                        for g in range(2):
                            CP(Vw[:, slot, g, 0, 0:64], ps[7][0:64, 384 + g * 64:384 + g * 64 + 64], [pk(7)], ["Vw"])
                            CP(Vw[:, slot, g, 1, 64:128], ps[7][0:64, 384 + g * 64:384 + g * 64 + 64], [pk(7)], ["Vw"])

                chk(7)
                mixP = SETS[0].ctmp_full[:].bitcast(BF16).rearrange("p (q k t) -> p q k t", q=2, k=8)
                hbs = [S0.hb, hb1, hbA, hbB]
                prevB = None
                for b0 in range(0, NB, 2):
                    pp = (b0 // 2) % 2
                    mvp = mixP[:, pp]
                    mk = ("mixP", pp)

                    def genA(b, q, mvp=mvp, mk=mk):
                        make_h(b * 64, 64, b % 4)
                        yield
                        yield from gdn_corr(64, b, b % 4, mvp[:, :, (b % 2) * 64:(b % 2 + 1) * 64], mk)

                    def genB2(b0=b0, mvp=mvp, mk=mk):
                        for b in (b0, b0 + 1):
                            yield from swa_proj(64, b, b % 3, b % 4)
                            yield from swa_attn(64, [64, 64, 64], "mask0" if b == 0 else ("mask1" if b == 1 else None), b % 4,
                                                mvp[:, :, (b % 2) * 64:(b % 2 + 1) * 64], mk)
                        yield from out_proj(128, b0 * 64, 0, mvp, mk)
                    run((genA(b0, 0), 0), (genA(b0 + 1, 1), 1), prevB)
                    prevB = genB2()
                run(prevB)
                hbs = [S0.hb, hb1]
                chk(12)
                T.barrier()
                T.dma("sp", "ldc", halo[:], d_cc[:, li], writes=["halo"])
                T.dma("sp", "ldc", Tt[:, :, 128:256], d_st[:, li], writes=["state"])
                T.dma("sp", "ldc", ckb[:, 0, :], d_ck[:, li], writes=["ckb"])
                T.dma("sp", "ldc", ckb[:, 1, :], d_cv[:, li], writes=["ckb"])
                make_h(TP, 32, 0)
                run(gdn_pre(32, "32", 0))
                T.dma("sp", "outc", o_convs[:, li], halo[:], reads=["halo"])
                run(gdn_out(32, Tt, 128, TP, 0))
                run(chain_step(32, Tt, 128, 128))
                T.dma("sp", "outd", o_deltas[:, li], Tt[:, :, 128:256], reads=["state"])
                for hc in range(2):
                    rows = slice(hc * 64, hc * 64 + 64)
                    def fn(h, rows=rows):
                        ins = None
                        for g in range(2):
                            ins = h.matmul(ps[7][0:64, 256 + g * 64:256 + g * 64 + 64], lhsT=ckb[rows, 0, g * 64:(g + 1) * 64],
                                           rhs=C("ident")[rows, rows], start=True, stop=True)
                        return ins
                    T.op("pe", fn, ["ckb", "cst", "kTw"], [pk(7)])
                    CP(kTw[:, :, hc, :], ps[7][0:64, 256:384].rearrange("p (g t) -> p g t", g=2), [pk(7)], ["kTw"])
                    def fn2(h, rows=rows):
                        return h.matmul(ps[7][0:64, 384:512], lhsT=C("ident")[rows, rows], rhs=ckb[rows, 1, :], start=True, stop=True)
                    T.op("pe", fn2, ["ckb", "cst"], [pk(7)])
                    for g in range(2):
                        CP(Vw[:, hc, g, 0, 0:64], ps[7][0:64, 384 + g * 64:384 + g * 64 + 64], [pk(7)], ["Vw"])
                        CP(Vw[:, hc, g, 1, 64:128], ps[7][0:64, 384 + g * 64:384 + g * 64 + 64], [pk(7)], ["Vw"])
                run(swa_proj(32, 32, 2, 0, kout=o_ks[96:128, li], vout=o_vs[96:128, li]))
                T.dma("sp", "outk", o_ks[0:96, li], ckb[32:128, 0, :], reads=["ckb"])
                T.dma("sp", "outk", o_vs[0:96, li], ckb[32:128, 1, :], reads=["ckb"])
                run(swa_attn(32, [64, 64, 32], None, 0))
                run(out_proj(32, TP, 0))
                T.barrier()
                p2.close()
                T.kmap = None

        psb16 = [p[:].bitcast(BF16) for p in ps]

        with ExitStack() as st0:
            ztile = sb("ztile", (128, 512), F32, st0)
            T.op("dve", lambda h: h.memset(ztile[:], 0.0), (), ["ztile"])
            for k in range(3):
                T.dma("sp", "x_in", x_in[k].ap(), ztile[:], reads=["ztile"], writes=[f"d:x_in{k}"])
            T.barrier()
        _stage = _KSTAGE
        for li in range(nlayers):
            if _stage in ("all", "ffn"):
                ffn(li * 2, li * 3)
            if _stage in ("all", "mix") and not _STOPPED[0]:
                try:
                    mixer(li)
                except _Stop:
                    T.barrier()
            if _stage in ("all",) and not _STOPPED[0]:
                ffn(li * 2 + 1, li * 3 + 2)

        with ExitStack() as st:
          if not _STOPPED[0]:
              sq = sb("sqF", (128, KC, FT), BF16, st)
              rstd = sb("rstdF", (128, FT), F32, st)
              yb = sb("yb", (128, KC, FT), F32, st)
              T.barrier()
              for t in range(NFT):
                  cs = slice(t * FT, (t + 1) * FT)
                  ACT(sq[:], xT[:, :, cs], AF.Square, ["xT"], ["sqF"])
                  PE([(ps[7][:, 0:FT], onesb[:], sq[:, kc, :], kc == 0, kc == KC - 1) for kc in range(KC)], ["sqF", "onesb"], [pk(7)])
                  RSQRT(rstd[:], ps[7][:, 0:FT], 1.0 / DM, 0, [pk(7), "epst"], ["rstdF"])
                  for kc in range(KC):
                      STT(yb[:, kc, :], xT[:, kc, cs], nrm[:, L * 3, kc:kc + 1], rstd[:], ALU.mult, ALU.mult,
                          ["xT", "nrm", "rstdF"], ["yb"])
                  T.dma("sp", "outy", o_y[:, :, cs], yb[:], reads=["yb"])
              T.barrier()
        with nc.Block() as block:
            T.emit(block)
    return nc


_NC_CACHE = {}


def _layout_inputs(inp):
    f = np.float32
    g = lambda k: np.asarray(inp[k], dtype=f)
    shared = {}
    wfi = np.stack([g("ff1_w_in"), g("ff2_w_in")], 1).reshape(L * 2, KC, 128, 2 * DFF).transpose(0, 2, 1, 3)
    shared["wfi"] = np.ascontiguousarray(wfi if _KSTAGE != "mix" else wfi[:2])
    wfo = np.stack([g("ff1_w_out"), g("ff2_w_out")], 1).reshape(L * 2, HC, 128, DM).transpose(0, 2, 1, 3)
    shared["wfo"] = np.ascontiguousarray(wfo if _KSTAGE != "mix" else wfo[:2])
    shared["wmi"] = np.ascontiguousarray(g("w_mix_in").reshape(L, KC, 128, DIN).transpose(0, 2, 1, 3))
    shared["wmo"] = np.ascontiguousarray(g("w_mix_out").reshape(L, KC, 128, DM).transpose(0, 2, 1, 3))
    nrm = np.zeros((L * 3 + 1, DM), f)
    for l in range(L):
        nrm[l * 3 + 0] = g("norm_ff1")[l]
        nrm[l * 3 + 1] = g("norm_mix")[l]
        nrm[l * 3 + 2] = g("norm_ff2")[l]
    nrm[L * 3] = g("norm_final")
    shared["nrm"] = np.ascontiguousarray(nrm.reshape(L * 3 + 1, KC, 128).transpose(2, 0, 1))
    shared["cw"] = np.ascontiguousarray(g("conv_w").reshape(L, 4, 12, 128).transpose(3, 0, 2, 1))
    row = np.concatenate([g("a_log"), g("dt_bias"), g("sinks"), g("gnorm_w")], axis=1)
    shared["rowc"] = np.ascontiguousarray(np.broadcast_to(row[None], (128, L, 144)))
    maps = []
    xp = g("x_prompt")[0]
    xs = g("x_sample")
    for c in range(NCORES):
        m = dict(shared)
        xc = np.concatenate([xp[c * TP:(c + 1) * TP], xs[c]], 0)
        m["xT"] = np.ascontiguousarray(xc.T.reshape(KC, 128, NT).transpose(1, 0, 2))
        m["cconv"] = np.ascontiguousarray(g("cache_conv")[:, c].reshape(L, 3, 12, 128).transpose(3, 0, 2, 1))
        m["state"] = np.ascontiguousarray(g("state_delta")[:, c].transpose(2, 0, 1, 3))
        m["ck"] = np.ascontiguousarray(g("cache_k")[:, c].reshape(L, 128, 128).transpose(1, 0, 2))
        m["cv"] = np.ascontiguousarray(g("cache_v")[:, c].reshape(L, 128, 128).transpose(1, 0, 2))
        m["cst"] = _make_consts(c)
        maps.append(m)
    return maps


def kernel(**inputs):
    if "nc" not in _NC_CACHE:
        _NC_CACHE["nc"] = build_program()
    nc = _NC_CACHE["nc"]
    maps = _layout_inputs(inputs)
    res = run_bass_kernel_spmd(nc, maps, core_ids=list(range(NCORES)))
    R = res.results
    f = np.float32
    yp = np.zeros((1, NCORES * TP, DM), f)
    ys = np.zeros((NCORES, TS, DM), f)
    for c in range(NCORES):
        y = np.asarray(R[c]["yT"]).transpose(1, 0, 2).reshape(DM, NT).T
        yp[0, c * TP:(c + 1) * TP] = y[:TP]
        ys[c] = y[TP:]

    def conv_out(a):
        return np.ascontiguousarray(np.asarray(a).transpose(1, 3, 2, 0).reshape(L, 3, 1536))

    def delta_out(a):
        return np.ascontiguousarray(np.asarray(a).transpose(1, 2, 0, 3))

    def kv_out(a):
        return np.ascontiguousarray(np.asarray(a).transpose(1, 0, 2).reshape(L, 128, 2, 64))
    last = R[NCORES - 1]
    conv_p = conv_out(last["convp"])[:, None]
    delta_p = delta_out(last["deltap"])[:, None]
    k_p = kv_out(last["kp"])[:, None]
    v_p = kv_out(last["vp"])[:, None]
    conv_s = np.stack([conv_out(R[c]["convs"]) for c in range(NCORES)], 1)
    delta_s = np.stack([delta_out(R[c]["deltas"]) for c in range(NCORES)], 1)
    k_s = np.stack([kv_out(R[c]["ks"]) for c in range(NCORES)], 1)
    v_s = np.stack([kv_out(R[c]["vs"]) for c in range(NCORES)], 1)
    return (yp, ys, conv_p.astype(f), delta_p.astype(f), k_p.astype(f), v_p.astype(f),
            conv_s.astype(f), delta_s.astype(f), k_s.astype(f), v_s.astype(f))
```

```python
import numpy as np
from contextlib import ExitStack
import concourse.bass as bass
import concourse.mybir as mybir
from concourse.bass_utils import run_bass_kernel_spmd

F32 = mybir.dt.float32
BF16 = mybir.dt.bfloat16
AF = mybir.ActivationFunctionType
ALU = mybir.AluOpType
AX = mybir.AxisListType

NCORES = 8
L = 4
DM = 1024
KC = 8
TP = 2048
TS = 32
NT = TP + TS
DFF = 2816
HC = 22
DIN = 2824
EPS = 1e-6
NB = TP // 64
FT = 416
NFT = NT // FT
QS = 128.0 ** -0.5
NEG = -30000.0


_STOPPED = [False]


class _Eng:
    def __init__(self, name, sem):
        self.name = name
        self.sem = sem
        self.count = 0
        self.waited = {}
        self.prog = []


class _Slot:
    def __init__(self, name, sem):
        self.name = name
        self.sem = sem
        self.count = 0


class Tracker:
    def __init__(self, sems):
        self._sems = list(sems)
        self.eng = {n: _Eng(n, self._sems.pop()) for n in ("pe", "act", "dve", "pool", "sp")}
        self.last_write = {}
        self.readers = {}
        self.slots = {}

    def slot(self, name):
        if name not in self.slots:
            self.slots[name] = _Slot(name, self._sems.pop())
        return self.slots[name]

    def _wait(self, e, tok, same_ok):
        if tok is None:
            return
        if tok[0] == "e":
            _, en, seq = tok
            if en == e.name and same_ok:
                return
            if e.waited.get(en, 0) >= seq:
                return
            e.waited[en] = seq
            e.prog.append(("w", self.eng[en].sem, seq))
        else:
            sl = tok[1]
            key = "slot:" + sl.name
            if e.waited.get(key, 0) >= sl.count:
                return
            e.waited[key] = sl.count
            e.prog.append(("w", sl.sem, sl.count))

    def _deps(self, e, reads, writes):
        for k in reads:
            self._wait(e, self.last_write.get(k), False)
        for k in writes:
            self._wait(e, self.last_write.get(k), True)
            for t in self.readers.get(k, ()):
                self._wait(e, t, True)

    def _record(self, tok, reads, writes):
        for k in writes:
            self.last_write[k] = tok
            self.readers[k] = []
        for k in reads:
            lst = self.readers.setdefault(k, [])
            if tok[0] == "e":
                lst[:] = [t for t in lst if not (t[0] == "e" and t[1] == tok[1])]
            elif tok in lst:
                continue
            lst.append(tok)

    kmap = None

    def _km(self, keys):
        if self.kmap is None:
            return list(keys)
        return [self.kmap(k) for k in keys]

    def op(self, engname, fn, reads=(), writes=()):
        if _STOPPED[0]:
            return
        reads, writes = self._km(reads), self._km(writes)
        e = self.eng[engname]
        if engname in ("act", "dve"):
            writes = list(writes) + [("psl", k[1]) for k in list(reads) + list(writes)
                                     if isinstance(k, tuple) and k[0] == "ps"]
        self._deps(e, reads, writes)
        e.count += 1
        e.prog.append(("o", fn, e.sem, 1))
        self._record(("e", engname, e.count), reads, writes)

    def dma(self, qname, slotname, out, in_, reads=(), writes=(), **kw):
        if _STOPPED[0]:
            return
        reads, writes = self._km(reads), self._km(writes)
        e = self.eng[qname]
        sl = self.slot(slotname)
        self._deps(e, reads, writes)
        e.prog.append(("o", (lambda h, out=out, in_=in_, kw=kw: h.dma_start(out=out, in_=in_, **kw)), sl.sem, 16))
        sl.count += 16
        self._record(("d", sl), reads, writes)

    def custom(self, qname, slotname, fn, inc, reads=(), writes=()):
        if _STOPPED[0]:
            return
        e = self.eng[qname]
        sl = self.slot(slotname)
        self._deps(e, reads, writes)
        e.prog.append(("o", fn, sl.sem, inc))
        sl.count += inc
        self._record(("d", sl), reads, writes)

    def barrier(self):
        for e in self.eng.values():
            for en, o in self.eng.items():
                if o.count > 0 and en != e.name:
                    self._wait(e, ("e", en, o.count), False)
            for sl in self.slots.values():
                if sl.count > 0:
                    self._wait(e, ("d", sl), False)

    def emit(self, block):
        def run(e):
            def body(h):
                for it in e.prog:
                    if it[0] == "w":
                        h.wait_ge(it[1], it[2])
                    else:
                        it[1](h).then_inc(it[2], it[3])
            return body
        block.tensor(run(self.eng["pe"]))
        block.scalar(run(self.eng["act"]))
        block.vector(run(self.eng["dve"]))
        block.gpsimd(run(self.eng["pool"]))
        block.sync(run(self.eng["sp"]))


def _const_layout():
    off = {}
    n = 0
    for name, w in (("ident", 128), ("ones", 128),
                    ("triU64", 64), ("mL64", 64), ("mU64", 64), ("mI64", 64),
                    ("triU32", 64), ("mL32", 64), ("mU32", 64), ("mI32", 64),
                    ("sel64", 128), ("sel32", 128), ("mask0", 192), ("mask1", 192),
                    ("cos", 33 * 8), ("sin", 33 * 8), ("ohp", 8), ("mlt", 8)):
        off[name] = (n, w)
        n += w
    return off, n


COFF, CN = _const_layout()


def _make_consts(core):
    c = np.zeros((128, CN), np.float32)

    def put(name, arr):
        o, w = COFF[name]
        c[:arr.shape[0], o:o + arr.shape[1]] = arr
    put("ident", np.eye(128, dtype=np.float32))
    put("ones", np.ones((128, 128), np.float32))
    for sz, sfx in ((64, "64"), (32, "32")):
        j = np.arange(sz)[:, None]
        i = np.arange(sz)[None, :]
        put("triU" + sfx, (j <= i).astype(np.float32))
        put("mL" + sfx, -(j > i).astype(np.float32))
        put("mU" + sfx, -(j < i).astype(np.float32))
        put("mI" + sfx, (j <= i).astype(np.float32) * QS)
    s64 = np.zeros((64, 128), np.float32); s64[63, :] = 1.0
    s32 = np.zeros((32, 128), np.float32); s32[31, :] = 1.0
    put("sel64", s64)
    put("sel32", s32)
    hv = NEG if core == 0 else 0.0
    m0 = np.zeros((64, 192), np.float32); m0[:, 64:192] = hv
    m1 = np.zeros((64, 192), np.float32); m1[:, 128:192] = hv
    put("mask0", m0)
    put("mask1", m1)
    inv = np.power(np.float32(500000.0), -np.arange(0, 16, 2, dtype=np.float32) / np.float32(16)).astype(np.float32)
    cos = np.zeros((64, 33, 8), np.float32)
    sin = np.zeros((64, 33, 8), np.float32)
    for b in range(33):
        if b < 32:
            pos = (core * TP + b * 64 + np.arange(64)).astype(np.float32)
        else:
            pos = (4096 + np.arange(32)).astype(np.float32)
        ang = pos[:, None] * inv[None, :]
        cos[:len(pos), b] = np.cos(ang)
        sin[:len(pos), b] = np.sin(ang)
    put("cos", cos.reshape(64, -1))
    put("sin", sin.reshape(64, -1))
    ohp = np.zeros((128, 8), np.float32)
    if core > 0:
        ohp[:, core - 1] = 1.0
    mlt = np.zeros((128, 8), np.float32)
    mlt[:, :core] = 1.0
    put("ohp", ohp)
    put("mlt", mlt)
    return c


class _Stop(Exception):
    pass


import os as _os
_KSTOP = int(_os.environ.get("KSTOP", "0"))
_KSTAGE = _os.environ.get("KSTAGE", "all")


def chk(n):
    if _KSTOP == n:
        _STOPPED[0] = True


def build_program(nlayers=L):
    nc = bass.Bass("TRN2", target_bir_lowering=False, dynamic_dma_scratch_size=8192)

    def din(name, shape):
        return nc.dram_tensor(name, list(shape), F32, kind="ExternalInput").ap()

    def dout(name, shape):
        return nc.dram_tensor(name, list(shape), F32, kind="ExternalOutput").ap()

    d_x = din("xT", (128, KC, NT))
    LF = 1 if _KSTAGE == "mix" else L
    d_wfi = din("wfi", (LF * 2, 128, KC, 2 * DFF))
    d_wfo = din("wfo", (LF * 2, 128, HC, DM))
    d_wmi = din("wmi", (L, 128, KC, DIN))
    d_wmo = din("wmo", (L, 128, KC, DM))
    d_nrm = din("nrm", (128, L * 3 + 1, KC))
    d_cw = din("cw", (128, L, 12, 4))
    d_row = din("rowc", (128, L, 144))
    d_cc = din("cconv", (128, L, 12, 3))
    d_st = din("state", (128, L, 4, 128))
    d_ck = din("ck", (128, L, 128))
    d_cv = din("cv", (128, L, 128))
    d_cst = din("cst", (128, CN))
    o_y = dout("yT", (128, KC, NT))
    o_convp = dout("convp", (128, L, 12, 3))
    o_deltap = dout("deltap", (128, L, 4, 128))
    o_kp = dout("kp", (128, L, 128))
    o_vp = dout("vp", (128, L, 128))
    o_convs = dout("convs", (128, L, 12, 3))
    o_deltas = dout("deltas", (128, L, 4, 128))
    o_ks = dout("ks", (128, L, 128))
    o_vs = dout("vs", (128, L, 128))
    XW = 512
    x_in = [nc.dram_tensor(f"x_in{k}", [128, XW], F32) for k in range(3)]
    x_g4 = [nc.dram_tensor(f"x_g4{k}", [4 * 128, XW], F32) for k in range(3)]
    x_g8 = [nc.dram_tensor(f"x_g8{k}", [8 * 128, XW], F32) for k in range(3)]
    sp_o = nc.dram_tensor("sp_o", [NB, 64, 512], F32)
    sp_z = nc.dram_tensor("sp_z", [NB, 128, 256], F32)
    G4 = [[0, 1, 2, 3], [4, 5, 6, 7]]
    G2 = [[0, 4], [1, 5], [2, 6], [3, 7]]

    es = ExitStack()
    with es:
        _uid = [0]

        def sb(name, shape, dt=F32, stack=None):
            _uid[0] += 1
            return (stack or es).enter_context(nc.sbuf_tensor(f"s{_uid[0]}_{name}", list(shape), dt))

        sems = [es.enter_context(nc.semaphore(f"s{i}")) for i in range(60)]
        T = Tracker(sems)
        ps = [es.enter_context(nc.psum_tensor(f"ps{i}", [128, 512], F32)) for i in range(8)]

        def pk(i):
            return ("ps", i)

        xT = sb("xT", (128, KC, NT))
        cst = sb("cst", (128, CN))
        nrm = sb("nrm", (128, L * 3 + 1, KC))
        cwt = sb("cwt", (128, L, 12, 4))
        rowc = sb("rowc", (128, L, 144))
        identb = sb("identb", (128, 128), BF16)
        onesb = sb("onesb", (128, 128), BF16)
        ea = sb("ea", (128, L, 4))

        def C(name, rows=128, c0=0, c1=None):
            o, w = COFF[name]
            c1 = w if c1 is None else c1
            return cst[0:rows, o + c0:o + c1]

        T.dma("sp", "ld0", xT[:], d_x, writes=["xT"])
        T.dma("sp", "ld1", cst[:], d_cst, writes=["cst"])
        T.dma("sp", "ld1", nrm[:], d_nrm, writes=["nrm"])
        T.dma("sp", "ld1", cwt[:], d_cw, writes=["cwt"])
        T.dma("sp", "ld1", rowc[:], d_row, writes=["rowc"])
        T.op("dve", lambda h: h.tensor_copy(out=identb[:], in_=C("ident")), reads=["cst"], writes=["identb"])
        T.op("dve", lambda h: h.tensor_copy(out=onesb[:], in_=C("ones")), reads=["cst"], writes=["onesb"])
        T.op("act", lambda h: h.activation(out=ea[:], in_=rowc[:, :, 0:4], func=AF.Exp), reads=["rowc"], writes=["ea"])

        def PE(mms, reads, writes):
            def fn(h, mms=mms):
                ins = None
                for (o, l, r, st, sp_) in mms:
                    ins = h.matmul(o, lhsT=l, rhs=r, start=st, stop=sp_)
                return ins
            T.op("pe", fn, reads, writes)

        def TR(outs_ins, ident, reads, writes):
            def fn(h, oi=outs_ins, ident=ident):
                ins = None
                for (o, i) in oi:
                    ins = h.transpose(out=o, in_=i, identity=ident)
                return ins
            T.op("pe", fn, reads, writes)

        def ACT(out, in_, func, reads, writes, bias=None, scale=1.0, accum=None):
            def fn(h):
                kw = {}
                if bias is not None:
                    kw["bias"] = bias
                if accum is not None:
                    kw["accum_out"] = accum
                return h.activation(out=out, in_=in_, func=func, scale=scale, **kw)
            T.op("act", fn, reads, writes)

        def TT(out, in0, in1, op, reads, writes, eng="dve"):
            T.op(eng, lambda h: h.tensor_tensor(out=out, in0=in0, in1=in1, op=op), reads, writes)

        def TS_(out, in0, s1, s2, op0, op1, reads, writes):
            if s2 is None:
                T.op("dve", lambda h: h.tensor_scalar(out=out, in0=in0, scalar1=s1, scalar2=None, op0=op0), reads, writes)
            else:
                T.op("dve", lambda h: h.tensor_scalar(out=out, in0=in0, scalar1=s1, scalar2=s2, op0=op0, op1=op1), reads, writes)

        def STT(out, in0, scalar, in1, op0, op1, reads, writes):
            T.op("dve", lambda h: h.scalar_tensor_tensor(out=out, in0=in0, scalar=scalar, in1=in1, op0=op0, op1=op1), reads, writes)

        def CP(out, in_, reads, writes, eng="dve"):
            if eng == "act":
                ACT(out, in_, AF.Copy, reads, writes)
            else:
                T.op("dve", lambda h: h.tensor_scalar(out=out, in0=in_, scalar1=1.0, scalar2=None, op0=ALU.mult), reads, writes)

        def RSQRT(out, in_, scale, eps, reads, writes):
            ACT(out, in_, AF.Ln, reads, writes, bias=eps_ap(eps, out.shape[0]), scale=scale)
            ACT(out, out, AF.Exp, writes, writes, scale=-0.5)

        epst = sb("epst", (128, 3))
        T.op("dve", lambda h: h.memset(epst[:, 0:1], EPS), (), ["epst"])
        T.op("dve", lambda h: h.memset(epst[:, 1:2], 128.0 * EPS), ["epst"], ["epst"])
        T.op("dve", lambda h: h.memset(epst[:, 2:3], 1.0), ["epst"], ["epst"])

        def eps_ap(which, rows=128):
            return epst[0:rows, which:which + 1]

        def ffn(li, widx):
            with ExitStack() as st:
                hT = sb("hT", (128, KC, NT), BF16, st)
                actT = sb("actT", (128, 11, NT), BF16, st)
                sq = sb("sqf", (128, KC, FT), BF16, st)
                rstd = sb("rstdf", (128, FT), F32, st)
                sg = sb("sg", (128, 2, FT), F32, st)
                wgu = [sb(f"wgu{i}", (128, 2, KC, 128), BF16, st) for i in range(4)]
                wo = [sb(f"wo{i}", (128, 11, 128), BF16, st) for i in range(3)]
                T.barrier()
                for t in range(NFT):
                    cs = slice(t * FT, (t + 1) * FT)
                    ACT(sq[:], xT[:, :, cs], AF.Square, ["xT"], ["sqf"])
                    PE([(ps[7][:, 0:FT], onesb[:], sq[:, kc, :], kc == 0, kc == KC - 1) for kc in range(KC)],
                       ["sqf", "onesb"], [pk(7)])
                    RSQRT(rstd[:], ps[7][:, 0:FT], 1.0 / DM, 0, [pk(7), "epst"], ["rstdf"])
                    for kc in range(KC):
                        STT(hT[:, kc, cs], xT[:, kc, cs], nrm[:, widx, kc:kc + 1], rstd[:],
                            ALU.mult, ALU.mult, ["xT", "nrm", "rstdf"], [("hT", t)])
                cnt = {"a": 0, "b": 0, "ga": 0, "gb": 0}

                def loadA(j):
                    s = cnt["a"] % 4
                    cnt["a"] += 1
                    T.dma("pool", f"wgu{s}", wgu[s][:, 0], d_wfi[li, :, :, j * 128:(j + 1) * 128], writes=[("wgu", s)])
                    T.dma("pool", f"wgu{s}", wgu[s][:, 1], d_wfi[li, :, :, DFF + j * 128:DFF + (j + 1) * 128], writes=[("wgu", s)])
                    return s

                def loadB(j0, m):
                    s = cnt["b"] % 3
                    cnt["b"] += 1
                    T.dma("pool", f"wo{s}", wo[s][:], d_wfo[li, :, j0:j0 + 11, m * 128:(m + 1) * 128], writes=[("wo", s)])
                    return s

                for piece in range(2):
                    j0 = piece * 11
                    pend = [loadA(j0 + q) for q in range(3)]
                    for jj in range(11):
                        if jj + 3 < 11:
                            pend.append(loadA(j0 + jj + 3))
                        s = pend.pop(0)
                        for t in range(NFT):
                            cs = slice(t * FT, (t + 1) * FT)
                            n = cnt["ga"] % 2
                            cnt["ga"] += 1
                            bg, bu = 2 * n, 2 * n + 1
                            PE([(ps[bg][:, 0:FT], wgu[s][:, 0, kc, :], hT[:, kc, cs], kc == 0, kc == KC - 1) for kc in range(KC)] +
                               [(ps[bu][:, 0:FT], wgu[s][:, 1, kc, :], hT[:, kc, cs], kc == 0, kc == KC - 1) for kc in range(KC)],
                               [("wgu", s), ("hT", t)], [pk(bg), pk(bu)])
                            ACT(sg[:, n, :], ps[bg][:, 0:FT], AF.Silu, [pk(bg)], [("sg", n)])
                            TT(actT[:, jj, cs], sg[:, n, :], ps[bu][:, 0:FT], ALU.mult, [("sg", n), pk(bu)], [("actT", t)])
                    pendb = [loadB(j0, m) for m in range(2)]
                    for m in range(KC):
                        if m + 2 < KC:
                            pendb.append(loadB(j0, m + 2))
                        s = pendb.pop(0)
                        for t in range(NFT):
                            cs = slice(t * FT, (t + 1) * FT)
                            n = 4 + cnt["gb"] % 3
                            cnt["gb"] += 1
                            PE([(ps[n][:, 0:FT], wo[s][:, q, :], actT[:, q, cs], q == 0, q == 10) for q in range(11)],
                               [("wo", s), ("actT", t)], [pk(n)])
                            STT(xT[:, m, cs], ps[n][:, 0:FT], 0.5, xT[:, m, cs], ALU.mult, ALU.add, [pk(n), "xT"], ["xT"])
                T.barrier()

        def allgather(k):
            T.custom("pool", "cc", lambda h, k=k: h.collective_compute(
                "AllGather", ALU.bypass, replica_groups=G4, ins=[x_in[k].ap().opt()], outs=[x_g4[k].ap().opt()]), 1,
                reads=[f"d:x_in{k}"], writes=[f"d:x_g4{k}"])
            T.custom("pool", "cc", lambda h, k=k: h.collective_compute(
                "AllGather", ALU.bypass, replica_groups=G2, ins=[x_g4[k].ap().opt()], outs=[x_g8[k].ap().opt()]), 1,
                reads=[f"d:x_g4{k}"], writes=[f"d:x_g8{k}"])

        g8v = [x_g8[k].ap().rearrange("(r p) f -> p r f", p=128) for k in range(3)]

        def mixer(li):
            with ExitStack() as st:
                CURQ = [0]
                LOCALKEYS = {"sqb", "rsb", "pre", "cvb", "ctmp", "sq2", "rs2", "qtok", "small", "dGB", "DU", "DL", "t1", "t2",
                             "Pm0", "Pm1", "Qm0", "Qm1", "Rm0", "Rm1", "Mqk", "nMqk", "uw", "kdec", "dE", "AT", "gta", "QpT",
                             "osb", "zs", "oa", "ssmA", "zst"}
                T.kmap = lambda k: (k, CURQ[0]) if (isinstance(k, str) and k in LOCALKEYS) else k
                BKS = [{0: 0, 1: 1, 2: 2, 3: 3, 4: 4, 5: 5}, {0: 0, 1: 1, 2: 2, 3: 3, 4: 4, 5: 5}]

                def PSB(i):
                    return ps[BKS[CURQ[0]][i]]

                def pkA(i):
                    return ("ps", BKS[CURQ[0]][i])

                class _NS:
                    pass

                def mkset(q, stk):
                    S = _NS()
                    S.hb = sb(f"hb{q}", (128, KC, 64), BF16, stk)
                    S.sqb = sb(f"sqb{q}", (128, KC, 64), BF16, stk)
                    S.rsb = sb(f"rsb{q}", (128, 64), F32, stk)
                    S.pre = sb(f"pre{q}", (128, 12, 67), F32, stk)
                    S.cvb = sb(f"cvb{q}", (128, 12, 64), F32, stk)
                    S.ctmp_full = sb(f"ctmp{q}", (128, 1024), F32, stk)
                    S.ctmp = S.ctmp_full[:, 0:768].rearrange("p (f t) -> p f t", f=12)
                    S.rhs_t = S.ctmp_full[0:64, :].rearrange("p (h w) -> p h w", h=4)
                    S.sq2 = sb(f"sq2{q}", (128, 8, 64), BF16, stk)
                    S.rs2 = sb(f"rs2{q}", (128, 8, 64), F32, stk)
                    S.qtok = sb(f"qtok{q}", (64, 4, 128), F32, stk)
                    S.sm_ = sb(f"small{q}", (128, 96), F32, stk)
                    S.dGB = sb(f"dGB{q}", (64, 4, 2, 64), F32, stk)
                    S.DU = sb(f"DU{q}", (64, 4, 64), F32, stk)
                    S.DL = sb(f"DL{q}", (64, 4, 64), F32, stk)
                    S.t1 = sb(f"t1{q}", (64, 4, 64), F32, stk)
                    S.t2 = sb(f"t2{q}", (64, 4, 64), F32, stk)
                    S.Pm = [sb(f"Pm{i}{q}", (64, 4, 64), F32, stk) for i in range(2)]
                    S.Qm = [sb(f"Qm{i}{q}", (64, 4, 64), F32, stk) for i in range(2)]
                    S.Rm = [sb(f"Rm{i}{q}", (64, 4, 64), F32, stk) for i in range(2)]
                    S.Mqk = sb(f"Mqk{q}", (64, 4, 64), F32, stk)
                    S.nMqk = sb(f"nMqk{q}", (64, 4, 64), F32, stk)
                    S.uw = sb(f"uw{q}", (128, 4, 256), F32, stk)
                    S.kdec = sb(f"kdec{q}", (64, 4, 128), F32, stk)
                    S.dE = sb(f"dE{q}", (64, 4, 64), F32, stk)
                    S.AT = sb(f"AT{q}", (128, 4, 128), F32, stk)
                    S.QpT = sb(f"QpT{q}", (128, 4, 64), F32, stk)
                    S.gta = sb(f"gta{q}", (128, 4), F32, stk)
                    return S

                SETS = [mkset(0, st), None]

                def B():
                    return SETS[CURQ[0]]

                hbs = [SETS[0].hb, None]
                Tt = sb("Tt", (128, 4, 256), F32, st)
                Ssm = sb("Ssm", (128, 4, 128), F32, st)
                halo = sb("halo", (128, 12, 3), F32, st)
                osb = sb("osb", (64, 512), F32, st)
                zst = sb("zst", (128, 256), F32, st)
                qb = sb("qb", (64, 10, 64), F32, st)
                rt = sb("rt", (64, 4, 10, 8), F32, st)
                vtok = sb("vtok", (64, 128), F32, st)
                qr = sb("qr", (64, 10, 64), BF16, st)
                Vw = sb("Vw", (64, 3, 2, 2, 128), BF16, st)
                kTw = sb("kTw", (64, 2, 3, 64), BF16, st)
                qT = sb("qT", (64, 8, 64), BF16, st)
                p1 = ExitStack()
                wmi1 = sb("wmi1", (128, KC, 1800), BF16, p1)
                hsel = sb("hsel", (128, 8, 36), F32, p1)
                SETS[1] = mkset(1, p1)
                hbs = [SETS[0].hb, SETS[1].hb]
                BKS[0] = {0: 0, 1: 1, 2: 2, 3: 0, 4: 0, 5: 3}
                BKS[1] = {0: 4, 1: 5, 2: 6, 3: 4, 4: 4, 5: 7}
                WS = {"w": wmi1, "gate": 1536, "kvb": 1544, "z": None, "qb": None}

                def W():
                    return WS["w"]

                T.barrier()
                for kc in range(KC):
                    T.dma("pool", "wmi", wmi1[:, kc, 0:1536], d_wmi[li, :, kc, 0:1536], writes=["wmi"])
                    T.dma("pool", "wmi", wmi1[:, kc, 1536:1544], d_wmi[li, :, kc, 2048:2056], writes=["wmi"])
                    T.dma("pool", "wmi", wmi1[:, kc, 1544:1800], d_wmi[li, :, kc, 2568:2824], writes=["wmi"])
                T.op("dve", lambda h: h.memset(Vw[:], 0.0), (), ["Vw"])

                a_dt = rowc[:, li, 4:8]
                sinks = rowc[:, li, 8:16]
                gnw = rowc[:, li, 16:144]

                def make_h(col0, P, par=0):
                    hb = hbs[par]
                    cs = slice(col0, col0 + P)
                    ACT(B().sqb[:, :, 0:P], xT[:, :, cs], AF.Square, ["xT"], ["sqb"])
                    PE([(PSB(5)[:, 0:P], onesb[:], B().sqb[:, kc, 0:P], kc == 0, kc == KC - 1) for kc in range(KC)],
                       ["sqb", "onesb"], [pkA(5)])
                    RSQRT(B().rsb[:, 0:P], PSB(5)[:, 0:P], 1.0 / DM, 0, [pkA(5), "epst"], ["rsb"])
                    for kc in range(KC):
                        STT(hb[:, kc, 0:P], xT[:, kc, cs], nrm[:, li * 3 + 1, kc:kc + 1], B().rsb[:, 0:P],
                            ALU.mult, ALU.mult, ["xT", "nrm", "rsb"], [("hb", par)])

                def proj_fm(P, ncols_from, nchunks, banks, par=0):
                    hb = hbs[par]
                    for g0 in range(0, nchunks, 4):
                        bnk = banks[g0 // 4]
                        mms = []
                        for f in range(g0, min(g0 + 4, nchunks)):
                            c0 = ncols_from + f * 128
                            for kc in range(KC):
                                mms.append((PSB(bnk)[:, (f - g0) * 128:(f - g0) * 128 + P], W()[:, kc, c0:c0 + 128],
                                            hb[:, kc, 0:P], kc == 0, kc == KC - 1))
                        PE(mms, ["wmi", ("hb", par)], [pkA(bnk)])

                def gdn_pre(P, sfx, par=0):
                    hb = hbs[par]
                    proj_fm(P, 0, 12, [0, 1, 2], par)
                    yield
                    for g0 in range(3):
                        CP(B().pre[:, g0 * 4:(g0 + 1) * 4, 3:3 + P],
                           PSB(g0)[:, 0:512].rearrange("p (f t) -> p f t", f=4)[:, :, 0:P], [pkA(g0)], ["pre"], eng="act")
                    CP(B().pre[:, :, 0:3], halo[:], ["halo"], ["pre"])
                    for j in range(4):
                        dst = B().cvb if j == 0 else B().ctmp
                        TT(dst[:, :, 0:P], B().pre[:, :, j:j + P], cwt[:, li, :, j:j + 1].broadcast_to([128, 12, P]), ALU.mult,
                           ["pre", "cwt"], ["cvb" if j == 0 else "ctmp"])
                        if j > 0:
                            TT(B().cvb[:, :, 0:P], B().cvb[:, :, 0:P], B().ctmp[:, :, 0:P], ALU.add, ["cvb", "ctmp"], ["cvb"])
                    chk(41)
                    yield
                    CP(halo[:], B().pre[:, :, P:P + 3], ["pre"], ["halo"])
                    yield "halo"
                    ACT(B().ctmp[:, :, 0:P], B().cvb[:, :, 0:P], AF.Exp, ["cvb"], ["ctmp"], scale=-1.0)
                    ACT(B().ctmp[:, :, 0:P], B().ctmp[:, :, 0:P], AF.Ln, ["ctmp", "epst"], ["ctmp"], bias=epst[:, 2:3])
                    ACT(B().ctmp[:, :, 0:P], B().ctmp[:, :, 0:P], AF.Exp, ["ctmp"], ["ctmp"], scale=-1.0)
                    TT(B().cvb[:, :, 0:P], B().cvb[:, :, 0:P], B().ctmp[:, :, 0:P], ALU.mult, ["cvb", "ctmp"], ["cvb"])
                    ACT(B().sq2[:, :, 0:P], B().cvb[:, 0:8, 0:P], AF.Square, ["cvb"], ["sq2"])
                    if P == 64:
                        PE([(PSB(3)[:, 0:512], onesb[:], B().sq2[:].rearrange("p f t -> p (f t)"), True, True)], ["sq2", "onesb"], [pkA(3)])
                    else:
                        PE([(PSB(3)[:, f * 64:f * 64 + P], onesb[:], B().sq2[:, f, 0:P], True, True) for f in range(8)],
                           ["sq2", "onesb"], [pkA(3)])
                    RSQRT(B().rs2[:, :, 0:P], PSB(3)[:, 0:512].rearrange("p (f t) -> p f t", f=8)[:, :, 0:P], 1.0, 0,
                          [pkA(3), "epst"], ["rs2"])
                    TT(B().cvb[:, 0:8, 0:P], B().cvb[:, 0:8, 0:P], B().rs2[:, :, 0:P], ALU.mult, ["cvb", "rs2"], ["cvb"])
                    chk(42)
                    yield
                    PE([(PSB(4)[0:P, 0:8], hb[:, kc, 0:P], W()[:, kc, WS["gate"]:WS["gate"] + 8], kc == 0, kc == KC - 1) for kc in range(KC)],
                       [("hb", par), "wmi"], [pkA(4)])
                    beta = B().sm_[0:P, 0:4]
                    gg = B().sm_[0:P, 4:8]
                    Gc = B().sm_[0:P, 8:12]
                    Gl = B().sm_[0:P, 12:16]
                    ex3 = B().sm_[0:P, 16:28]
                    bG = B().sm_[0:P, 28:32]
                    ACT(beta, PSB(4)[0:P, 0:4], AF.Exp, [pkA(4)], ["small"], scale=-1.0)
                    ACT(beta, beta, AF.Ln, ["small", "epst"], ["small"], bias=epst[0:P, 2:3])
                    ACT(beta, beta, AF.Exp, ["small"], ["small"], scale=-1.0)
                    TT(gg, PSB(4)[0:P, 4:8], a_dt[0:P], ALU.add, [pkA(4), "rowc"], ["small"])
                    ACT(gg, gg, AF.Exp, ["small"], ["small"])
                    ACT(gg, gg, AF.Ln, ["small", "epst"], ["small"], bias=epst[0:P, 2:3])
                    STT(gg, gg, -1.0, ea[0:P, li, :], ALU.mult, ALU.mult, ["small", "ea"], ["small"])
                    PE([(PSB(4)[0:P, 16:20], C("triU" + sfx, P, 0, P), gg, True, True),
                        (PSB(4)[0:P, 20:24], C("ones", P, 0, P), gg, True, True),
                        (PSB(4)[:, 24:28], C("ones", P, 0, 128), gg, True, True)],
                       ["small", "cst"], [pkA(4)])
                    CP(B().sm_[0:P, 8:16], PSB(4)[0:P, 16:24], [pkA(4)], ["small"])
                    CP(B().sm_[0:P, 32:36], Gc, ["small"], ["small"])
                    TT(B().sm_[0:P, 36:40], Gl, Gc, ALU.subtract, ["small"], ["small"])
                    CP(B().sm_[0:P, 40:44], Gl, ["small"], ["small"])
                    ACT(ex3, B().sm_[0:P, 32:44], AF.Exp, ["small"], ["small"])
                    ACT(B().gta[:], PSB(4)[:, 24:28], AF.Exp, [pkA(4)], ["gta"])
                    eG = B().sm_[0:P, 16:20]
                    eGlG = B().sm_[0:P, 20:24]
                    TT(bG, beta, eG, ALU.mult, ["small"], ["small"])
                    chk(43)
                    yield
                    idf = C("ident")
                    TR([(PSB(0)[0:P, f * 128:(f + 1) * 128], B().cvb[:, f, 0:P]) for f in range(4)], idf, ["cvb", "cst", "pre"], [pkA(0)])
                    TR([(PSB(1)[0:P, f * 128:(f + 1) * 128], B().cvb[:, 4 + f, 0:P]) for f in range(4)], idf, ["cvb", "cst"], [pkA(1)])
                    TR([(PSB(2)[0:P, f * 128:(f + 1) * 128], B().cvb[:, 8 + f, 0:P]) for f in range(4)], idf, ["cvb", "cst"], [pkA(2)])
                    CP(B().qtok[0:P], PSB(0)[0:P, :].rearrange("p (h d) -> p h d", h=4), [pkA(0)], ["qtok"], eng="act")
                    k3 = PSB(1)[0:P, :].rearrange("p (h d) -> p h d", h=4)
                    v3 = PSB(2)[0:P, :].rearrange("p (h d) -> p h d", h=4)
                    TT(B().rhs_t[0:P, :, 128:256], k3, bG.unsqueeze(2).broadcast_to([P, 4, 128]), ALU.mult, [pkA(1), "small"], ["ctmp"])
                    TT(B().kdec[0:P], k3, eGlG.unsqueeze(2).broadcast_to([P, 4, 128]), ALU.mult, [pkA(1), "small"], ["kdec"])
                    TT(B().rhs_t[0:P, :, 0:128], v3, beta.unsqueeze(2).broadcast_to([P, 4, 128]), ALU.mult, [pkA(2), "small"], ["ctmp"])
                    chk(44)
                    yield
                    for hh in range(4):
                        TS_(B().dGB[0:P, hh, 0, 0:P], C("ident", P, 0, P), Gc[:, hh:hh + 1], None, ALU.mult, None, ["cst", "small"], ["dGB"])
                        TS_(B().dGB[0:P, hh, 1, 0:P], C("ident", P, 0, P), beta[:, hh:hh + 1], None, ALU.mult, None, ["cst", "small"], ["dGB"])
                    if P == 64:
                        PE([(PSB(5)[0:P, 0:512], C("ones", P, 0, P), B().dGB[:].rearrange("p h w t -> p (h w t)"), True, True)],
                           ["dGB", "cst", "qtok"], [pkA(5)])
                    else:
                        PE([(PSB(5)[0:P, hh * 128 + w * 64: hh * 128 + w * 64 + P], C("ones", P, 0, P), B().dGB[0:P, hh, w, 0:P], True, True)
                            for hh in range(4) for w in range(2)], ["dGB", "cst", "qtok"], [pkA(5)])
                    GB = PSB(5)[0:P, :].rearrange("p (h w t) -> p h w t", h=4, w=2)
                    Grow = GB[:, :, 0, 0:P]
                    brow = GB[:, :, 1, 0:P]
                    PE([(PSB(3)[0:P, hh * 64:hh * 64 + P], B().cvb[:, 4 + hh, 0:P], B().cvb[:, 4 + hh, 0:P], True, True) for hh in range(4)] +
                       [(PSB(3)[0:P, 256 + hh * 64:256 + hh * 64 + P], B().cvb[:, 4 + hh, 0:P], B().cvb[:, hh, 0:P], True, True) for hh in range(4)],
                       ["cvb", "kdec", "ctmp", "rs2"], [pkA(3)])
                    yield
                    kk = PSB(3)[0:P, 0:256].rearrange("p (h t) -> p h t", h=4)[:, :, 0:P]
                    qkT = PSB(3)[0:P, 256:512].rearrange("p (h t) -> p h t", h=4)[:, :, 0:P]
                    Gb = Gc.unsqueeze(2).broadcast_to([P, 4, P])
                    TT(B().t1[0:P, :, 0:P], Grow, Gb, ALU.subtract, [pkA(5), "small"], ["t1"])
                    TS_(B().DU[0:P, :, 0:P], B().t1[0:P, :, 0:P], 0.0, None, ALU.min, None, ["t1"], ["DU"])
                    TS_(B().DL[0:P, :, 0:P], B().t1[0:P, :, 0:P], 0.0, None, ALU.max, None, ["t1"], ["DL"])
                    ACT(B().DU[0:P, :, 0:P], B().DU[0:P, :, 0:P], AF.Exp, ["DU"], ["DU"])
                    ACT(B().DL[0:P, :, 0:P], B().DL[0:P, :, 0:P], AF.Exp, ["DL"], ["DL"], scale=-1.0)
                    yield
                    mU = C("mU" + sfx, P, 0, P).unsqueeze(1).broadcast_to([P, 4, P])
                    mL = C("mL" + sfx, P, 0, P).unsqueeze(1).broadcast_to([P, 4, P])
                    mI = C("mI" + sfx, P, 0, P).unsqueeze(1).broadcast_to([P, 4, P])
                    TT(B().t1[0:P, :, 0:P], kk, B().DU[0:P, :, 0:P], ALU.mult, [pkA(3), "DU"], ["t1"])
                    TT(B().t2[0:P, :, 0:P], brow, mU, ALU.mult, [pkA(5), "cst"], ["t2"])
                    TT(B().Pm[0][0:P, :, 0:P], B().t1[0:P, :, 0:P], B().t2[0:P, :, 0:P], ALU.mult, ["t1", "t2"], ["Pm0"])
                    TT(B().t1[0:P, :, 0:P], kk, B().DL[0:P, :, 0:P], ALU.mult, [pkA(3), "DL"], ["t1"])
                    TT(B().t2[0:P, :, 0:P], mL, beta.unsqueeze(2).broadcast_to([P, 4, P]), ALU.mult, ["cst", "small"], ["t2"])
                    TT(B().Qm[0][0:P, :, 0:P], B().t1[0:P, :, 0:P], B().t2[0:P, :, 0:P], ALU.mult, ["t1", "t2"], ["Qm0"])
                    TT(B().t1[0:P, :, 0:P], qkT, B().DU[0:P, :, 0:P], ALU.mult, [pkA(3), "DU"], ["t1"])
                    TT(B().Mqk[0:P, :, 0:P], B().t1[0:P, :, 0:P], mI, ALU.mult, ["t1", "cst"], ["Mqk"])
                    TS_(B().nMqk[0:P, :, 0:P], B().Mqk[0:P, :, 0:P], -1.0, None, ALU.mult, None, ["Mqk"], ["nMqk"])
                    for hh in range(4):
                        TS_(B().dE[0:P, hh, 0:P], C("ident", P, 0, P), eG[:, hh:hh + 1], QS, ALU.mult, ALU.mult, ["cst", "small"], ["dE"])
                    chk(45)
                    TT(B().Rm[0][0:P, :, 0:P], B().Pm[0][0:P, :, 0:P], C("ident", P, 0, P).unsqueeze(1).broadcast_to([P, 4, P]), ALU.add,
                       ["Pm0", "cst"], ["Rm0"])
                    nr = 5 if P == 64 else 4
                    cur = 0
                    rc = 0
                    yield
                    for r in range(nr + 1):
                        nx = 1 - cur
                        sq_ = r < nr
                        upd = r >= 1
                        mms = []
                        if sq_:
                            if r < nr - 1:
                                mms += [(PSB(5)[0:P, hh * 64:hh * 64 + P], B().Qm[cur][0:P, hh, 0:P], B().Pm[cur][0:P, hh, 0:P], True, True) for hh in range(4)]
                            mms += [(PSB(5)[0:P, 256 + hh * 64:256 + hh * 64 + P], B().Pm[cur][0:P, hh, 0:P], B().Qm[cur][0:P, hh, 0:P], True, True) for hh in range(4)]
                        wr = [pkA(5)] if sq_ else []
                        if upd:
                            mms += [(PSB(4)[0:P, hh * 64:hh * 64 + P], B().Qm[cur][0:P, hh, 0:P], B().Rm[rc][0:P, hh, 0:P], True, True) for hh in range(4)]
                            wr.append(pkA(4))
                        PE(mms, [f"Pm{cur}", f"Qm{cur}", f"Rm{rc}", "t1", "t2", "Mqk", "gta", "small"], wr)
                        pq = PSB(5)[0:P, :].rearrange("p (w h t) -> p w h t", w=2, h=4)
                        if sq_:
                            if r < nr - 1:
                                CP(B().Pm[nx][0:P, :, 0:P], pq[:, 0, :, 0:P], [pkA(5)], [f"Pm{nx}"], eng="act")
                            CP(B().Qm[nx][0:P, :, 0:P], pq[:, 1, :, 0:P], [pkA(5)], [f"Qm{nx}"], eng="act")
                        if upd:
                            TT(B().Rm[1 - rc][0:P, :, 0:P], B().Rm[rc][0:P, :, 0:P],
                               PSB(4)[0:P, 0:256].rearrange("p (h t) -> p h t", h=4)[:, :, 0:P], ALU.add, [f"Rm{rc}", pkA(4)], [f"Rm{1 - rc}"])
                            rc = 1 - rc
                        cur = nx
                        yield
                    cur = rc
                    chk(46)
                    PE([(PSB((0, 1)[hh // 2])[0:P, (hh % 2) * 256:(hh % 2) * 256 + 256], B().Rm[cur][0:P, hh, 0:P], B().rhs_t[0:P, hh, :], True, True)
                        for hh in range(4)], [f"Rm{cur}", "ctmp"], [pkA(0), pkA(1)])
                    CP(B().uw[0:P, 0:2, :], PSB(0)[0:P, :].rearrange("p (h w) -> p h w", h=2), [pkA(0)], ["uw"], eng="act")
                    CP(B().uw[0:P, 2:4, :], PSB(1)[0:P, :].rearrange("p (h w) -> p h w", h=2), [pkA(1)], ["uw"])
                    yield
                    PE([(PSB(2)[:, hh * 128:(hh + 1) * 128], B().uw[0:P, hh, 128:256], B().kdec[0:P, hh, :], True, True) for hh in range(4)],
                       ["uw", "kdec"], [pkA(2)])
                    for hh in range(4):
                        STT(B().AT[:, hh, :], C("ident"), B().gta[:, hh:hh + 1], PSB(2)[:, hh * 128:(hh + 1) * 128], ALU.mult, ALU.subtract,
                            ["cst", "gta", pkA(2)], ["AT"])

                def chain_step(P, state, c0, width):
                    mms = []
                    for hh in range(4):
                        bnk = (0, 1)[hh // 2]
                        o0 = (hh % 2) * 256
                        mms.append((PSB(bnk)[:, o0:o0 + width], B().AT[:, hh, :], state[:, hh, c0:c0 + width], True, False))
                        mms.append((PSB(bnk)[:, o0 + width - 128:o0 + width], B().kdec[0:P, hh, :], B().uw[0:P, hh, 0:128], False, True))
                    PE(mms, ["AT", "state", "kdec", "uw"], [pkA(0), pkA(1)])
                    yield
                    for hp in range(2):
                        CP(state[:, 2 * hp:2 * hp + 2, c0:c0 + width],
                           PSB(hp)[:, :].rearrange("p (h w) -> p h w", h=2)[:, :, 0:width], [pkA(hp)], ["state"],
                           eng=("act" if hp == 0 else "dve"))

                def gdn_qp(P):
                    PE([m for hh in range(4) for m in (
                        (PSB(3)[:, hh * 64:hh * 64 + P], B().qtok[0:P, hh, :], B().dE[0:P, hh, 0:P], True, False),
                        (PSB(3)[:, hh * 64:hh * 64 + P], B().uw[0:P, hh, 128:256], B().nMqk[0:P, hh, 0:P], False, True))],
                       ["qtok", "dE", "uw", "nMqk"], [pkA(3)])
                    yield
                    CP(B().QpT[:, :, 0:P], PSB(3)[:, 0:256].rearrange("p (h t) -> p h t", h=4)[:, :, 0:P], [pkA(3)], ["QpT"], eng="act")

                def gdn_fin(P, par, mv=None, mk=None):
                    mixT = mixTs[par] if mv is None else mv
                    mk = ("mixT", par) if mk is None else mk
                    for hh in range(4):
                        ACT(B().oa[0:P, hh * 128:(hh + 1) * 128], B().osb[0:P, hh * 128:(hh + 1) * 128], AF.Square, ["osb"], ["oa", "ssmA"],
                            accum=B().ssmA[0:P, hh:hh + 1])
                    yield
                    RSQRT(B().ssmA[0:P, 4:8], B().ssmA[0:P, 0:4], 1.0 / 128, 0, ["ssmA", "epst"], ["ssmA"])
                    o3 = B().osb[0:P, :].rearrange("p (h d) -> p h d", h=4)
                    TT(o3, o3, B().ssmA[0:P, 4:8].unsqueeze(2).broadcast_to([P, 4, 128]), ALU.mult, ["osb", "ssmA"], ["osb"])
                    TT(o3, o3, gnw[0:P].unsqueeze(1).broadcast_to([P, 4, 128]), ALU.mult, ["osb", "rowc"], ["osb"])
                    TT(B().oa[0:P, :], B().osb[0:P, :], B().zs[0:P, :], ALU.mult, ["osb", "zs"], ["oa"])
                    TRb([(pbf(BKS[CURQ[0]][2])[:, hh * 64:hh * 64 + P], B().oa[0:P, hh * 128:(hh + 1) * 128]) for hh in range(4)],
                        ["oa", "identb"], [pkA(2)])
                    yield
                    CP(mixT[:, 0:4, 0:P], pbf(BKS[CURQ[0]][2])[:, 0:256].rearrange("p (h t) -> p h t", h=4)[:, :, 0:P], [pkA(2)], [mk], eng="act")

                def zproj(P, par):
                    hb = hbs[par]
                    PE([(PSB(5)[0:P, :], hb[:, kc, 0:P], W()[:, kc, WS["z"]:WS["z"] + 512], kc == 0, kc == KC - 1) for kc in range(KC)],
                       [("hb", par), "wmi"], [pkA(5)])
                    yield
                    ACT(B().zs[0:P, :], PSB(5)[0:P, :], AF.Exp, [pkA(5)], ["zs"], scale=-1.0)
                    ACT(B().zs[0:P, :], B().zs[0:P, :], AF.Ln, ["zs", "epst"], ["zs"], bias=epst[0:P, 2:3])
                    ACT(B().zs[0:P, :], B().zs[0:P, :], AF.Exp, ["zs"], ["zs"], scale=-1.0)
                    TT(B().zs[0:P, :], B().zs[0:P, :], PSB(5)[0:P, :], ALU.mult, ["zs", pkA(5)], ["zs"])

                def gdn_out(P, state, c0, col0, par=0):
                    yield from gdn_qp(P)
                    PE([m for hh in range(4) for m in (
                        (PSB(4)[0:P, hh * 128:(hh + 1) * 128], B().QpT[:, hh, 0:P], state[:, hh, c0:c0 + 128], True, False),
                        (PSB(4)[0:P, hh * 128:(hh + 1) * 128], B().Mqk[0:P, hh, 0:P], B().uw[0:P, hh, 0:128], False, True))],
                       ["QpT", "state", "Mqk", "uw"], [pkA(4)])
                    yield from zproj(P, par)
                    CP(B().osb[0:P, :], PSB(4)[0:P, :], [pkA(4)], ["osb"])
                    yield from gdn_fin(P, par)

                def gdn_loc(P, b):
                    yield from gdn_qp(P)
                    PE([(PSB(5)[:, hh * 64:hh * 64 + P], Tt[:, hh, 0:128], B().QpT[:, hh, 0:P], True, True) for hh in range(4)] +
                       [m for hh in range(4) for m in (
                        (PSB(4)[0:P, hh * 128:(hh + 1) * 128], B().QpT[:, hh, 0:P], Tt[:, hh, 128:256], True, False),
                        (PSB(4)[0:P, hh * 128:(hh + 1) * 128], B().Mqk[0:P, hh, 0:P], B().uw[0:P, hh, 0:128], False, True))],
                       ["QpT", "state", "Mqk", "uw"], [pkA(5), pkA(4)])
                    yield
                    CP(zst[:], PSB(5)[:, 0:256], [pkA(5)], ["zstS"], eng="act")
                    CP(osb[0:P, :], PSB(4)[0:P, :], [pkA(4)], ["osbS"])
                    T.dma("sp", "spill", sp_z.ap()[b], zst[:], reads=["zstS"], writes=[("d:spz", b)])
                    T.dma("sp", "spill", sp_o.ap()[b], osb[0:P, :], reads=["osbS"], writes=[("d:spo", b)])

                def gdn_corr(P, b, par, mv=None, mk=None):
                    T.dma("sp", "fill", B().zst[:], sp_z.ap()[b], reads=[("d:spz", b)], writes=["zst"])
                    T.dma("sp", "fill", B().osb[0:P, :], sp_o.ap()[b], reads=[("d:spo", b)], writes=["osb"])
                    yield from zproj(P, par)
                    PE([(PSB(4)[0:P, hh * 128:(hh + 1) * 128], B().zst[:, hh * 64:hh * 64 + P], Tt[:, hh, 128:256], True, True) for hh in range(4)],
                       ["zst", "state"], [pkA(4)])
                    yield
                    TT(B().osb[0:P, :], B().osb[0:P, :], PSB(4)[0:P, :], ALU.add, ["osb", pkA(4)], ["osb"])
                    yield from gdn_fin(P, par, mv, mk)

                def pbf(i):
                    return psb16[i]

                def TRb(oi, reads, writes):
                    def fn(h, oi=oi):
                        ins = None
                        for (o, i) in oi:
                            ins = h.transpose(out=o, in_=i, identity=identb[0:i.shape[0], 0:i.shape[0]])
                        return ins
                    T.op("pe", fn, reads, writes)

                def swa_proj(P, bidx, slot, par=0, kout=None, vout=None, kv_only=False):
                    hb = hbs[par]
                    if kv_only:
                        PE([(ps[7][0:P, 0:256], hb[:, kc, 0:P], W()[:, kc, WS["kvb"]:WS["kvb"] + 256], kc == 0, kc == KC - 1) for kc in range(KC)],
                           [("hb", par), "wmi"], [pk(7)])
                        yield
                        CP(qb[0:P, 8:10, :], ps[7][0:P, 0:128].rearrange("p (h d) -> p h d", h=2), [pk(7)], ["qb"], eng="act")
                        CP(vtok[0:P, :], ps[7][0:P, 128:256], [pk(7)], ["vtok"], eng="act")
                        co, _ = COFF["cos"]
                        so, _ = COFF["sin"]
                        cosb = cst[0:P, co + bidx * 8:co + bidx * 8 + 8].unsqueeze(1).broadcast_to([P, 2, 8])
                        sinb = cst[0:P, so + bidx * 8:so + bidx * 8 + 8].unsqueeze(1).broadcast_to([P, 2, 8])
                        x1 = qb[0:P, 8:10, 0:8]
                        x2 = qb[0:P, 8:10, 8:16]
                        TT(rt[0:P, 0, 8:10], x1, cosb, ALU.mult, ["qb", "cst"], ["rt"])
                        TT(rt[0:P, 1, 8:10], x2, sinb, ALU.mult, ["qb", "cst"], ["rt"])
                        TT(rt[0:P, 2, 8:10], x2, cosb, ALU.mult, ["qb", "cst"], ["rt"])
                        TT(rt[0:P, 3, 8:10], x1, sinb, ALU.mult, ["qb", "cst"], ["rt"])
                        TT(x1, rt[0:P, 0, 8:10], rt[0:P, 1, 8:10], ALU.subtract, ["rt"], ["qb"])
                        TT(x2, rt[0:P, 2, 8:10], rt[0:P, 3, 8:10], ALU.add, ["rt"], ["qb"])
                        return
                    PE([(ps[6][0:P, :], hb[:, kc, 0:P], W()[:, kc, WS["qb"]:WS["qb"] + 512], kc == 0, kc == KC - 1) for kc in range(KC)] +
                       [(ps[7][0:P, 0:256], hb[:, kc, 0:P], W()[:, kc, WS["kvb"]:WS["kvb"] + 256], kc == 0, kc == KC - 1) for kc in range(KC)],
                       [("hb", par), "wmi"], [pk(6), pk(7)])
                    yield
                    CP(qb[0:P, 0:8, :], ps[6][0:P, :].rearrange("p (h d) -> p h d", h=8), [pk(6)], ["qb"], eng="act")
                    CP(qb[0:P, 8:10, :], ps[7][0:P, 0:128].rearrange("p (h d) -> p h d", h=2), [pk(7)], ["qb"], eng="act")
                    CP(vtok[0:P, :], ps[7][0:P, 128:256], [pk(7)], ["vtok"], eng="act")
                    yield
                    co, _ = COFF["cos"]
                    so, _ = COFF["sin"]
                    cosb = cst[0:P, co + bidx * 8:co + bidx * 8 + 8].unsqueeze(1).broadcast_to([P, 10, 8])
                    sinb = cst[0:P, so + bidx * 8:so + bidx * 8 + 8].unsqueeze(1).broadcast_to([P, 10, 8])
                    x1 = qb[0:P, :, 0:8]
                    x2 = qb[0:P, :, 8:16]
                    TT(rt[0:P, 0], x1, cosb, ALU.mult, ["qb", "cst"], ["rt"])
                    TT(rt[0:P, 1], x2, sinb, ALU.mult, ["qb", "cst"], ["rt"])
                    TT(rt[0:P, 2], x2, cosb, ALU.mult, ["qb", "cst"], ["rt"])
                    TT(rt[0:P, 3], x1, sinb, ALU.mult, ["qb", "cst"], ["rt"])
                    TT(x1, rt[0:P, 0], rt[0:P, 1], ALU.subtract, ["rt"], ["qb"])
                    TT(x2, rt[0:P, 2], rt[0:P, 3], ALU.add, ["rt"], ["qb"])
                    CP(qr[0:P], qb[0:P], ["qb"], ["qr"])
                    for g in range(2):
                        CP(Vw[0:P, slot, g, 0, 0:64], vtok[0:P, g * 64:(g + 1) * 64], ["vtok"], ["Vw"])
                        CP(Vw[0:P, slot, g, 1, 64:128], vtok[0:P, g * 64:(g + 1) * 64], ["vtok"], ["Vw"])
                    yield
                    TRb([(pbf(6)[0:64, hh * 64:hh * 64 + P], qr[0:P, hh, :]) for hh in range(10)], ["qr", "identb"], [pk(6)])
                    yield
                    CP(qT[:, :, 0:P], pbf(6)[0:64, 0:512].rearrange("p (h t) -> p h t", h=8)[:, :, 0:P], [pk(6)], ["qT"], eng="act")
                    CP(kTw[:, :, slot, 0:P], pbf(6)[0:64, 512:640].rearrange("p (h t) -> p h t", h=2)[:, :, 0:P], [pk(6)], ["kTw"], eng="act")
                    if kout is not None:
                        T.dma("sp", "outk", kout, qb[0:P, 8:10, :].rearrange("p h d -> p (h d)"), reads=["qb"])
                        T.dma("sp", "outk", vout, vtok[0:P, :], reads=["vtok"])

                def swa_attn(P, nkeys, maskname, par=0, mv=None, mk=None):
                    mixT = mixTs[par] if mv is None else mv
                    mk = ("mixT", par) if mk is None else mk
                    NK = 192
                    for hg in range(2):
                        PE([(ps[6 + (hq // 2)][0:P, (hq % 2) * 192 + s * 64: (hq % 2) * 192 + s * 64 + nkeys[s]], qT[:, hg * 4 + hq, 0:P],
                             kTw[:, hg, s, 0:nkeys[s]], True, True) for hq in range(4) for s in range(3)],
                           ["qT", "kTw"], [pk(6), pk(7)])
                        yield
                        for hp in range(2):
                            src = ps[6 + hp][0:P, 0:384].rearrange("p (h k) -> p h k", h=2)
                            if maskname is None:
                                nkw = 128 + nkeys[2]
                                TS_(sco[0:P, 2 * hp:2 * hp + 2, 0:nkw], src[:, :, 0:nkw], 0.125, None, ALU.mult, None, [pk(6 + hp)], ["sco"])
                            else:
                                STT(sco[0:P, 2 * hp:2 * hp + 2, :], src, 0.125,
                                    C(maskname, P).unsqueeze(1).broadcast_to([P, 2, 192]), ALU.mult, ALU.add,
                                    [pk(6 + hp), "cst"], ["sco"])
                        if nkeys[2] < 64 or nkeys[1] < 64 or nkeys[0] < 64:
                            for s in range(3):
                                if nkeys[s] < 64:
                                    T.op("dve", lambda h, s=s: h.memset(sco[0:P, :, s * 64 + nkeys[s]:(s + 1) * 64], NEG), ["sco"], ["sco"])
                        mx = ssm[0:P, 8:12]
                        T.op("dve", lambda h: h.tensor_reduce(out=mx, in_=sco[0:P], axis=AX.X, op=ALU.max), ["sco"], ["ssm"])
                        TT(mx, mx, sinks[0:P, hg * 4:hg * 4 + 4], ALU.max, ["ssm", "rowc"], ["ssm"])
                        TS_(ssm[0:P, 12:16], mx, -1.0, None, ALU.mult, None, ["ssm"], ["ssm"])
                        yield
                        for hq in range(4):
                            ACT(sco[0:P, hq, :], sco[0:P, hq, :], AF.Exp, ["sco", "ssm"], ["sco", "ssm2"],
                                bias=ssm[0:P, 12 + hq:13 + hq], accum=ssm[0:P, 16 + hq:17 + hq])
                        yield
                        TT(ssm[0:P, 20:24], sinks[0:P, hg * 4:hg * 4 + 4], mx, ALU.subtract, ["ssm", "rowc"], ["ssm"])
                        ACT(ssm[0:P, 20:24], ssm[0:P, 20:24], AF.Exp, ["ssm"], ["ssm"])
                        TT(ssm[0:P, 24:28], ssm[0:P, 20:24], ssm[0:P, 16:20], ALU.add, ["ssm", "ssm2"], ["ssm"])
                        T.op("dve", lambda h: h.reciprocal(out=ssm[0:P, 24:28], in_=ssm[0:P, 24:28]), ["ssm"], ["ssm"])
                        TT(pn[0:P], sco[0:P], ssm[0:P, 24:28].unsqueeze(2).broadcast_to([P, 4, 192]), ALU.mult, ["sco", "ssm"], ["pn"])
                        yield
                        TRb([(pbf(6)[0:nkeys[s], (hq * 3 + s) * 64:(hq * 3 + s) * 64 + P], pn[0:P, hq, s * 64:s * 64 + nkeys[s]])
                             for hq in range(4) for s in range(3)], ["pn", "identb", "qT", "kTw"], [pk(6)])
                        yield
                        CP(pT[:, :, :, 0:P], pbf(6)[0:64, 0:768].rearrange("p (h s t) -> p h s t", h=4, s=3)[:, :, :, 0:P], [pk(6)], ["pT"],
                           eng="act")
                        yield
                        mms = []
                        for pr in range(2):
                            k = 0
                            for par_ in range(2):
                                for s in range(3):
                                    mms.append((ps[7][:, pr * 64:pr * 64 + P], Vw[0:nkeys[s], s, hg, par_, :],
                                                pT[0:nkeys[s], pr * 2 + par_, s, 0:P], k == 0, k == 5))
                                    k += 1
                        PE(mms, ["Vw", "pT"], [pk(7)])
                        yield
                        CP(mixT[:, 4 + hg * 2:6 + hg * 2, 0:P], ps[7][:, 0:128].rearrange("p (h t) -> p h t", h=2)[:, :, 0:P], [pk(7)], [mk],
                           eng="act")

                def out_proj(P, col0, par=0, mv=None, mk=None):
                    mixT = mixTs[par] if mv is None else mv
                    mk = ("mixT", par) if mk is None else mk
                    sw = max(64, P)
                    for mh in range(2):
                        PE([(ps[6 + mh][:, mm * sw:mm * sw + P], wmo[:, kc, (mh * 4 + mm) * 128:(mh * 4 + mm + 1) * 128], mixT[:, kc, 0:P],
                             kc == 0, kc == KC - 1) for mm in range(4) for kc in range(KC)], ["wmo", mk], [pk(6 + mh)])
                        yield
                        TT(xT[:, mh * 4:mh * 4 + 4, col0:col0 + P], xT[:, mh * 4:mh * 4 + 4, col0:col0 + P],
                           ps[6 + mh][:, 0:4 * sw].rearrange("p (m t) -> p m t", m=4)[:, :, 0:P], ALU.add, ["xT", pk(6 + mh)], ["xT"])

                def run(*gens):
                    gens = [(g if isinstance(g, tuple) else (g, 0)) for g in gens if g is not None]
                    while gens:
                        for it in list(gens):
                            CURQ[0] = it[1]
                            try:
                                next(it[0])
                            except StopIteration:
                                gens.remove(it)
                    CURQ[0] = 0

                make_h(TP - 64, 64, 0)
                PE([(ps[0][:, f * 4:f * 4 + 3], W()[:, kc, f * 128:(f + 1) * 128], hbs[0][:, kc, 61:64], kc == 0, kc == KC - 1)
                    for f in range(12) for kc in range(KC)], ["wmi", ("hb", 0)], [pk(0)])
                CP(halo[:], ps[0][:, 0:48].rearrange("p (f t) -> p f t", f=12)[:, :, 0:3], [pk(0)], ["halo"])
                T.dma("sp", "x_in", x_in[2].ap()[:, 256:292], halo[:].rearrange("p f t -> p (f t)"), reads=["halo"], writes=["d:x_in2"])
                chk(1)
                allgather(2)
                make_h(0, 64, 0)
                CURQ[0] = 1
                make_h(64, 64, 1)
                CURQ[0] = 0
                T.dma("sp", "xl", hsel[:], g8v[2][:, :, 256:292], reads=["d:x_g82"], writes=["hsel"])
                hf = halo[:].rearrange("p f t -> p (f t)")
                T.op("dve", lambda h: h.memset(hf, 0.0), ["halo"], ["halo"])
                for r in range(NCORES - 1):
                    STT(hf, hsel[:, r, :], C("ohp", 128, r, r + 1), hf, ALU.mult, ALU.add, ["hsel", "cst", "halo"], ["halo"])

                chk(2)
                T.op("dve", lambda h: h.memset(Tt[:], 0.0), ["state"], ["state"])
                for hh in range(4):
                    CP(Tt[:, hh, 0:128], C("ident"), ["cst", "state"], ["state"])
                order = {"halo": 0, "chain": 0}

                def p1_block(b, q):
                    while order["halo"] < b:
                        yield
                    if b >= 2:
                        make_h(b * 64, 64, q)
                    g = gdn_pre(64, "64", q)
                    for _ in g:
                        if order["halo"] == b and _ == "halo":
                            order["halo"] = b + 1
                        yield
                    order["halo"] = max(order["halo"], b + 1)
                    while order["chain"] < b:
                        yield
                    yield from gdn_loc(64, b)
                    yield from chain_step(64, Tt, 0, 256)
                    order["chain"] = b + 1
                    if b >= NB - 2:
                        yield from swa_proj(64, b, b % 3, q, kv_only=True)
                        hc = b - (NB - 2)
                        T.dma("sp", "x_in", x_in[2].ap()[hc * 64:(hc + 1) * 64, 0:128], qb[:, 8:10, :].rearrange("p h d -> p (h d)"),
                              reads=["qb"], writes=["d:x_in2"])
                        T.dma("sp", "x_in", x_in[2].ap()[hc * 64:(hc + 1) * 64, 128:256], vtok[:], reads=["vtok"], writes=["d:x_in2"])
                        T.dma("sp", "outk", o_kp[hc * 64:(hc + 1) * 64, li], qb[:, 8:10, :].rearrange("p h d -> p (h d)"), reads=["qb"])
                        T.dma("sp", "outk", o_vp[hc * 64:(hc + 1) * 64, li], vtok[:], reads=["vtok"])

                def p1_stream(q):
                    for b in range(q, NB, 2):
                        yield from p1_block(b, q)
                run((p1_stream(0), 0), (p1_stream(1), 1))
                T.dma("sp", "outc", o_convp[:, li], halo[:], reads=["halo"])
                chk(6)
                T.barrier()
                p1.close()
                SETS[1] = None
                p2 = ExitStack()
                wmi = sb("wmi", (128, KC, DIN), BF16, p2)
                wmo = sb("wmo", (128, KC, DM), BF16, p2)
                hb1 = sb("hb1b", (128, KC, 64), BF16, p2)
                hbs = [SETS[0].hb, hb1]
                zs = sb("zs", (128, 512), F32, p2)
                oa = sb("oa", (64, 512), BF16, p2)
                mixTs = [sb(f"mixT{i}", (128, 8, 64), BF16, p2) for i in range(2)]
                sco = sb("sco", (64, 4, 192), F32, p2)
                ssmA = sb("ssmA", (64, 8), F32, p2)
                pn = sb("pn", (64, 4, 192), BF16, p2)
                pT = sb("pT", (64, 4, 3, 64), BF16, p2)
                ssm = sb("ssm", (64, 32), F32, p2)
                kvh = sb("kvh", (128, 2, 128), F32, p2)
                ckb = sb("ckb", (128, 2, 128), F32, p2)
                S0 = SETS[0]
                S0.osb, S0.zst, S0.zs, S0.oa, S0.ssmA = osb, zst, zs, oa, ssmA
                S1 = _NS()
                S1.sqb = S0.cvb[:].rearrange("p f t -> p (f t)").bitcast(BF16)[:, 0:512].rearrange("p (k t) -> p k t", k=8)
                S1.rsb = S0.QpT[:, 0, :]
                S1.osb = S0.uw[:, 0:2, :].rearrange("p h w -> p (h w)")
                S1.zs = S0.uw[:, 2:4, :].rearrange("p h w -> p (h w)")
                S1.zst = S0.AT[:, 0:2, :].rearrange("p h d -> p (h d)")
                S1.oa = S0.kdec[:].rearrange("p h d -> p (h d)").bitcast(BF16)[:, 0:512]
                S1.ssmA = S0.sm_[0:64, 0:8]
                SETS[1] = S1
                hbA = S0.sq2
                hbB = S0.pre[:].rearrange("p f t -> p (f t)").bitcast(BF16)[:, 0:512].rearrange("p (k t) -> p k t", k=8)
                WS.update({"w": wmi, "gate": 2048, "kvb": 2568, "z": 1536, "qb": 2056})
                BKS[0] = {0: 0, 1: 1, 2: 2, 3: 3, 4: 4, 5: 5}
                BKS[1] = {0: 0, 1: 1, 2: 3, 3: 3, 4: 1, 5: 0}
                for kc in range(KC):
                    T.dma("pool", "wmi", wmi[:, kc, :], d_wmi[li, :, kc, :], writes=["wmi"])
                T.dma("pool", "wmo", wmo[:], d_wmo[li], writes=["wmo"])
                hbs = [S0.hb, hb1, hbA, hbB]
                make_h(0, 64, 0)
                CURQ[0] = 1
                make_h(64, 64, 1)
                CURQ[0] = 0
                TR([(ps[0][:, hh * 128:(hh + 1) * 128], Tt[:, hh, 0:128]) for hh in range(4)], C("ident"), ["state", "cst"], [pk(0)])
                CP(B().uw[:, :, 0:128], ps[0][:, :].rearrange("p (h d) -> p h d", h=4), [pk(0)], ["uw"])
                CP(B().uw[:, :, 128:256], Tt[:, :, 128:256], ["state"], ["uw"])
                T.dma("sp", "x_in", x_in[0].ap(), B().uw[:, 0:2, :].rearrange("p h w -> p (h w)"), reads=["uw"], writes=["d:x_in0"])
                T.dma("sp", "x_in", x_in[1].ap(), B().uw[:, 2:4, :].rearrange("p h w -> p (h w)"), reads=["uw"], writes=["d:x_in1"])
                allgather(0)
                allgather(1)
                allgather(2)
                T.op("dve", lambda h: h.memset(Ssm[:], 0.0), ["Ssm"], ["Ssm"])
                T.op("dve", lambda h: h.memset(Tt[:, :, 128:256], 0.0), ["state"], ["state"])
                T.op("dve", lambda h: h.memset(kvh[:], 0.0), (), ["kvh"])
                for r in range(NCORES):
                    T.dma("sp", "xl", B().uw[:, 0:2, :].rearrange("p h w -> p (h w)"), g8v[0][:, r, :], reads=["d:x_g80"], writes=["uw"])
                    T.dma("sp", "xl", B().uw[:, 2:4, :].rearrange("p h w -> p (h w)"), g8v[1][:, r, :], reads=["d:x_g81"], writes=["uw"])
                    T.dma("sp", "xl2", zs[:, 0:256], g8v[2][:, r, 0:256], reads=["d:x_g82"], writes=["zs"])
                    if r < NCORES - 1:
                        STT(kvh[:].rearrange("p a b -> p (a b)"), zs[:, 0:256], C("ohp", 128, r, r + 1), kvh[:].rearrange("p a b -> p (a b)"),
                            ALU.mult, ALU.add, ["zs", "cst", "kvh"], ["kvh"])
                    if r < NCORES - 1:
                        PE([(ps[1][:, hh * 128:(hh + 1) * 128], B().uw[:, hh, 0:128], Ssm[:, hh, :], True, True) for hh in range(4)],
                           ["uw", "Ssm"], [pk(1)])
                        TT(B().AT[:], ps[1][:, :].rearrange("p (h d) -> p h d", h=4), B().uw[:, :, 128:256], ALU.add, [pk(1), "uw"], ["AT"])
                        TT(B().AT[:], B().AT[:], Ssm[:], ALU.subtract, ["AT", "Ssm"], ["AT"])
                        STT(Ssm[:], B().AT[:], C("mlt", 128, r, r + 1), Ssm[:], ALU.mult, ALU.add, ["AT", "cst", "Ssm"], ["Ssm"])
                    PE([(ps[2][:, hh * 128:(hh + 1) * 128], B().uw[:, hh, 0:128], Tt[:, hh, 128:256], True, True) for hh in range(4)],
                       ["uw", "state"], [pk(2)])
                    TT(Tt[:, :, 128:256], ps[2][:, :].rearrange("p (h d) -> p h d", h=4), B().uw[:, :, 128:256], ALU.add, [pk(2), "uw"], ["state"])
                T.dma("sp", "outd", o_deltap[:, li], Tt[:, :, 128:256], reads=["state"])
                CP(Tt[:, :, 128:256], Ssm[:], ["Ssm", "state"], ["state"])
                halo_kv_to_window = True
                if halo_kv_to_window:
                    for hc in range(2):
                        slot = 1 + hc
                        rows = slice(hc * 64, hc * 64 + 64)
                        def fn(h, rows=rows):
                            ins = None
                            for g in range(2):
                                ins = h.matmul(ps[7][0:64, 256 + g * 64:256 + g * 64 + 64], lhsT=kvh[rows, 0, g * 64:(g + 1) * 64],
                                               rhs=C("ident")[rows, rows], start=True, stop=True)
                            return ins
                        T.op("pe", fn, ["kvh", "cst"], [pk(7)])
                        CP(kTw[:, :, slot, :], ps[7][0:64, 256:384].rearrange("p (g t) -> p g t", g=2), [pk(7)], ["kTw"])
                        def fn2(h, rows=rows):
                            return h.matmul(ps[7][0:64, 384:512], lhsT=C("ident")[rows, rows], rhs=kvh[rows, 1, :], start=True, stop=True)
                        T.op("pe", fn2, ["kvh", "cst"], [pk(7)])
                        for g in range(2):
                            CP(Vw[:, slot, g, 0, 0:64], ps[7][0:64, 384 + g * 64:384 + g * 64 + 64], [pk(7)], ["Vw"])
                            CP(Vw[:, slot, g, 1, 64:128], ps[7][0:64, 384 + g * 64:384 + g * 64 + 64], [pk(7)], ["Vw"])

                chk(7)
                mixP = SETS[0].ctmp_full[:].bitcast(BF16).rearrange("p (q k t) -> p q k t", q=2, k=8)
                hbs = [S0.hb, hb1, hbA, hbB]
                prevB = None
                for b0 in range(0, NB, 2):
                    pp = (b0 // 2) % 2
                    mvp = mixP[:, pp]
                    mk = ("mixP", pp)

                    def genA(b, q, mvp=mvp, mk=mk):
                        if b >= 2:
                            make_h(b * 64, 64, b % 4)
                        yield
                        yield from gdn_corr(64, b, b % 4, mvp[:, :, (b % 2) * 64:(b % 2 + 1) * 64], mk)

                    def genB2(b0=b0, mvp=mvp, mk=mk):
                        for b in (b0, b0 + 1):
                            yield from swa_proj(64, b, b % 3, b % 4)
                            yield from swa_attn(64, [64, 64, 64], "mask0" if b == 0 else ("mask1" if b == 1 else None), b % 4,
                                                mvp[:, :, (b % 2) * 64:(b % 2 + 1) * 64], mk)
                        yield from out_proj(128, b0 * 64, 0, mvp, mk)
                    run((genA(b0, 0), 0), (genA(b0 + 1, 1), 1), prevB)
                    prevB = genB2()
                run(prevB)
                hbs = [S0.hb, hb1]
                chk(12)
                T.barrier()
                T.dma("sp", "ldc", halo[:], d_cc[:, li], writes=["halo"])
                T.dma("sp", "ldc", Tt[:, :, 128:256], d_st[:, li], writes=["state"])
                T.dma("sp", "ldc", ckb[:, 0, :], d_ck[:, li], writes=["ckb"])
                T.dma("sp", "ldc", ckb[:, 1, :], d_cv[:, li], writes=["ckb"])
                make_h(TP, 32, 0)
                run(gdn_pre(32, "32", 0))
                T.dma("sp", "outc", o_convs[:, li], halo[:], reads=["halo"])
                run(gdn_out(32, Tt, 128, TP, 0))
                run(chain_step(32, Tt, 128, 128))
                T.dma("sp", "outd", o_deltas[:, li], Tt[:, :, 128:256], reads=["state"])
                for hc in range(2):
                    rows = slice(hc * 64, hc * 64 + 64)
                    def fn(h, rows=rows):
                        ins = None
                        for g in range(2):
                            ins = h.matmul(ps[7][0:64, 256 + g * 64:256 + g * 64 + 64], lhsT=ckb[rows, 0, g * 64:(g + 1) * 64],
                                           rhs=C("ident")[rows, rows], start=True, stop=True)
                        return ins
                    T.op("pe", fn, ["ckb", "cst", "kTw"], [pk(7)])
                    CP(kTw[:, :, hc, :], ps[7][0:64, 256:384].rearrange("p (g t) -> p g t", g=2), [pk(7)], ["kTw"])
                    def fn2(h, rows=rows):
                        return h.matmul(ps[7][0:64, 384:512], lhsT=C("ident")[rows, rows], rhs=ckb[rows, 1, :], start=True, stop=True)
                    T.op("pe", fn2, ["ckb", "cst"], [pk(7)])
                    for g in range(2):
                        CP(Vw[:, hc, g, 0, 0:64], ps[7][0:64, 384 + g * 64:384 + g * 64 + 64], [pk(7)], ["Vw"])
                        CP(Vw[:, hc, g, 1, 64:128], ps[7][0:64, 384 + g * 64:384 + g * 64 + 64], [pk(7)], ["Vw"])
                run(swa_proj(32, 32, 2, 0, kout=o_ks[96:128, li], vout=o_vs[96:128, li]))
                T.dma("sp", "outk", o_ks[0:96, li], ckb[32:128, 0, :], reads=["ckb"])
                T.dma("sp", "outk", o_vs[0:96, li], ckb[32:128, 1, :], reads=["ckb"])
                run(swa_attn(32, [64, 64, 32], None, 0))
                run(out_proj(32, TP, 0))
                T.barrier()
                p2.close()
                T.kmap = None

        psb16 = [p[:].bitcast(BF16) for p in ps]

        with ExitStack() as st0:
            ztile = sb("ztile", (128, 512), F32, st0)
            T.op("dve", lambda h: h.memset(ztile[:], 0.0), (), ["ztile"])
            for k in range(3):
                T.dma("sp", "x_in", x_in[k].ap(), ztile[:], reads=["ztile"], writes=[f"d:x_in{k}"])
            T.barrier()
        _stage = _KSTAGE
        for li in range(nlayers):
            if _stage in ("all", "ffn"):
                ffn(li * 2, li * 3)
            if _stage in ("all", "mix") and not _STOPPED[0]:
                try:
                    mixer(li)
                except _Stop:
                    T.barrier()
            if _stage in ("all",) and not _STOPPED[0]:
                ffn(li * 2 + 1, li * 3 + 2)

        with ExitStack() as st:
          if not _STOPPED[0]:
              sq = sb("sqF", (128, KC, FT), BF16, st)
              rstd = sb("rstdF", (128, FT), F32, st)
              yb = sb("yb", (128, KC, FT), F32, st)
              T.barrier()
              for t in range(NFT):
                  cs = slice(t * FT, (t + 1) * FT)
                  ACT(sq[:], xT[:, :, cs], AF.Square, ["xT"], ["sqF"])
                  PE([(ps[7][:, 0:FT], onesb[:], sq[:, kc, :], kc == 0, kc == KC - 1) for kc in range(KC)], ["sqF", "onesb"], [pk(7)])
                  RSQRT(rstd[:], ps[7][:, 0:FT], 1.0 / DM, 0, [pk(7), "epst"], ["rstdF"])
                  for kc in range(KC):
                      STT(yb[:, kc, :], xT[:, kc, cs], nrm[:, L * 3, kc:kc + 1], rstd[:], ALU.mult, ALU.mult,
                          ["xT", "nrm", "rstdF"], ["yb"])
                  T.dma("sp", "outy", o_y[:, :, cs], yb[:], reads=["yb"])
              T.barrier()
        with nc.Block() as block:
            T.emit(block)
    return nc


_NC_CACHE = {}


def _layout_inputs(inp):
    f = np.float32
    g = lambda k: np.asarray(inp[k], dtype=f)
    shared = {}
    wfi = np.stack([g("ff1_w_in"), g("ff2_w_in")], 1).reshape(L * 2, KC, 128, 2 * DFF).transpose(0, 2, 1, 3)
    shared["wfi"] = np.ascontiguousarray(wfi if _KSTAGE != "mix" else wfi[:2])
    wfo = np.stack([g("ff1_w_out"), g("ff2_w_out")], 1).reshape(L * 2, HC, 128, DM).transpose(0, 2, 1, 3)
    shared["wfo"] = np.ascontiguousarray(wfo if _KSTAGE != "mix" else wfo[:2])
    shared["wmi"] = np.ascontiguousarray(g("w_mix_in").reshape(L, KC, 128, DIN).transpose(0, 2, 1, 3))
    shared["wmo"] = np.ascontiguousarray(g("w_mix_out").reshape(L, KC, 128, DM).transpose(0, 2, 1, 3))
    nrm = np.zeros((L * 3 + 1, DM), f)
    for l in range(L):
        nrm[l * 3 + 0] = g("norm_ff1")[l]
        nrm[l * 3 + 1] = g("norm_mix")[l]
        nrm[l * 3 + 2] = g("norm_ff2")[l]
    nrm[L * 3] = g("norm_final")
    shared["nrm"] = np.ascontiguousarray(nrm.reshape(L * 3 + 1, KC, 128).transpose(2, 0, 1))
    shared["cw"] = np.ascontiguousarray(g("conv_w").reshape(L, 4, 12, 128).transpose(3, 0, 2, 1))
    row = np.concatenate([g("a_log"), g("dt_bias"), g("sinks"), g("gnorm_w")], axis=1)
    shared["rowc"] = np.ascontiguousarray(np.broadcast_to(row[None], (128, L, 144)))
    maps = []
    xp = g("x_prompt")[0]
    xs = g("x_sample")
    for c in range(NCORES):
        m = dict(shared)
        xc = np.concatenate([xp[c * TP:(c + 1) * TP], xs[c]], 0)
        m["xT"] = np.ascontiguousarray(xc.T.reshape(KC, 128, NT).transpose(1, 0, 2))
        m["cconv"] = np.ascontiguousarray(g("cache_conv")[:, c].reshape(L, 3, 12, 128).transpose(3, 0, 2, 1))
        m["state"] = np.ascontiguousarray(g("state_delta")[:, c].transpose(2, 0, 1, 3))
        m["ck"] = np.ascontiguousarray(g("cache_k")[:, c].reshape(L, 128, 128).transpose(1, 0, 2))
        m["cv"] = np.ascontiguousarray(g("cache_v")[:, c].reshape(L, 128, 128).transpose(1, 0, 2))
        m["cst"] = _make_consts(c)
        maps.append(m)
    return maps


def kernel(**inputs):
    if "nc" not in _NC_CACHE:
        _NC_CACHE["nc"] = build_program()
    nc = _NC_CACHE["nc"]
    maps = _layout_inputs(inputs)
    res = run_bass_kernel_spmd(nc, maps, core_ids=list(range(NCORES)))
    R = res.results
    f = np.float32
    yp = np.zeros((1, NCORES * TP, DM), f)
    ys = np.zeros((NCORES, TS, DM), f)
    for c in range(NCORES):
        y = np.asarray(R[c]["yT"]).transpose(1, 0, 2).reshape(DM, NT).T
        yp[0, c * TP:(c + 1) * TP] = y[:TP]
        ys[c] = y[TP:]

    def conv_out(a):
        return np.ascontiguousarray(np.asarray(a).transpose(1, 3, 2, 0).reshape(L, 3, 1536))

    def delta_out(a):
        return np.ascontiguousarray(np.asarray(a).transpose(1, 2, 0, 3))

    def kv_out(a):
        return np.ascontiguousarray(np.asarray(a).transpose(1, 0, 2).reshape(L, 128, 2, 64))
    last = R[NCORES - 1]
    conv_p = conv_out(last["convp"])[:, None]
    delta_p = delta_out(last["deltap"])[:, None]
    k_p = kv_out(last["kp"])[:, None]
    v_p = kv_out(last["vp"])[:, None]
    conv_s = np.stack([conv_out(R[c]["convs"]) for c in range(NCORES)], 1)
    delta_s = np.stack([delta_out(R[c]["deltas"]) for c in range(NCORES)], 1)
    k_s = np.stack([kv_out(R[c]["ks"]) for c in range(NCORES)], 1)
    v_s = np.stack([kv_out(R[c]["vs"]) for c in range(NCORES)], 1)
    return (yp, ys, conv_p.astype(f), delta_p.astype(f), k_p.astype(f), v_p.astype(f),
            conv_s.astype(f), delta_s.astype(f), k_s.astype(f), v_s.astype(f))
```
